# Optimizing a Trainium2 kernel written in Bass

```python
import math
import jax, jax.numpy as jnp
from jax import lax
import numpy as np

D_MODEL = 4096
BATCH = 4
SEQ = 2048
DEPTH = 2

CHUNK = 64

D_MIX = D_MODEL
D_GROUP = D_MIX // 4

CONV_K = 31

RWKV_HEAD = 64
RWKV_HEADS = D_GROUP // RWKV_HEAD
DECAY_LORA = 64
ICLR_LORA = 64
GATE_LORA = 160
LNX_EPS = 64e-5

SGU_BLOCK = 128
SGU_HEADS = 8
SGU_HEAD = D_GROUP // SGU_HEADS

SB_HEAD = 128
SB_HEADS = D_GROUP // SB_HEAD
SB_BLOCK = 128

D_FF = 11008
EPS = 1e-6

N_CONV_IN = 2 * D_GROUP
N_RWKV_IN = 3 * D_GROUP + DECAY_LORA + ICLR_LORA + GATE_LORA
N_SGU_IN = 2 * D_GROUP
N_SB_IN = 3 * D_GROUP
N_IN = N_CONV_IN + N_RWKV_IN + N_SGU_IN + N_SB_IN

kernel_name = "hybrid_parallel_conv_rwkv7_sgu_stickbreaking_macaron"


def _rmsnorm(x, g):
    xf = x.astype(jnp.float32)
    y = xf * lax.rsqrt(jnp.mean(xf * xf, axis=-1, keepdims=True) + EPS)
    return (y * g.astype(jnp.float32)).astype(x.dtype)


def _layernorm(x, g, b, eps=EPS):
    xf = x.astype(jnp.float32)
    mu = jnp.mean(xf, axis=-1, keepdims=True)
    var = jnp.mean(jnp.square(xf - mu), axis=-1, keepdims=True)
    y = (xf - mu) * lax.rsqrt(var + eps)
    return (y * g.astype(jnp.float32) + b.astype(jnp.float32)).astype(x.dtype)


def _swiglu(x, w_in, w_out):
    gate, up = jnp.split(x @ w_in, 2, axis=-1)
    return (jax.nn.silu(gate) * up) @ w_out


def _token_shift(h):
    return jnp.pad(h[:, :-1], ((0, 0), (1, 0), (0, 0)))


def _conv_mixer(h, w_dw, b_dw, ln_g, ln_b):
    a, gate = jnp.split(h, 2, axis=-1)
    y = a * jax.nn.sigmoid(gate)
    y = lax.conv_general_dilated(
        y, w_dw[:, None, :].astype(y.dtype), window_strides=(1,),
        padding=[(CONV_K - 1, 0)],
        dimension_numbers=("NWC", "WIO", "NWC"),
        feature_group_count=y.shape[-1]) + b_dw
    y = _layernorm(y, ln_g, ln_b)
    return jax.nn.silu(y)


def _rwkv_step(state, inp):
    r_t, w_t, k_t, v_t, kk_t, a_t = inp
    sa = jnp.einsum('bhvk,bhk->bhv', state, -kk_t)
    state = (state * w_t[:, :, None, :]
             + sa[..., None] * (kk_t * a_t)[:, :, None, :]
             + v_t[..., None] * k_t[:, :, None, :])
    y_t = jnp.einsum('bhvk,bhk->bhv', state, r_t)
    return state, y_t


def _rwkv7_mixer(h, mu, w0, w_up, a0, a_up, g_up, k_k, k_a, r_k, lnx_g, lnx_b):
    B, S, _ = h.shape
    C, H, N = D_GROUP, RWKV_HEADS, RWKV_HEAD
    h = h + (_token_shift(h) - h) * mu
    r, k, v, xw, xa, xg = jnp.split(
        h, [C, 2 * C, 3 * C, 3 * C + DECAY_LORA, 3 * C + DECAY_LORA + ICLR_LORA], axis=-1)
    f32 = jnp.float32
    w_log = -jax.nn.softplus(-(w0 + jnp.tanh(xw) @ w_up).astype(f32)) - 0.5
    decay = jnp.exp(-jnp.exp(w_log))
    a = jax.nn.sigmoid((a0 + xa @ a_up).astype(f32))
    g = jax.nn.sigmoid(xg) @ g_up
    r = r.astype(f32).reshape(B, S, H, N)
    v = v.astype(f32).reshape(B, S, H, N)
    kf = k.astype(f32)
    kk = (kf * k_k.astype(f32)).reshape(B, S, H, N)
    kk = kk / jnp.maximum(jnp.linalg.norm(kk, axis=-1, keepdims=True), 1e-12)
    kf = (kf * (1.0 + (a - 1.0) * k_a.astype(f32))).reshape(B, S, H, N)
    decay = decay.reshape(B, S, H, N)
    a = a.reshape(B, S, H, N)
    xs = tuple(jnp.moveaxis(t, 1, 0) for t in (r, decay, kf, v, kk, a))
    state0 = jnp.zeros((B, H, N, N), f32)
    _, ys = lax.scan(_rwkv_step, state0, xs)
    y = jnp.moveaxis(ys, 0, 1)
    y = _layernorm(y, lnx_g.reshape(H, N), lnx_b.reshape(H, N), eps=LNX_EPS)
    bonus = jnp.sum(r * kf * r_k.astype(f32), axis=-1, keepdims=True) * v
    y = (y + bonus).reshape(B, S, C)
    return (y * g.astype(f32)).astype(h.dtype)


def _sgu_mixer(h, ln_g, ln_b, w_s, b_s):
    B, S, _ = h.shape
    u, v = jnp.split(jax.nn.gelu(h), 2, axis=-1)
    v = _layernorm(v, ln_g, ln_b)
    nb = S // SGU_BLOCK
    v = v.reshape(B, nb, SGU_BLOCK, SGU_HEADS, SGU_HEAD)
    pos = jnp.arange(SGU_BLOCK)
    chunk_causal = (pos[None, :] // CHUNK) <= (pos[:, None] // CHUNK)
    w = jnp.where(chunk_causal[None], w_s, 0.0).astype(v.dtype)
    sv = jnp.einsum('hij,bnjhc->bnihc', w, v) + b_s.T[None, None, :, :, None]
    return u * sv.reshape(B, S, D_GROUP)


def _stick_breaking_mixer(h):
    B, S, _ = h.shape
    q, k, v = jnp.split(h, 3, axis=-1)
    to_heads = lambda t: t.reshape(B, S, SB_HEADS, SB_HEAD).transpose(0, 2, 1, 3)
    q, k, v = to_heads(q), to_heads(k), to_heads(v)
    scale = SB_HEAD ** -0.5
    outs = []
    for blk in range(S // SB_BLOCK):
        q0 = blk * SB_BLOCK
        L = q0 + SB_BLOCK
        z = jnp.einsum('bhqd,bhkd->bhqk', q[:, :, q0:L], k[:, :, :L]).astype(jnp.float32) * scale
        t_idx = q0 + jnp.arange(SB_BLOCK)[:, None]
        s_idx = jnp.arange(L)[None, :]
        before = s_idx < t_idx
        log_keep = jnp.where(before, jax.nn.log_sigmoid(-z), 0.0)
        later = lax.cumsum(log_keep, axis=3, reverse=True) - log_keep
        weights = jnp.where(before, jnp.exp(jax.nn.log_sigmoid(z) + later), 0.0)
        outs.append(jnp.einsum('bhqk,bhkd->bhqd', weights.astype(v.dtype), v[:, :, :L]))
    o = jnp.concatenate(outs, axis=2)
    return o.transpose(0, 2, 1, 3).reshape(B, S, D_GROUP)


def setup_inputs(seed: int = 0) -> dict:
    key = jax.random.key(seed)
    ks = jax.random.split(key, 32)
    f32 = jnp.float32
    nrm = lambda k, shape, s: jax.random.normal(k, shape, f32) * s
    gain = lambda k, shape: 1.0 + 0.01 * jax.random.normal(k, shape, f32)
    return {
        "x": jax.random.normal(ks[0], (BATCH, SEQ, D_MODEL), f32),
        "ffn1_norm": gain(ks[1], (DEPTH, D_MODEL)),
        "ffn1_w_in": nrm(ks[2], (DEPTH, D_MODEL, 2 * D_FF), D_MODEL ** -0.5),
        "ffn1_w_out": nrm(ks[3], (DEPTH, D_FF, D_MODEL), D_FF ** -0.5),
        "mix_norm": gain(ks[4], (DEPTH, D_MODEL)),
        "mix_w_in": nrm(ks[5], (DEPTH, D_MODEL, N_IN), D_MODEL ** -0.5),
        "conv_w": nrm(ks[6], (DEPTH, CONV_K, D_GROUP), CONV_K ** -0.5),
        "conv_b": nrm(ks[7], (DEPTH, D_GROUP), 0.01),
        "conv_ln_g": gain(ks[8], (DEPTH, D_GROUP)),
        "conv_ln_b": nrm(ks[9], (DEPTH, D_GROUP), 0.01),
        "rwkv_mu": jax.random.uniform(ks[10], (DEPTH, N_RWKV_IN), f32),
        "rwkv_w0": jax.random.uniform(ks[11], (DEPTH, D_GROUP), f32, -3.0, 1.0),
        "rwkv_w_up": nrm(ks[12], (DEPTH, DECAY_LORA, D_GROUP), 0.1 * DECAY_LORA ** -0.5),
        "rwkv_a0": nrm(ks[13], (DEPTH, D_GROUP), 0.1),
        "rwkv_a_up": nrm(ks[14], (DEPTH, ICLR_LORA, D_GROUP), 0.1 * ICLR_LORA ** -0.5),
        "rwkv_g_up": nrm(ks[15], (DEPTH, GATE_LORA, D_GROUP), GATE_LORA ** -0.5),
        "rwkv_k_k": 0.85 + nrm(ks[16], (DEPTH, D_GROUP), 0.02),
        "rwkv_k_a": 1.0 + nrm(ks[17], (DEPTH, D_GROUP), 0.02),
        "rwkv_r_k": nrm(ks[18], (DEPTH, RWKV_HEADS, RWKV_HEAD), 0.1),
        "rwkv_lnx_g": gain(ks[19], (DEPTH, D_GROUP)),
        "rwkv_lnx_b": nrm(ks[20], (DEPTH, D_GROUP), 0.01),
        "sgu_ln_g": gain(ks[21], (DEPTH, D_GROUP)),
        "sgu_ln_b": nrm(ks[22], (DEPTH, D_GROUP), 0.01),
        "sgu_w_s": nrm(ks[23], (DEPTH, SGU_HEADS, SGU_BLOCK, SGU_BLOCK), SGU_BLOCK ** -0.5),
        "sgu_b_s": 1.0 + nrm(ks[24], (DEPTH, SGU_HEADS, SGU_BLOCK), 0.01),
        "mix_w_out": nrm(ks[25], (DEPTH, D_MIX, D_MODEL), D_MIX ** -0.5),
        "ffn2_norm": gain(ks[26], (DEPTH, D_MODEL)),
        "ffn2_w_in": nrm(ks[27], (DEPTH, D_MODEL, 2 * D_FF), D_MODEL ** -0.5),
        "ffn2_w_out": nrm(ks[28], (DEPTH, D_FF, D_MODEL), D_FF ** -0.5),
        "final_norm": gain(ks[29], (D_MODEL,)),
    }


def reference(x, ffn1_norm, ffn1_w_in, ffn1_w_out, mix_norm, mix_w_in, conv_w, conv_b,
              conv_ln_g, conv_ln_b, rwkv_mu, rwkv_w0, rwkv_w_up, rwkv_a0, rwkv_a_up,
              rwkv_g_up, rwkv_k_k, rwkv_k_a, rwkv_r_k, rwkv_lnx_g, rwkv_lnx_b,
              sgu_ln_g, sgu_ln_b, sgu_w_s, sgu_b_s, mix_w_out, ffn2_norm, ffn2_w_in,
              ffn2_w_out, final_norm):
    splits = [N_CONV_IN, N_CONV_IN + N_RWKV_IN, N_CONV_IN + N_RWKV_IN + N_SGU_IN]
    for l in range(DEPTH):
        x = x + 0.5 * _swiglu(_rmsnorm(x, ffn1_norm[l]), ffn1_w_in[l], ffn1_w_out[l])
        h = _rmsnorm(x, mix_norm[l]) @ mix_w_in[l]
        h_conv, h_rwkv, h_sgu, h_sb = jnp.split(h, splits, axis=-1)
        y_conv = _conv_mixer(h_conv, conv_w[l], conv_b[l], conv_ln_g[l], conv_ln_b[l])
        y_rwkv = _rwkv7_mixer(h_rwkv, rwkv_mu[l], rwkv_w0[l], rwkv_w_up[l], rwkv_a0[l],
                              rwkv_a_up[l], rwkv_g_up[l], rwkv_k_k[l], rwkv_k_a[l],
                              rwkv_r_k[l], rwkv_lnx_g[l], rwkv_lnx_b[l])
        y_sgu = _sgu_mixer(h_sgu, sgu_ln_g[l], sgu_ln_b[l], sgu_w_s[l], sgu_b_s[l])
        y_sb = _stick_breaking_mixer(h_sb)
        y = jnp.concatenate([y_conv, y_rwkv.astype(y_conv.dtype), y_sgu, y_sb], axis=-1)
        x = x + y @ mix_w_out[l]
        x = x + 0.5 * _swiglu(_rmsnorm(x, ffn2_norm[l]), ffn2_w_in[l], ffn2_w_out[l])
    return _rmsnorm(x, final_norm)
```

```python
import numpy as np
from contextlib import ExitStack
import concourse.bass as bass
import concourse.mybir as mybir
from concourse.bass_utils import run_bass_kernel_spmd

F32 = mybir.dt.float32
BF16 = mybir.dt.bfloat16
AF = mybir.ActivationFunctionType
ALU = mybir.AluOpType
AX = mybir.AxisListType

SEM_LIMIT = 30000
N_DMA_SEMS = 12


class Buf:
    __slots__ = ("name", "w", "r")

    def __init__(self, name=""):
        self.name = name
        self.w = None
        self.r = []


class PB:
    ENG = ("pe", "act", "dve", "pool", "sp")

    def __init__(self, nc):
        self.nc = nc
        self.es = ExitStack()
        self.eng = {"pe": nc.tensor, "act": nc.scalar, "dve": nc.vector, "pool": nc.gpsimd, "sp": nc.sync}
        self.q = {e: [] for e in self.ENG}
        self.sem = {}
        self.cnt = {}
        self.nsem = 0
        for e in ("pe", "act", "dve", "pool"):
            self._new_sem(e)
        self.seen = {e: {} for e in self.ENG}
        self.dsem = {}
        for qn in ("sp", "act", "pool"):
            self.dsem[qn] = [[self._alloc_sem(f"d{qn}{i}"), 0, None] for i in range(N_DMA_SEMS)]
        self.dnext = {qn: 0 for qn in self.dsem}
        self.pe_pend_r = []
        self.pe_pend_w = []
        self.n_inst = 0

    def _alloc_sem(self, name):
        self.nsem += 1
        return self.es.enter_context(self.nc.semaphore(f"{name}_{self.nsem}"))

    def _new_sem(self, e):
        self.sem[e] = self._alloc_sem(f"s{e}")
        self.cnt[e] = 0

    def sbuf(self, name, shape, dt):
        return self.es.enter_context(self.nc.sbuf_tensor(name, list(shape), dt))

    def psum(self, name, shape, dt):
        return self.es.enter_context(self.nc.psum_tensor(name, list(shape), dt))

    def _need(self, e, reads, writes):
        toks = []
        for b in reads:
            if b.w is not None:
                toks.append(b.w)
        for b in writes:
            if b.w is not None:
                toks.append(b.w)
            toks.extend(b.r)
        best = {}
        for (s, v) in toks:
            k = id(s)
            if k not in best or best[k][1] < v:
                best[k] = (s, v)
        out = []
        for k, (s, v) in best.items():
            if e == "pe" and s is self.sem["pe"]:
                continue
            if self.seen[e].get(k, 0) >= v:
                continue
            self.seen[e][k] = v
            out.append((s, v))
        return out

    def _emit_waits(self, e, waits):
        for (s, v) in waits:
            self.q[e].append(lambda E, s=s, v=v: E.wait_ge(s, v))

    def _commit(self, tok, reads, writes):
        for b in reads:
            b.r.append(tok)
        for b in writes:
            b.w = tok
            b.r = []

    def op(self, e, fn, reads=(), writes=()):
        waits = self._need(e, reads, writes)
        self._emit_waits(e, waits)
        if self.cnt[e] >= SEM_LIMIT:
            self._new_sem(e)
        self.cnt[e] += 1
        s = self.sem[e]
        self.q[e].append(lambda E, fn=fn, s=s: fn(E).then_inc(s, 1))
        tok = (s, self.cnt[e])
        self._commit(tok, reads, writes)
        self.n_inst += 1
        return tok

    def mm(self, fn, reads=(), writes=(), last=True):
        e = "pe"
        waits = self._need(e, reads, writes)
        self._emit_waits(e, waits)
        self.n_inst += 1
        if not last:
            self.q[e].append(lambda E, fn=fn: fn(E))
            self.pe_pend_r.extend(reads)
            self.pe_pend_w.extend(writes)
            return None
        if self.cnt[e] >= SEM_LIMIT:
            self._new_sem(e)
        self.cnt[e] += 1
        s = self.sem[e]
        self.q[e].append(lambda E, fn=fn, s=s: fn(E).then_inc(s, 1))
        tok = (s, self.cnt[e])
        rs = {id(b): b for b in list(self.pe_pend_r) + list(reads)}
        ws = {id(b): b for b in list(self.pe_pend_w) + list(writes)}
        self.pe_pend_r = []
        self.pe_pend_w = []
        self._commit(tok, [b for k, b in rs.items() if k not in ws], list(ws.values()))
        return tok

    def barrier(self):
        toks = [(self.sem[e], self.cnt[e]) for e in ("pe", "act", "dve", "pool") if self.cnt[e] > 0]
        for qn, slots in self.dsem.items():
            for (s, n, prev) in slots:
                if prev is not None:
                    toks.append(prev)
        if getattr(self, "ccnt", 0) > 0:
            toks.append((self.csem, self.ccnt))
        for e in self.ENG:
            for (s, v) in toks:
                k = id(s)
                if self.seen[e].get(k, 0) >= v:
                    continue
                if e == "pe" and s is self.sem["pe"]:
                    continue
                self.seen[e][k] = v
                self.q[e].append(lambda E, s=s, v=v: E.wait_ge(s, v))

    def dma(self, qn, out, in_, reads=(), writes=(), **kw):
        slots = self.dsem[qn]
        i = self.dnext[qn]
        self.dnext[qn] = (i + 1) % len(slots)
        slot = slots[i]
        s, n, prev = slot
        waits = self._need(qn, reads, writes)
        if prev is not None:
            k = id(prev[0])
            if self.seen[qn].get(k, 0) < prev[1]:
                self.seen[qn][k] = prev[1]
                waits.append(prev)
        self._emit_waits(qn, waits)
        n += 1
        tok = (s, 16 * n)
        slot[1] = n
        slot[2] = tok
        self.q[qn].append(lambda E, out=out, in_=in_, s=s, kw=kw: E.dma_start(out=out, in_=in_, **kw).then_inc(s, 16))
        self._commit(tok, reads, writes)
        self.n_inst += 1
        return tok

    def wait_all(self, e, toks):
        for (s, v) in toks:
            self.q[e].append(lambda E, s=s, v=v: E.wait_ge(s, v))

    def finish(self, final_toks):
        self.wait_all("sp", final_toks)
        nc = self.nc
        q = self.q
        with nc.Block() as block:
            @block.sync
            def _(E):
                for f in q["sp"]:
                    f(E)

            @block.scalar
            def _(E):
                for f in q["act"]:
                    f(E)

            @block.vector
            def _(E):
                for f in q["dve"]:
                    f(E)

            @block.gpsimd
            def _(E):
                for f in q["pool"]:
                    f(E)

            @block.tensor
            def _(E):
                for f in q["pe"]:
                    f(E)
        self.es.close()


def pb_collective(self, kind, ins, outs, groups, reads=(), writes=()):
    qn = "pool"
    if not hasattr(self, "csem"):
        self.csem = self._alloc_sem("cc")
        self.ccnt = 0
    waits = self._need(qn, reads, writes)
    self._emit_waits(qn, waits)
    self.ccnt += 1
    s = self.csem
    tok = (s, self.ccnt)
    self.q[qn].append(lambda E, s=s: E.collective_compute(kind, ALU.bypass, replica_groups=groups, ins=ins, outs=outs).then_inc(s))
    self._commit(tok, reads, writes)
    return tok


PB.collective = pb_collective


EPS = 1e-6


class Ctx:
    def __init__(self, P, NT):
        self.P = P
        self.NT = NT
        self.ps = [P.psum(f"ps{i}", [128, 512], F32) for i in range(8)]
        self.psb = [Buf(f"ps{i}") for i in range(8)]
        self.ones = P.sbuf("ones_f", [128, 128], F32)
        self.b_ones = Buf("ones")
        P.op("dve", lambda E: E.memset(self.ones[:], 1.0), writes=[self.b_ones])
        self.NW = 4
        self.wb = [P.sbuf(f"wb{i}", [128, 4, 512], BF16) for i in range(self.NW)]
        self.wbb = [Buf(f"wb{i}") for i in range(self.NW)]
        self.wi = 0

    def next_w(self):
        i = self.wi
        self.wi = (i + 1) % self.NW
        return self.wb[i], self.wbb[i]


def split_chunks(n, nch):
    base, rem = divmod(n, nch)
    out = []
    s = 0
    for i in range(nch):
        c = base + (1 if i < rem else 0)
        out.append((s, c))
        s += c
    return out


def rmsnorm(C, xT, g_sb, D, xn_sb=None, b_xn=None, xres=None, out_f32=None, tmp=None):
    P = C.P
    NT = C.NT
    KT = D // 128
    Q = 256
    xf, b_xf, sq, b_sq, rs, b_rs, rstd, b_rstd = tmp["xf"], tmp["b_xf"], tmp["sq"], tmp["b_sq"], tmp["rs"], tmp["b_rs"], tmp["rstd"], tmp["b_rstd"]
    toks = []
    for q in range(NT // Q):
        c0 = q * Q
        h = c0 // 512
        for k in range(KT):
            rd = [xres[(k, h)]] if xres is not None else []
            P.dma("sp", xf[:, k, :], xT[k * 128:(k + 1) * 128, c0:c0 + Q], reads=rd, writes=[b_xf[k]])
        bank = q % 2
        for k in range(KT):
            P.op("act", lambda E, k=k, i=k % 2: E.activation(out=sq[i][:], in_=xf[:, k, :], func=AF.Square),
                 reads=[b_xf[k]], writes=[b_sq[k % 2]])
            P.mm(lambda E, k=k, i=k % 2, bank=bank: E.matmul(C.ps[bank][:, 0:Q], lhsT=C.ones[:], rhs=sq[i][:], start=(k == 0), stop=(k == KT - 1)),
                 reads=[b_sq[k % 2], C.b_ones], writes=[C.psb[bank]], last=True)
        P.op("act", lambda E, bank=bank: E.activation(out=rs[:], in_=C.ps[bank][:, 0:Q], func=AF.Sqrt, scale=1.0 / D, bias=tmp["eps"][:]),
             reads=[C.psb[bank], tmp["b_eps"]], writes=[b_rs])
        P.op("dve", lambda E: E.reciprocal(out=rstd[:], in_=rs[:]), reads=[b_rs], writes=[b_rstd])
        for k in range(KT):
            if xn_sb is not None:
                P.op("dve", lambda E, k=k, c0=c0: E.scalar_tensor_tensor(out=xn_sb[:, k, c0:c0 + Q], in0=xf[:, k, :], scalar=g_sb[:, k:k + 1], in1=rstd[:], op0=ALU.mult, op1=ALU.mult),
                     reads=[b_xf[k], b_rstd, tmp["b_g"]], writes=[b_xn])
            else:
                P.op("dve", lambda E, k=k: E.scalar_tensor_tensor(out=xf[:, k, :], in0=xf[:, k, :], scalar=g_sb[:, k:k + 1], in1=rstd[:], op0=ALU.mult, op1=ALU.mult),
                     reads=[b_xf[k], b_rstd, tmp["b_g"]], writes=[b_xf[k]])
                toks.append(P.dma("sp", out_f32[k * 128:(k + 1) * 128, c0:c0 + Q], xf[:, k, :], reads=[b_xf[k]], writes=[Buf()]))
    return toks


def norm_tmp(P, AR, KT):
    t = {}
    t["xf"] = AR.alloc((KT, 256), F32)
    t["b_xf"] = [Buf() for _ in range(KT)]
    t["sq"] = [AR.alloc((256,), F32) for i in range(2)]
    t["b_sq"] = [Buf(), Buf()]
    t["rs"] = AR.alloc((256,), F32)
    t["b_rs"] = Buf()
    t["rstd"] = AR.alloc((256,), F32)
    t["b_rstd"] = Buf()
    t["eps"] = AR.alloc((1,), F32)
    t["b_eps"] = Buf()
    P.op("dve", lambda E: E.memset(t["eps"][:, :], EPS), writes=[t["b_eps"]])
    t["b_g"] = Buf()
    return t


def proj_rmw(C, act_sb, b_act, kt0, nk, W, wrow0, D, xT, xres, scale, rt):
    P = C.P
    NT = C.NT
    H = NT // 512
    NG = D // 512 if D >= 512 else 1
    JW = min(4, D // 128)
    assert JW * H <= 8
    for g in range(NG):
        for j in range(JW):
            for h in range(H):
                f = g * JW + j
                xo, b_xo = rt["xold"][j * H + h], rt["b_xold"][j * H + h]
                P.dma("sp", xo[:], xT[f * 128:(f + 1) * 128, h * 512:(h + 1) * 512], reads=[xres[(f, h)]], writes=[b_xo])
        kb = 0
        while kb < nk:
            nkk = min(4, nk - kb)
            wb, bwb = C.next_w()
            r0 = wrow0 + kb * 128
            P.dma("pool", wb[:, 0:nkk, 0:JW * 128], W[r0:r0 + nkk * 128, g * JW * 128:(g + 1) * JW * 128].rearrange("(kk p) n -> p kk n", p=128), writes=[bwb])
            for kk in range(nkk):
                k = kb + kk
                for j in range(JW):
                    for h in range(H):
                        lastk = (kk == nkk - 1 and j == JW - 1 and h == H - 1)
                        P.mm(lambda E, wb=wb, kk=kk, j=j, h=h, k=k: E.matmul(C.ps[j * H + h][:, :], lhsT=wb[:, kk, j * 128:(j + 1) * 128], rhs=act_sb[:, kt0 + k, h * 512:(h + 1) * 512], start=(k == 0), stop=(k == nk - 1)),
                             reads=[bwb, b_act], writes=[C.psb[j * H + h]], last=lastk)
            kb += nkk
        for j in range(JW):
            for h in range(H):
                f = g * JW + j
                i = j * H + h
                xo, b_xo = rt["xold"][i], rt["b_xold"][i]
                P.op("dve", lambda E, i=i, xo=xo: E.scalar_tensor_tensor(out=xo[:], in0=C.ps[i][:, :], scalar=float(scale), in1=xo[:], op0=ALU.mult, op1=ALU.add),
                     reads=[C.psb[i], b_xo], writes=[b_xo])
                P.dma("sp", xT[f * 128:(f + 1) * 128, h * 512:(h + 1) * 512], xo[:], reads=[b_xo], writes=[xres[(f, h)]])


def rmw_tmp(P, AR):
    rt = {}
    rt["xold"] = [AR.alloc((512,), F32) for i in range(8)]
    rt["b_xold"] = [Buf() for _ in range(8)]
    return rt


def ffn(C, xn_sb, b_xn, W_in, W_out, D, DFF, xT, xres, rt, gT, b_gT, stmp, b_stmp, nch=4):
    P = C.P
    NT = C.NT
    H = NT // 512
    KT = D // 128
    FT = DFF // 128
    Win4 = W_in.rearrange("(kt p) (two f) -> p kt two f", p=128, two=2)
    for (c0, cn) in split_chunks(FT, nch):
        t = 0
        while t < cn:
            gs = min(2, cn - t)
            n0 = (c0 + t) * 128
            for k4 in range(0, KT, 4):
                nkk = min(4, KT - k4)
                wb, bwb = C.next_w()
                wv = wb[:].rearrange("p k (two f) -> p k two f", two=2)
                for gu in range(2):
                    P.dma("pool", wv[:, 0:nkk, gu, 0:gs * 128], Win4[:, k4:k4 + nkk, gu, n0:n0 + gs * 128], writes=[bwb])
                for kk in range(nkk):
                    k = k4 + kk
                    for j in range(gs):
                        for gu in range(2):
                            for h in range(H):
                                bank = gu * 4 + j * H + h
                                lastk = (kk == nkk - 1 and j == gs - 1 and gu == 1 and h == H - 1)
                                P.mm(lambda E, wv=wv, kk=kk, gu=gu, j=j, h=h, k=k, bank=bank: E.matmul(C.ps[bank][:, :], lhsT=wv[:, kk, gu, j * 128:(j + 1) * 128], rhs=xn_sb[:, k, h * 512:(h + 1) * 512], start=(k == 0), stop=(k == KT - 1)),
                                     reads=[bwb, b_xn], writes=[C.psb[bank]], last=lastk)
            for j in range(gs):
                for h in range(H):
                    bg = j * H + h
                    bu = 4 + j * H + h
                    si = (j * H + h) % 2
                    P.op("act", lambda E, bg=bg, si=si: E.activation(out=stmp[si][:], in_=C.ps[bg][:, :], func=AF.Silu),
                         reads=[C.psb[bg]], writes=[b_stmp[si]])
                    P.op("dve", lambda E, bu=bu, si=si, tt=t + j, h=h: E.tensor_tensor(out=gT[:, tt, h * 512:(h + 1) * 512], in0=stmp[si][:], in1=C.ps[bu][:, :], op=ALU.mult),
                         reads=[b_stmp[si], C.psb[bu]], writes=[b_gT])
            t += gs
        proj_rmw(C, gT, b_gT, 0, cn, W_out, c0 * 128, D, xT, xres, 0.5, rt)

import math


class Arena:
    def __init__(self, P, name, nbytes):
        self.n = nbytes // 4
        self.t = P.sbuf(name, [128, self.n], F32)
        self.off = 0
        self.peak = 0

    def reset(self, off=0):
        self.off = off

    def alloc(self, shape, dt):
        sz = 1
        for s in shape:
            sz *= s
        words = sz if dt == F32 else (sz + 1) // 2
        a = self.t[:, self.off:self.off + words]
        self.off += words
        self.peak = max(self.peak, self.off)
        assert self.off <= self.n, ("arena overflow", self.off, self.n)
        if dt != F32:
            a = a.bitcast(dt)
            if sz % 2:
                a = a[:, 0:sz]
        if len(shape) == 1:
            return a
        if len(shape) == 2:
            return a.rearrange("p (a b) -> p a b", a=shape[0])
        if len(shape) == 3:
            return a.rearrange("p (a b c) -> p a b c", a=shape[0], b=shape[1])
        raise ValueError


class DV:
    def __init__(self, ap=None, fn=None):
        self.ap = ap
        self.fn = fn

    def tile(self, k, c0, n):
        if self.fn is not None:
            return self.fn(k, c0, n)
        return self.ap[k * 128:(k + 1) * 128, c0:c0 + n]


class RV:
    def __init__(self, ap=None, fn=None):
        self.ap = ap
        self.fn = fn

    def rows(self, r0, nr, c0, n):
        if self.fn is not None:
            return self.fn(r0, nr, c0, n)
        return self.ap[r0:r0 + nr, c0:c0 + n]


class ArenaView:
    def __init__(self, ar, dt):
        self.ar = ar
        self.dt = dt

    def alloc(self, *shape):
        return self.ar.alloc(shape, self.dt)


def proj_blocks(C, xn, b_xn, chunks, W, KT, segs, blocks, evac, bank0=0):
    P = C.P
    nb = len(blocks) * len(chunks)
    assert bank0 + nb <= 8
    soff = []
    o = 0
    for (c0_, nc_) in segs:
        soff.append(o)
        o += nc_
    assert o <= 512
    for k4 in range(0, KT, 4):
        nkk = min(4, KT - k4)
        wb, bwb = C.next_w()
        for si, (col0, ncols) in enumerate(segs):
            P.dma("pool", wb[:, 0:nkk, soff[si]:soff[si] + ncols],
                  W[k4 * 128:(k4 + nkk) * 128, col0:col0 + ncols].rearrange("(kk p) n -> p kk n", p=128), writes=[bwb])
        for kk in range(nkk):
            k = k4 + kk
            for bi, (si, off, M) in enumerate(blocks):
                for ci, (c0, n) in enumerate(chunks):
                    bank = bank0 + bi * len(chunks) + ci
                    lastk = (kk == nkk - 1 and bi == len(blocks) - 1 and ci == len(chunks) - 1)
                    P.mm(lambda E, wb=wb, kk=kk, a=soff[si] + off, M=M, c0=c0, n=n, k=k, bank=bank:
                         E.matmul(C.ps[bank][0:M, 0:n], lhsT=wb[:, kk, a:a + M], rhs=xn[:, k, c0:c0 + n], start=(k == 0), stop=(k == KT - 1)),
                         reads=[bwb, b_xn], writes=[C.psb[bank]], last=lastk)
    for bi, (si, off, M) in enumerate(blocks):
        for ci, (c0, n) in enumerate(chunks):
            bank = bank0 + bi * len(chunks) + ci
            evac(bi, ci, C.ps[bank][0:M, 0:n], C.psb[bank])


def layernorm_fm(C, x, b_x, nt, n, ncol, gcol, bcol, b_par, eps, out_fn, tmp):
    P = C.P
    CH = nt * 128
    bs, bq = 0, 1
    bxl = b_x if isinstance(b_x, list) else [b_x] * nt
    for k in range(nt):
        P.mm(lambda E, k=k: E.matmul(C.ps[bs][:, 0:n], lhsT=C.ones[:], rhs=x[:, k, ncol:ncol + n], start=(k == 0), stop=(k == nt - 1)),
             reads=[bxl[k], C.b_ones], writes=[C.psb[bs]], last=(k == nt - 1))
    for k in range(nt):
        i = k % 2
        P.op("act", lambda E, k=k, i=i: E.activation(out=tmp["sq"][i][:, 0:n], in_=x[:, k, ncol:ncol + n], func=AF.Square),
             reads=[bxl[k]], writes=[tmp["b_sq"][i]])
        P.mm(lambda E, k=k, i=i: E.matmul(C.ps[bq][:, 0:n], lhsT=C.ones[:], rhs=tmp["sq"][i][:, 0:n], start=(k == 0), stop=(k == nt - 1)),
             reads=[tmp["b_sq"][i], C.b_ones], writes=[C.psb[bq]], last=True)
    mean, rstd = tmp["mean"], tmp["rstd"]
    P.op("act", lambda E: E.activation(out=mean[:, 0:n], in_=C.ps[bs][:, 0:n], func=AF.Copy, scale=1.0 / CH), reads=[C.psb[bs]], writes=[tmp["b_mean"]])
    P.op("dve", lambda E: E.tensor_tensor(out=rstd[:, 0:n], in0=mean[:, 0:n], in1=mean[:, 0:n], op=ALU.mult), reads=[tmp["b_mean"]], writes=[tmp["b_rstd"]])
    P.op("dve", lambda E: E.scalar_tensor_tensor(out=rstd[:, 0:n], in0=C.ps[bq][:, 0:n], scalar=1.0 / CH, in1=rstd[:, 0:n], op0=ALU.mult, op1=ALU.subtract),
         reads=[C.psb[bq], tmp["b_rstd"]], writes=[tmp["b_rstd"]])
    P.op("dve", lambda E: E.tensor_scalar(out=rstd[:, 0:n], in0=rstd[:, 0:n], scalar1=0.0, scalar2=float(eps), op0=ALU.max, op1=ALU.add), reads=[tmp["b_rstd"]], writes=[tmp["b_rstd"]])
    P.op("act", lambda E: E.activation(out=rstd[:, 0:n], in_=rstd[:, 0:n], func=AF.Sqrt), reads=[tmp["b_rstd"]], writes=[tmp["b_rstd"]])
    P.op("dve", lambda E: E.reciprocal(out=rstd[:, 0:n], in_=rstd[:, 0:n]), reads=[tmp["b_rstd"]], writes=[tmp["b_rstd"]])
    for k in range(nt):
        i = k % 2
        t = tmp["t"][i]
        P.op("dve", lambda E, k=k, t=t: E.tensor_tensor(out=t[:, 0:n], in0=x[:, k, ncol:ncol + n], in1=mean[:, 0:n], op=ALU.subtract),
             reads=[bxl[k], tmp["b_mean"]], writes=[tmp["b_t"][i]])
        P.op("dve", lambda E, t=t: E.tensor_tensor(out=t[:, 0:n], in0=t[:, 0:n], in1=rstd[:, 0:n], op=ALU.mult),
             reads=[tmp["b_rstd"]], writes=[tmp["b_t"][i]])
        out_fn(k, t[:, 0:n], tmp["b_t"][i])


def ln_tmp(AR):
    t = {}
    t["sq"] = [AR.alloc(512), AR.alloc(512)]
    t["b_sq"] = [Buf(), Buf()]
    t["mean"] = AR.alloc(512)
    t["rstd"] = AR.alloc(512)
    t["b_mean"] = Buf()
    t["b_rstd"] = Buf()
    t["t"] = [AR.alloc(512), AR.alloc(512)]
    t["b_t"] = [Buf(), Buf()]
    return t


def conv_mixer(C, AF32, ABF, XNO, W, col_a, col_g, KT, NT, CT, par, YOUT, yrow0, K=31, halo_mask=None, b_hm=None):
    P = C.P
    HALO = 32
    NTT = HALO + NT
    xn = ABF.alloc(KT, 512)
    b_xn = Buf()
    yglu = AF32.alloc(CT, NTT)
    b_yglu = Buf()
    sg = [AF32.alloc(512), AF32.alloc(512)]
    b_sg = [Buf(), Buf()]
    chunks_all = [(0, HALO)] + [(HALO + i * 512, 512) for i in range(NT // 512)]
    for (c0, n) in chunks_all:
        for k in range(KT):
            P.dma("sp", xn[:, k, 0:n], XNO.tile(k, c0, n), writes=[b_xn])
        if halo_mask is not None and c0 == 0:
            P.op("dve", lambda E: E.tensor_scalar(out=xn[:, :, 0:HALO], in0=xn[:, :, 0:HALO], scalar1=halo_mask, scalar2=None, op0=ALU.mult), reads=[b_hm], writes=[b_xn])
        for g in range(CT // 2):
            def evac(bi, ci, ps, psb, g=g, c0=c0, n=n):
                pass
            res = {}

            def evac2(bi, ci, ps, psb, res=res, g=g, c0=c0, n=n):
                res[bi] = (ps, psb)
                if bi == 3:
                    for j in range(2):
                        pa, ba = res[j]
                        pg, bg = res[2 + j]
                        i = j % 2
                        P.op("act", lambda E, pg=pg, i=i, n=n: E.activation(out=sg[i][:, 0:n], in_=pg, func=AF.Sigmoid), reads=[bg], writes=[b_sg[i]])
                        P.op("dve", lambda E, pa=pa, i=i, n=n, ct=g * 2 + j, c0=c0: E.tensor_tensor(out=yglu[:, ct, c0:c0 + n], in0=sg[i][:, 0:n], in1=pa, op=ALU.mult),
                             reads=[b_sg[i], ba], writes=[b_yglu])
            proj_blocks(C, xn, b_xn, [(0, n)], W, KT, [(col_a + g * 256, 256), (col_g + g * 256, 256)],
                        [(0, 0, 128), (0, 128, 128), (1, 0, 128), (1, 128, 128)], evac2)
    cv = AF32.alloc(CT, NT)
    b_cv = [Buf() for _ in range(CT)]
    cw, cb = par["cw"], par["cb"]
    for ct in range(CT):
        e = "dve"
        bct = b_cv[ct]
        off = HALO - (K - 1)
        P.op(e, lambda E, ct=ct: E.tensor_scalar(out=cv[:, ct, :], in0=yglu[:, ct, off:off + NT], scalar1=cw[:, ct, 0:1], scalar2=cb[:, ct:ct + 1], op0=ALU.mult, op1=ALU.add),
             reads=[b_yglu, par["b"]], writes=[bct])
        for j in range(1, K):
            P.op(e, lambda E, ct=ct, j=j: E.scalar_tensor_tensor(out=cv[:, ct, :], in0=yglu[:, ct, off + j:off + j + NT], scalar=cw[:, ct, j:j + 1], in1=cv[:, ct, :], op0=ALU.mult, op1=ALU.add),
                 reads=[b_yglu, par["b"]], writes=[bct])
    tmp = ln_tmp(AF32)
    yo = [ABF.alloc(512), ABF.alloc(512)]
    b_yo = [Buf(), Buf()]
    toks = []
    for hh in range(NT // 512):
        def out_fn(k, xh, b_xh, hh=hh):
            i = k % 2
            P.op("act", lambda E, k=k, xh=xh, i=i: E.activation(out=yo[i][:], in_=xh, func=AF.Silu, scale=par["lg"][:, k:k + 1], bias=par["lb"][:, k:k + 1]),
                 reads=[b_xh, par["b"]], writes=[b_yo[i]])
            toks.append(P.dma("sp", YOUT[yrow0 + k * 128:yrow0 + (k + 1) * 128, hh * 512:(hh + 1) * 512], yo[i][:], reads=[b_yo[i]], writes=[Buf()]))
        layernorm_fm(C, cv, b_cv, CT, 512, hh * 512, None, None, None, EPS, out_fn, tmp)
    return toks


def gelu_tanh(P, eng_v, out_ap, in_ps, b_in, n, t1, b_t1, t2, b_t2, writes):
    c2 = 2.0 * math.sqrt(2.0 / math.pi)
    P.op("act", lambda E: E.activation(out=t1, in_=in_ps, func=AF.Copy), reads=[b_in], writes=[b_t1])
    P.op("act", lambda E: E.activation(out=t2, in_=in_ps, func=AF.Square), reads=[b_in], writes=[b_t2])
    P.op(eng_v, lambda E: E.tensor_scalar(out=t2, in0=t2, scalar1=0.044715, scalar2=1.0, op0=ALU.mult, op1=ALU.add), reads=[b_t2], writes=[b_t2])
    P.op(eng_v, lambda E: E.tensor_tensor(out=t2, in0=t2, in1=t1, op=ALU.mult), reads=[b_t1, b_t2], writes=[b_t2])
    P.op("act", lambda E: E.activation(out=t2, in_=t2, func=AF.Sigmoid, scale=c2), reads=[b_t2], writes=[b_t2])
    P.op(eng_v, lambda E: E.tensor_tensor(out=out_ap, in0=t2, in1=t1, op=ALU.mult), reads=[b_t1, b_t2], writes=writes)


def sgu_mixer(C, AF32, ABF, XNO, W, col_u, col_v, KT, NT, CT, par, YOUT, yrow0):
    P = C.P
    HALO = 32
    xn = ABF.alloc(KT, 512)
    b_xn = Buf()
    u = AF32.alloc(CT, NT)
    b_u = Buf()
    v = AF32.alloc(CT, NT)
    b_v = Buf()
    t1 = [AF32.alloc(512), AF32.alloc(512)]
    t2 = [AF32.alloc(512), AF32.alloc(512)]
    b_t1 = [Buf(), Buf()]
    b_t2 = [Buf(), Buf()]
    for hh in range(NT // 512):
        c0 = HALO + hh * 512
        for k in range(KT):
            P.dma("sp", xn[:, k, :], XNO.tile(k, c0, 512), writes=[b_xn])
        for (col, dst, b_dst) in ((col_u, u, b_u), (col_v, v, b_v)):
            gsz = min(4, CT)
            for g in range(CT // gsz):
                def evac(bi, ci, ps, psb, g=g, dst=dst, b_dst=b_dst, hh=hh):
                    i = bi % 2
                    gelu_tanh(P, "dve", dst[:, g * gsz + bi, hh * 512:(hh + 1) * 512], ps, psb, 512, t1[i][:], b_t1[i], t2[i][:], b_t2[i], [b_dst])
                proj_blocks(C, xn, b_xn, [(0, 512)], W, KT, [(col + g * gsz * 128, gsz * 128)], [(0, j * 128, 128) for j in range(gsz)], evac)
    tmp = ln_tmp(AF32)
    vn = v
    b_vn = b_v
    for hh in range(NT // 512):
        def out_fn(k, xh, b_xh, hh=hh):
            P.op("act", lambda E, k=k, xh=xh: E.activation(out=vn[:, k, hh * 512:(hh + 1) * 512], in_=xh, func=AF.Identity, scale=par["lg"][:, k:k + 1], bias=par["lb"][:, k:k + 1]),
                 reads=[b_xh, par["b"]], writes=[b_vn])
        layernorm_fm(C, v, b_v, CT, 512, hh * 512, None, None, None, EPS, out_fn, tmp)
    vt = [ABF.alloc(128), ABF.alloc(128)]
    b_vt = [Buf(), Buf()]
    yo = [ABF.alloc(128), ABF.alloc(128)]
    b_yo = [Buf(), Buf()]
    sv = [AF32.alloc(128), AF32.alloc(128)]
    b_sv = [Buf(), Buf()]
    toks = []
    it = 0
    for h in range(CT):
        for nb in range(NT // 128):
            i = it % 2
            it += 1
            tb = 4 + i
            P.mm(lambda E, h=h, nb=nb, tb=tb: E.matmul(C.ps[tb][:, 0:128], lhsT=vn[:, h, nb * 128:(nb + 1) * 128], rhs=C.ident[:], start=True, stop=True),
                 reads=[b_vn, C.b_ident], writes=[C.psb[tb]], last=True)
            P.op("act", lambda E, i=i, tb=tb: E.activation(out=vt[i][:], in_=C.ps[tb][:, 0:128], func=AF.Copy), reads=[C.psb[tb]], writes=[b_vt[i]])
            bank = 2 + i
            P.mm(lambda E, h=h, i=i, bank=bank: E.matmul(C.ps[bank][:, 0:128], lhsT=vt[i][:], rhs=par["wsT"][:, h, :], start=True, stop=True),
                 reads=[b_vt[i], par["b"]], writes=[C.psb[bank]], last=True)
            P.op("dve", lambda E, h=h, i=i, bank=bank: E.tensor_tensor(out=sv[i][:], in0=C.ps[bank][:, 0:128], in1=par["bsb"][:, h, :], op=ALU.add),
                 reads=[C.psb[bank], par["b"]], writes=[b_sv[i]])
            P.op("dve", lambda E, h=h, nb=nb, i=i: E.tensor_tensor(out=yo[i][:], in0=sv[i][:], in1=u[:, h, nb * 128:(nb + 1) * 128], op=ALU.mult),
                 reads=[b_sv[i], b_u], writes=[b_yo[i]])
            toks.append(P.dma("sp", YOUT[yrow0 + h * 128:yrow0 + (h + 1) * 128, nb * 128:(nb + 1) * 128], yo[i][:], reads=[b_yo[i]], writes=[Buf()]))
    return toks


def rwkv_mixer(C, AF32, ABF, XN, W, cols, KT, S, NH, par, cst, YOUT, yrow0, TR=256, DBG=None):
    P = C.P
    L = 64
    NCH = TR // L
    NT_ = S // TR
    HW = NH * 64
    ones64 = C.ones[0:64, 0:64]
    id64 = C.ident[0:64, 0:64]
    xn = ABF.alloc(KT, TR)
    b_xn = Buf()
    R = AF32.alloc(NH, TR); Kb = AF32.alloc(NH, TR); V = AF32.alloc(NH, TR)
    b_R, b_K, b_V = Buf(), Buf(), Buf()
    XL = AF32.alloc(5, TR); b_XL = Buf()
    LW = AF32.alloc(NH, TR); b_LW = Buf()
    CS = [AF32.alloc(NH, TR), AF32.alloc(NH, TR)]; b_CS = [Buf(), Buf()]
    A = AF32.alloc(NH, TR); b_A = Buf()
    G = AF32.alloc(NH, TR); b_G = Buf()
    KK = AF32.alloc(NH, TR); b_KK = Buf()
    BON = AF32.alloc(NH, TR); b_BON = Buf()
    PX = AF32.alloc(NH, TR); b_PX = Buf()
    YB = AF32.alloc(NH, TR); b_YB = Buf()
    PL = AF32.alloc(NH, NCH); b_PL = Buf()
    carry = AF32.alloc(3 * NH + 5); b_carry = Buf()
    ST = [AF32.alloc(NH, 64), AF32.alloc(NH, 64)]; b_ST = [Buf(), Buf()]
    t64 = [AF32.alloc(TR), AF32.alloc(TR)]; b_t64 = [Buf(), Buf()]
    def cb():
        return [AF32.alloc(NH, 64), AF32.alloc(NH, 64)], [Buf(), Buf()]
    VT, b_VT = cb(); KtT, b_KtT = cb(); BtT, b_BtT = cb()
    M1, b_M1 = cb(); N1, b_N1 = cb(); N2, b_N2 = cb()
    Xm, b_X = cb(); Xt, b_Xt = cb()
    Aa, b_Aa = cb(); At, b_At = cb()
    RH, b_RH = cb(); NU, b_NU = cb()
    tmpS = AF32.alloc(NH, 64); b_tmpS = Buf()
    yo = [ABF.alloc(TR), ABF.alloc(TR)]; b_yo = [Buf(), Buf()]
    P.op("dve", lambda E: E.memset(carry[:, :], 0.0), writes=[b_carry])
    P.op("dve", lambda E: E.memset(ST[0][:, :, :], 0.0), writes=[b_ST[0]])
    bp = par["b"]
    toks = []
    sti = 0
    gch = 0
    for tt in range(NT_):
        c0 = tt * TR
        for k in range(KT):
            P.dma("sp", xn[:, k, :], XN.tile(k, c0, TR), writes=[b_xn])

        def shift_evac(dst, b_dst, hidx, cidx, mu, omu, M=64):
            def ev(bi, ci, ps, psb):
                h = hidx(bi)
                cc = cidx(bi)
                P.op("dve", lambda E: E.tensor_scalar(out=dst[0:M, h, :], in0=ps, scalar1=omu[0:M, h:h + 1], scalar2=None, op0=ALU.mult), reads=[psb, bp], writes=[b_dst])
                P.op("dve", lambda E: E.scalar_tensor_tensor(out=dst[0:M, h, 1:TR], in0=ps[:, 0:TR - 1], scalar=mu[0:M, h:h + 1], in1=dst[0:M, h, 1:TR], op0=ALU.mult, op1=ALU.add), reads=[psb, bp], writes=[b_dst])
                P.op("dve", lambda E: E.scalar_tensor_tensor(out=dst[0:M, h, 0:1], in0=carry[0:M, cc:cc + 1], scalar=mu[0:M, h:h + 1], in1=dst[0:M, h, 0:1], op0=ALU.mult, op1=ALU.add), reads=[b_carry, bp], writes=[b_dst])
                P.op("dve", lambda E: E.tensor_copy(out=carry[0:M, cc:cc + 1], in_=ps[:, TR - 1:TR]), reads=[psb], writes=[b_carry])
            return ev
        for qi, (nm, dst, b_dst) in enumerate((("r", R, b_R), ("k", Kb, b_K), ("v", V, b_V))):
            for h0 in range(0, NH, 8):
                nh = min(8, NH - h0)
                proj_blocks(C, xn, b_xn, [(0, TR)], W, KT, [(cols[nm] + h0 * 64, nh * 64)], [(0, j * 64, 64) for j in range(nh)],
                            shift_evac(dst, b_dst, lambda bi, h0=h0: h0 + bi, lambda bi, h0=h0, qi=qi: qi * NH + h0 + bi, par["mu_" + nm], par["omu_" + nm]))
        lb = [(0, 0, 64), (0, 64, 64), (0, 128, 64), (0, 192, 64), (0, 256, 32)]

        def ev_l(bi, ci, ps, psb):
            M = lb[bi][2]
            shift_evac(XL, b_XL, lambda b: b, lambda b: 3 * NH + b, par["mu_l"], par["omu_l"], M=M)(bi, ci, ps, psb)
        proj_blocks(C, xn, b_xn, [(0, TR)], W, KT, [(cols["lora"], 288)], lb, ev_l)

        P.op("act", lambda E: E.activation(out=XL[0:64, 0, :], in_=XL[0:64, 0, :], func=AF.Tanh), reads=[], writes=[b_XL])
        for j in (2, 3):
            P.op("act", lambda E, j=j: E.activation(out=XL[0:64, j, :], in_=XL[0:64, j, :], func=AF.Sigmoid), reads=[], writes=[b_XL])
        P.op("act", lambda E: E.activation(out=XL[0:32, 4, :], in_=XL[0:32, 4, :], func=AF.Sigmoid), reads=[], writes=[b_XL])
        for h in range(NH):
            bk = h % 2
            P.mm(lambda E, h=h, bk=bk: E.matmul(C.ps[bk][0:64, 0:TR], lhsT=par["wup"][0:64, h * 64:(h + 1) * 64], rhs=XL[0:64, 0, :], start=True, stop=True),
                 reads=[b_XL, bp], writes=[C.psb[bk]], last=True)
            P.op("act", lambda E, h=h, bk=bk: E.activation(out=LW[0:64, h, :], in_=C.ps[bk][0:64, 0:TR], func=AF.Sigmoid, bias=par["w0"][0:64, h:h + 1]), reads=[C.psb[bk], bp], writes=[b_LW])
            bk2 = 2 + h % 2
            P.mm(lambda E, h=h, bk2=bk2: E.matmul(C.ps[bk2][0:64, 0:TR], lhsT=par["aup"][0:64, h * 64:(h + 1) * 64], rhs=XL[0:64, 1, :], start=True, stop=True),
                 reads=[b_XL, bp], writes=[C.psb[bk2]], last=True)
            P.op("act", lambda E, h=h, bk2=bk2: E.activation(out=A[0:64, h, :], in_=C.ps[bk2][0:64, 0:TR], func=AF.Sigmoid, bias=par["a0"][0:64, h:h + 1]), reads=[C.psb[bk2], bp], writes=[b_A])
            bk3 = 4 + h % 2
            for j, M in ((0, 64), (1, 64), (2, 32)):
                P.mm(lambda E, h=h, bk3=bk3, j=j, M=M: E.matmul(C.ps[bk3][0:64, 0:TR], lhsT=par["gup"][0:M, j, h * 64:(h + 1) * 64], rhs=XL[0:M, 2 + j, :], start=(j == 0), stop=(j == 2)),
                     reads=[b_XL, bp], writes=[C.psb[bk3]], last=(j == 2))
            P.op("act", lambda E, h=h, bk3=bk3: E.activation(out=G[0:64, h, :], in_=C.ps[bk3][0:64, 0:TR], func=AF.Copy), reads=[C.psb[bk3]], writes=[b_G])
        P.op("dve", lambda E: E.tensor_scalar(out=LW[0:64, :, :], in0=LW[0:64, :, :], scalar1=-math.exp(-0.5), scalar2=None, op0=ALU.mult), reads=[], writes=[b_LW])

        for h in range(NH):
            P.op("dve", lambda E, h=h: E.tensor_scalar(out=KK[0:64, h, :], in0=Kb[0:64, h, :], scalar1=par["kk"][0:64, h:h + 1], scalar2=None, op0=ALU.mult), reads=[b_K, bp], writes=[b_KK])
        for h in range(NH):
            i = h % 2
            bk = 6 + i
            P.op("act", lambda E, h=h, i=i: E.activation(out=t64[i][0:64, :], in_=KK[0:64, h, :], func=AF.Square), reads=[b_KK], writes=[b_t64[i]])
            P.mm(lambda E, i=i, bk=bk: E.matmul(C.ps[bk][0:64, 0:TR], lhsT=ones64, rhs=t64[i][0:64, :], start=True, stop=True), reads=[b_t64[i], C.b_ones], writes=[C.psb[bk]], last=True)
            P.op("act", lambda E, i=i, bk=bk: E.activation(out=t64[i][0:64, :], in_=C.ps[bk][0:64, 0:TR], func=AF.Sqrt), reads=[C.psb[bk]], writes=[b_t64[i]])
            P.op("dve", lambda E, i=i: E.tensor_scalar(out=t64[i][0:64, :], in0=t64[i][0:64, :], scalar1=1e-12, scalar2=None, op0=ALU.max), reads=[], writes=[b_t64[i]])
            P.op("dve", lambda E, i=i: E.reciprocal(out=t64[i][0:64, :], in_=t64[i][0:64, :]), reads=[], writes=[b_t64[i]])
            P.op("dve", lambda E, h=h, i=i: E.tensor_tensor(out=KK[0:64, h, :], in0=KK[0:64, h, :], in1=t64[i][0:64, :], op=ALU.mult), reads=[b_t64[i]], writes=[b_KK])
        for h in range(NH):
            P.op("dve", lambda E, h=h: E.tensor_scalar(out=PX[0:64, h, :], in0=A[0:64, h, :], scalar1=par["ka"][0:64, h:h + 1], scalar2=par["oka"][0:64, h:h + 1], op0=ALU.mult, op1=ALU.add), reads=[b_A, bp], writes=[b_PX])
        P.op("dve", lambda E: E.tensor_tensor(out=Kb[0:64, :, :], in0=Kb[0:64, :, :], in1=PX[0:64, :, :], op=ALU.mult), reads=[b_PX], writes=[b_K])
        P.op("dve", lambda E: E.tensor_tensor(out=A[0:64, :, :], in0=A[0:64, :, :], in1=KK[0:64, :, :], op=ALU.mult), reads=[b_KK], writes=[b_A])
        P.op("dve", lambda E: E.tensor_tensor(out=PX[0:64, :, :], in0=R[0:64, :, :], in1=Kb[0:64, :, :], op=ALU.mult), reads=[b_R, b_K], writes=[b_PX])
        for h in range(NH):
            i = h % 2
            bk = 6 + i
            P.op("dve", lambda E, h=h, i=i: E.tensor_scalar(out=t64[i][0:64, :], in0=PX[0:64, h, :], scalar1=par["rk"][0:64, h:h + 1], scalar2=None, op0=ALU.mult), reads=[b_PX, bp], writes=[b_t64[i]])
            P.mm(lambda E, i=i, bk=bk: E.matmul(C.ps[bk][0:64, 0:TR], lhsT=ones64, rhs=t64[i][0:64, :], start=True, stop=True), reads=[b_t64[i], C.b_ones], writes=[C.psb[bk]], last=True)
            P.op("dve", lambda E, h=h, bk=bk: E.tensor_tensor(out=BON[0:64, h, :], in0=C.ps[bk][0:64, 0:TR], in1=V[0:64, h, :], op=ALU.mult), reads=[C.psb[bk], b_V], writes=[b_BON])

        if DBG is not None and tt == 0:
            for di, (buf_, b_) in enumerate(((R, b_R), (Kb, b_K), (V, b_V), (LW, b_LW), (A, b_A), (G, b_G), (KK, b_KK), (BON, b_BON))):
                toks.append(P.dma("sp", DBG[di, :, :].rearrange("p (h t) -> p h t", h=NH), buf_[0:64, :, :], reads=[b_], writes=[Buf()]))
        def v4(ap):
            return ap[0:64, :, :].rearrange("p h (c l) -> p h c l", l=L)
        src, b_src = LW, b_LW
        pi = 0
        d = 1
        while d < L:
            dst, b_dst = CS[pi], b_CS[pi]
            for h in range(NH):
                s4 = src[0:64, h, :].rearrange("p (c l) -> p c l", l=L)
                d4 = dst[0:64, h, :].rearrange("p (c l) -> p c l", l=L)
                P.op("dve", lambda E, s4=s4, d4=d4, d=d: E.tensor_copy(out=d4[:, :, 0:d], in_=s4[:, :, 0:d]), reads=[b_src], writes=[b_dst])
                P.op("dve", lambda E, s4=s4, d4=d4, d=d: E.tensor_tensor(out=d4[:, :, d:L], in0=s4[:, :, d:L], in1=s4[:, :, 0:L - d], op=ALU.add), reads=[b_src], writes=[b_dst])
            src, b_src = dst, b_dst
            pi = 1 - pi
            d *= 2
        CUM, b_CUM = src, b_src
        OTH, b_OTH = CS[pi], b_CS[pi]
        P.op("act", lambda E: E.activation(out=PX[0:64, :, :], in_=CUM[0:64, :, :], func=AF.Exp), reads=[b_CUM], writes=[b_PX])
        P.op("dve", lambda E: E.tensor_tensor(out=R[0:64, :, :], in0=R[0:64, :, :], in1=PX[0:64, :, :], op=ALU.mult), reads=[b_PX], writes=[b_R])
        for h in range(NH):
            p4 = PX[0:64, h, :].rearrange("p (c l) -> p c l", l=L)
            P.op("dve", lambda E, h=h, p4=p4: E.tensor_copy(out=PL[0:64, h, :], in_=p4[:, :, L - 1]), reads=[b_PX], writes=[b_PL])
        P.op("act", lambda E: E.activation(out=PX[0:64, :, :], in_=CUM[0:64, :, :], func=AF.Exp, scale=-1.0), reads=[b_CUM, b_PL, b_R], writes=[b_PX])
        P.op("dve", lambda E: E.tensor_tensor(out=Kb[0:64, :, :], in0=Kb[0:64, :, :], in1=PX[0:64, :, :], op=ALU.mult), reads=[b_PX], writes=[b_K])
        P.op("dve", lambda E: E.tensor_tensor(out=A[0:64, :, :], in0=A[0:64, :, :], in1=PX[0:64, :, :], op=ALU.mult), reads=[b_PX], writes=[b_A])
        P.op("dve", lambda E: E.tensor_tensor(out=OTH[0:64, :, :], in0=CUM[0:64, :, :], in1=LW[0:64, :, :], op=ALU.subtract), reads=[b_CUM, b_LW, b_K, b_A], writes=[b_OTH])
        P.op("act", lambda E: E.activation(out=OTH[0:64, :, :], in_=OTH[0:64, :, :], func=AF.Exp), reads=[], writes=[b_OTH])
        P.op("dve", lambda E: E.tensor_tensor(out=KK[0:64, :, :], in0=KK[0:64, :, :], in1=OTH[0:64, :, :], op=ALU.mult), reads=[b_OTH], writes=[b_KK])

        def do_chunk(c, q, cs, So, b_So, Sn, b_Sn):

            def hv(buf, h):
                return buf[0:64, h, :]
            for (src_, b_s, dstl, b_dl, bank) in ((V, b_V, VT, b_VT, 0), (Kb, b_K, KtT, b_KtT, 1), (A, b_A, BtT, b_BtT, 2)):
                for h in range(NH):
                    P.mm(lambda E, src_=src_, h=h, bank=bank: E.matmul(C.ps[bank][0:64, h * 64:(h + 1) * 64], lhsT=src_[0:64, h, cs], rhs=id64, start=True, stop=True),
                         reads=[b_s, C.b_ident], writes=[C.psb[bank]], last=(h == NH - 1))
                P.op("act", lambda E, dstl=dstl, bank=bank: E.activation(out=dstl[q][0:64, :, :], in_=C.ps[bank][0:64, 0:HW].rearrange("p (h l) -> p h l", l=64), func=AF.Copy),
                     reads=[C.psb[bank]], writes=[b_dl[q]])
            specs = ((Kb, b_K, KK, b_KK, M1, b_M1, "mS8", 3), (A, b_A, KK, b_KK, Aa, b_Aa, "mS8", 4), (KK, b_KK, A, b_A, At, b_At, "mL8", 5),
                     (Kb, b_K, R, b_R, N1, b_N1, "mI8", 6), (A, b_A, R, b_R, N2, b_N2, "mI8", 7))
            for (l_, b_l, r_, b_r, dstl, b_dl, mk, bank) in specs:
                for h in range(NH):
                    P.mm(lambda E, l_=l_, r_=r_, h=h, bank=bank: E.matmul(C.ps[bank][0:64, h * 64:(h + 1) * 64], lhsT=l_[0:64, h, cs], rhs=r_[0:64, h, cs], start=True, stop=True),
                         reads=[b_l, b_r], writes=[C.psb[bank]], last=(h == NH - 1))
                P.op("dve", lambda E, dstl=dstl, bank=bank, mk=mk: E.tensor_tensor(out=dstl[q][0:64, :, :].rearrange("p h l -> p (h l)"), in0=C.ps[bank][0:64, 0:HW], in1=cst[mk][0:64, 0:HW], op=ALU.mult),
                     reads=[C.psb[bank], cst["b"]], writes=[b_dl[q]])
            fl = lambda t: t[0:64, :, :].rearrange("p h l -> p (h l)")
            P.op("dve", lambda E: E.tensor_tensor(out=fl(Xm[q]), in0=cst["ident8"][0:64, 0:HW], in1=fl(Aa[q]), op=ALU.subtract), reads=[b_Aa[q], cst["b"]], writes=[b_X[q]])
            P.op("dve", lambda E: E.tensor_tensor(out=fl(Xt[q]), in0=cst["ident8"][0:64, 0:HW], in1=fl(At[q]), op=ALU.subtract), reads=[b_At[q], cst["b"]], writes=[b_Xt[q]])
            NIT = 5
            for it in range(NIT):
                lastit = (it == NIT - 1)
                for h in range(NH):
                    P.mm(lambda E, h=h: E.matmul(C.ps[0][0:64, h * 64:(h + 1) * 64], lhsT=At[q][0:64, h, :], rhs=Aa[q][0:64, h, :], start=True, stop=True),
                         reads=[b_At[q], b_Aa[q]], writes=[C.psb[0]], last=(h == NH - 1))
                if not lastit:
                    for h in range(NH):
                        P.mm(lambda E, h=h: E.matmul(C.ps[1][0:64, h * 64:(h + 1) * 64], lhsT=Aa[q][0:64, h, :], rhs=At[q][0:64, h, :], start=True, stop=True),
                             reads=[b_At[q], b_Aa[q]], writes=[C.psb[1]], last=(h == NH - 1))
                P.op("act", lambda E: E.activation(out=fl(Aa[q]), in_=C.ps[0][0:64, 0:HW], func=AF.Copy), reads=[C.psb[0], C.psb[1] if not lastit else C.psb[0]], writes=[b_Aa[q]])
                if not lastit:
                    P.op("act", lambda E: E.activation(out=fl(At[q]), in_=C.ps[1][0:64, 0:HW], func=AF.Copy), reads=[C.psb[1]], writes=[b_At[q]])
                for h in range(NH):
                    P.mm(lambda E, h=h: E.matmul(C.ps[2][0:64, h * 64:(h + 1) * 64], lhsT=Xt[q][0:64, h, :], rhs=Aa[q][0:64, h, :], start=True, stop=True),
                         reads=[b_Xt[q], b_Aa[q]], writes=[C.psb[2]], last=(h == NH - 1))
                if not lastit:
                    for h in range(NH):
                        P.mm(lambda E, h=h: E.matmul(C.ps[3][0:64, h * 64:(h + 1) * 64], lhsT=Aa[q][0:64, h, :], rhs=Xt[q][0:64, h, :], start=True, stop=True),
                             reads=[b_Xt[q], b_Aa[q]], writes=[C.psb[3]], last=(h == NH - 1))
                P.op("dve", lambda E: E.tensor_tensor(out=fl(Xm[q]), in0=C.ps[2][0:64, 0:HW], in1=fl(Xm[q]), op=ALU.add), reads=[C.psb[2], C.psb[3] if not lastit else C.psb[2]], writes=[b_X[q]])
                if not lastit:
                    P.op("dve", lambda E: E.tensor_tensor(out=fl(Xt[q]), in0=C.ps[3][0:64, 0:HW], in1=fl(Xt[q]), op=ALU.add), reads=[C.psb[3]], writes=[b_Xt[q]])
            for h in range(NH):
                P.mm(lambda E, h=h: E.matmul(C.ps[4][0:64, h * 64:(h + 1) * 64], lhsT=KK[0:64, h, cs], rhs=So[0:64, h, :], start=True, stop=False),
                     reads=[b_KK, b_So], writes=[C.psb[4]], last=False)
                P.mm(lambda E, h=h: E.matmul(C.ps[4][0:64, h * 64:(h + 1) * 64], lhsT=M1[q][0:64, h, :], rhs=VT[q][0:64, h, :], start=False, stop=True),
                     reads=[b_M1[q], b_VT[q]], writes=[C.psb[4]], last=(h == NH - 1))
            P.op("act", lambda E: E.activation(out=fl(RH[q]), in_=C.ps[4][0:64, 0:HW], func=AF.Copy), reads=[C.psb[4]], writes=[b_RH[q]])
            for h in range(NH):
                P.mm(lambda E, h=h: E.matmul(C.ps[5][0:64, h * 64:(h + 1) * 64], lhsT=Xm[q][0:64, h, :], rhs=RH[q][0:64, h, :], start=True, stop=True),
                     reads=[b_X[q], b_RH[q]], writes=[C.psb[5]], last=(h == NH - 1))
            P.op("act", lambda E: E.activation(out=fl(NU[q]), in_=C.ps[5][0:64, 0:HW], func=AF.Copy, scale=-1.0), reads=[C.psb[5]], writes=[b_NU[q]])
            for h in range(NH):
                P.mm(lambda E, h=h: E.matmul(C.ps[6][0:64, h * 64:(h + 1) * 64], lhsT=So[0:64, h, :], rhs=R[0:64, h, cs], start=True, stop=False),
                     reads=[b_So, b_R], writes=[C.psb[6]], last=False)
                P.mm(lambda E, h=h: E.matmul(C.ps[6][0:64, h * 64:(h + 1) * 64], lhsT=VT[q][0:64, h, :], rhs=N1[q][0:64, h, :], start=False, stop=False),
                     reads=[b_VT[q], b_N1[q]], writes=[C.psb[6]], last=False)
                P.mm(lambda E, h=h: E.matmul(C.ps[6][0:64, h * 64:(h + 1) * 64], lhsT=NU[q][0:64, h, :], rhs=N2[q][0:64, h, :], start=False, stop=True),
                     reads=[b_NU[q], b_N2[q]], writes=[C.psb[6]], last=(h == NH - 1))
            P.op("act", lambda E: E.activation(out=YB[0:64, :, cs], in_=C.ps[6][0:64, 0:HW].rearrange("p (h l) -> p h l", l=64), func=AF.Copy), reads=[C.psb[6]], writes=[b_YB])
            for h in range(NH):
                P.mm(lambda E, h=h: E.matmul(C.ps[7][0:64, h * 64:(h + 1) * 64], lhsT=KtT[q][0:64, h, :], rhs=VT[q][0:64, h, :], start=True, stop=False),
                     reads=[b_KtT[q], b_VT[q]], writes=[C.psb[7]], last=False)
                P.mm(lambda E, h=h: E.matmul(C.ps[7][0:64, h * 64:(h + 1) * 64], lhsT=BtT[q][0:64, h, :], rhs=NU[q][0:64, h, :], start=False, stop=True),
                     reads=[b_BtT[q], b_NU[q]], writes=[C.psb[7]], last=(h == NH - 1))
            P.op("dve", lambda E: E.tensor_tensor(out=fl(tmpS), in0=C.ps[7][0:64, 0:HW], in1=fl(So), op=ALU.add), reads=[C.psb[7], b_So], writes=[b_tmpS])
            for h in range(NH):
                P.op("dve", lambda E, h=h, c=c: E.tensor_scalar(out=Sn[0:64, h, :], in0=tmpS[0:64, h, :], scalar1=PL[0:64, h, c:c + 1], scalar2=None, op0=ALU.mult), reads=[b_tmpS, b_PL], writes=[b_Sn])

        for c in range(NCH):
            q = gch % 2
            gch += 1
            do_chunk(c, q, slice(c * L, (c + 1) * L), ST[sti], b_ST[sti], ST[1 - sti], b_ST[1 - sti])
            sti = 1 - sti
        if DBG is not None and tt == 0:
            for di, (buf_, b_) in enumerate(((R, b_R), (Kb, b_K), (A, b_A), (KK, b_KK), (YB, b_YB))):
                toks.append(P.dma("sp", DBG[8 + di, :, :].rearrange("p (h t) -> p h t", h=NH), buf_[0:64, :, :], reads=[b_], writes=[Buf()]))
        for h in range(NH):
            i = h % 2
            P.mm(lambda E, h=h: E.matmul(C.ps[0][0:64, 0:TR], lhsT=ones64, rhs=YB[0:64, h, :], start=True, stop=True), reads=[b_YB, C.b_ones], writes=[C.psb[0]], last=True)
            P.op("act", lambda E, h=h, i=i: E.activation(out=t64[i][0:64, :], in_=YB[0:64, h, :], func=AF.Square), reads=[b_YB], writes=[b_t64[i]])
            P.mm(lambda E, i=i: E.matmul(C.ps[1][0:64, 0:TR], lhsT=ones64, rhs=t64[i][0:64, :], start=True, stop=True), reads=[b_t64[i], C.b_ones], writes=[C.psb[1]], last=True)
            mean = PX[0:64, 0, :]
            var = PX[0:64, 1, :]
            P.op("act", lambda E: E.activation(out=mean, in_=C.ps[0][0:64, 0:TR], func=AF.Copy, scale=1.0 / 64), reads=[C.psb[0]], writes=[b_PX])
            P.op("dve", lambda E: E.tensor_tensor(out=var, in0=mean, in1=mean, op=ALU.mult), reads=[], writes=[b_PX])
            P.op("dve", lambda E: E.scalar_tensor_tensor(out=var, in0=C.ps[1][0:64, 0:TR], scalar=1.0 / 64, in1=var, op0=ALU.mult, op1=ALU.subtract), reads=[C.psb[1]], writes=[b_PX])
            P.op("dve", lambda E: E.tensor_scalar(out=var, in0=var, scalar1=0.0, scalar2=64e-5, op0=ALU.max, op1=ALU.add), reads=[], writes=[b_PX])
            P.op("act", lambda E: E.activation(out=var, in_=var, func=AF.Sqrt), reads=[], writes=[b_PX])
            P.op("dve", lambda E: E.reciprocal(out=var, in_=var), reads=[], writes=[b_PX])
            P.op("dve", lambda E, h=h: E.tensor_tensor(out=YB[0:64, h, :], in0=YB[0:64, h, :], in1=mean, op=ALU.subtract), reads=[b_PX], writes=[b_YB])
            P.op("dve", lambda E, h=h: E.tensor_tensor(out=YB[0:64, h, :], in0=YB[0:64, h, :], in1=var, op=ALU.mult), reads=[b_PX], writes=[b_YB])
            P.op("dve", lambda E, h=h: E.tensor_scalar(out=YB[0:64, h, :], in0=YB[0:64, h, :], scalar1=par["lg"][0:64, h:h + 1], scalar2=par["lb"][0:64, h:h + 1], op0=ALU.mult, op1=ALU.add), reads=[bp], writes=[b_YB])
            P.op("dve", lambda E, h=h: E.tensor_tensor(out=YB[0:64, h, :], in0=YB[0:64, h, :], in1=BON[0:64, h, :], op=ALU.add), reads=[b_BON], writes=[b_YB])
            P.op("dve", lambda E, h=h, i=i: E.tensor_tensor(out=yo[i][0:64, :], in0=YB[0:64, h, :], in1=G[0:64, h, :], op=ALU.mult), reads=[b_G, b_YB], writes=[b_yo[i]])
            toks.append(P.dma("sp", YOUT.rows(yrow0 + h * 64, 64, c0, TR), yo[i][0:64, :], reads=[b_yo[i]], writes=[Buf()]))
    return toks


def sb_mixer(C, AF32, ABF, XN, W, cols, KT, S, NHS, cst, YOUT, yrow0, TT=512):
    P = C.P
    NB = S // 128
    scale = 128 ** -0.5
    xn = ABF.alloc(KT, TT)
    b_xn = Buf()
    QT = ABF.alloc(NHS, S); b_QT = Buf()
    KTt = ABF.alloc(NHS, S); b_KT = Buf()
    VT = ABF.alloc(NB, NHS, 128); b_VT = Buf()
    vf = [AF32.alloc(TT), AF32.alloc(TT)]; b_vf = [Buf(), Buf()]
    nvf = 0
    for tt in range(S // TT):
        c0 = tt * TT
        for k in range(KT):
            P.dma("sp", xn[:, k, :], XN.tile(k, c0, TT), writes=[b_xn])
        for h0 in range(0, NHS, 4):
            nh = min(4, NHS - h0)
            blocks = [(0, j * 128, 128) for j in range(nh)]

            def ev_q(bi, ci, ps, psb, h0=h0, c0=c0):
                P.op("act", lambda E: E.activation(out=QT[:, h0 + bi, c0:c0 + TT], in_=ps, func=AF.Copy, scale=scale), reads=[psb], writes=[b_QT])

            def ev_k(bi, ci, ps, psb, h0=h0, c0=c0):
                P.op("act", lambda E: E.activation(out=KTt[:, h0 + bi, c0:c0 + TT], in_=ps, func=AF.Copy), reads=[psb], writes=[b_KT])
            proj_blocks(C, xn, b_xn, [(0, TT)], W, KT, [(cols["q"] + h0 * 128, nh * 128)], blocks, ev_q, bank0=0)
            proj_blocks(C, xn, b_xn, [(0, TT)], W, KT, [(cols["k"] + h0 * 128, nh * 128)], blocks, ev_k, bank0=4)
            for j in range(nh):
                h = h0 + j

                def ev_v(bi, ci, ps, psb, h=h, c0=c0):
                    nonlocal nvf
                    i = nvf % 2
                    nvf += 1
                    P.op("act", lambda E, i=i: E.activation(out=vf[i][:, :], in_=ps, func=AF.Copy), reads=[psb], writes=[b_vf[i]])
                    for bb in range(TT // 128):
                        bank = 4 + (bb % 4)
                        P.mm(lambda E, i=i, bb=bb, bank=bank: E.matmul(C.ps[bank][:, 0:128], lhsT=vf[i][:, bb * 128:(bb + 1) * 128], rhs=C.ident[:], start=True, stop=True),
                             reads=[b_vf[i], C.b_ident], writes=[C.psb[bank]], last=True)
                        P.op("dve", lambda E, bb=bb, bank=bank: E.tensor_copy(out=VT[:, c0 // 128 + bb, h, :], in_=C.ps[bank][:, 0:128]), reads=[C.psb[bank]], writes=[b_VT])
                proj_blocks(C, xn, b_xn, [(0, TT)], W, KT, [(cols["v"] + h * 128, 128)], [(0, 0, 128)], ev_v, bank0=0)
    E1 = [AF32.alloc(128), AF32.alloc(128)]; b_E1 = [Buf(), Buf()]
    SP = [AF32.alloc(128), AF32.alloc(128)]; b_SP = [Buf(), Buf()]
    ARG = [AF32.alloc(128), AF32.alloc(128)]; b_ARG = [Buf(), Buf()]
    WT = [ABF.alloc(128), ABF.alloc(128)]; b_WT = [Buf(), Buf()]
    SPS = AF32.alloc(128); b_SPS = Buf()
    yo = [ABF.alloc(128), ABF.alloc(128)]; b_yo = [Buf(), Buf()]
    toks = []
    it = 0
    nq = 0
    for h in range(NHS):
        for qb in range(NB):
            bo = 4 + nq % 2
            nq += 1
            for kb in range(qb, -1, -1):
                i = it % 2
                it += 1
                bz = i
                bl = 2 + i
                diag = (kb == qb)
                P.mm(lambda E, h=h, kb=kb, qb=qb, bz=bz: E.matmul(C.ps[bz][:, 0:128], lhsT=KTt[:, h, kb * 128:(kb + 1) * 128], rhs=QT[:, h, qb * 128:(qb + 1) * 128], start=True, stop=True),
                     reads=[b_KT, b_QT], writes=[C.psb[bz]], last=True)
                P.op("act", lambda E, i=i, bz=bz: E.activation(out=E1[i][:, :], in_=C.ps[bz][:, 0:128], func=AF.Exp), reads=[C.psb[bz]], writes=[b_E1[i]])
                P.op("act", lambda E, i=i: E.activation(out=SP[i][:, :], in_=E1[i][:, :], func=AF.Ln, bias=1.0), reads=[b_E1[i]], writes=[b_SP[i]])
                P.op("dve", lambda E, i=i, bz=bz: E.tensor_tensor(out=ARG[i][:, :], in0=C.ps[bz][:, 0:128], in1=SP[i][:, :], op=ALU.subtract), reads=[C.psb[bz], b_SP[i]], writes=[b_ARG[i]])
                if diag:
                    P.op("dve", lambda E, i=i: E.tensor_tensor(out=SP[i][:, :], in0=SP[i][:, :], in1=cst["mS"][:, :], op=ALU.mult), reads=[cst["b"], b_ARG[i]], writes=[b_SP[i]])
                P.mm(lambda E, i=i, bl=bl, diag=diag: E.matmul(C.ps[bl][:, 0:128], lhsT=cst["ntri"][:, :], rhs=SP[i][:, :], start=True, stop=diag),
                     reads=[b_SP[i], cst["b"]], writes=[C.psb[bl]], last=diag)
                if not diag:
                    P.mm(lambda E, bl=bl: E.matmul(C.ps[bl][:, 0:128], lhsT=cst["nones"][:, :], rhs=SPS[:, :], start=False, stop=True),
                         reads=[b_SPS, cst["b"]], writes=[C.psb[bl]], last=True)
                P.op("dve", lambda E, i=i, bl=bl: E.tensor_tensor(out=ARG[i][:, :], in0=ARG[i][:, :], in1=C.ps[bl][:, 0:128], op=ALU.add), reads=[C.psb[bl]], writes=[b_ARG[i]])
                P.op("act", lambda E, i=i: E.activation(out=WT[i][:, :], in_=ARG[i][:, :], func=AF.Exp), reads=[b_ARG[i]], writes=[b_WT[i]])
                if diag:
                    P.op("dve", lambda E, i=i: E.tensor_tensor(out=WT[i][:, :], in0=WT[i][:, :], in1=cst["mSb"][:, :], op=ALU.mult), reads=[cst["b"]], writes=[b_WT[i]])
                if kb > 0:
                    if diag:
                        P.op("pool", lambda E, i=i: E.tensor_copy(out=SPS[:, :], in_=SP[i][:, :]), reads=[b_SP[i]], writes=[b_SPS])
                    else:
                        P.op("pool", lambda E, i=i: E.tensor_tensor(out=SPS[:, :], in0=SPS[:, :], in1=SP[i][:, :], op=ALU.add), reads=[b_SP[i]], writes=[b_SPS])
                P.mm(lambda E, i=i, h=h, kb=kb, qb=qb, bo=bo: E.matmul(C.ps[bo][:, 0:128], lhsT=VT[:, kb, h, :], rhs=WT[i][:, :], start=(kb == qb), stop=(kb == 0)),
                     reads=[b_VT, b_WT[i]], writes=[C.psb[bo]], last=True)
            j = nq % 2
            P.op("act", lambda E, j=j, bo=bo: E.activation(out=yo[j][:, :], in_=C.ps[bo][:, 0:128], func=AF.Copy), reads=[C.psb[bo]], writes=[b_yo[j]])
            toks.append(P.dma("sp", YOUT.rows(yrow0 + h * 128, 128, qb * 128, 128), yo[j][:, :], reads=[b_yo[j]], writes=[Buf()]))
    return toks

import numpy as np, ml_dtypes

LORA = 288
CONV_K = 31


class Cfg:
    def __init__(self, D=4096, DFF=11008, SEQ=2048, B=4, DG=1024):
        self.D, self.DFF, self.SEQ, self.B, self.DG = D, DFF, SEQ, B, DG
        self.KT = D // 128
        self.NT = SEQ // 2
        self.CT = DG // 128
        self.NHR = DG // 64
        self.NHS = DG // 128
        self.NHRc = self.NHR // 2
        self.NHSc = self.NHS // 2
        self.NRW = 3 * DG + LORA
        self.NIN = 2 * DG + self.NRW + 2 * DG + 3 * DG
        c = 0
        self.c_ca = c; c += DG
        self.c_cg = c; c += DG
        self.c_su = c; c += DG
        self.c_sv = c; c += DG
        self.c_r = c; c += self.NHRc * 64
        self.c_k = c; c += self.NHRc * 64
        self.c_v = c; c += self.NHRc * 64
        self.c_l = c; c += LORA
        self.c_q = c; c += self.NHSc * 128
        self.c_sk = c; c += self.NHSc * 128
        self.c_svv = c; c += self.NHSc * 128
        self.NC = c
        CT = self.CT
        o = 0
        self.o_cw = o; o += CT * CONV_K
        self.o_cb = o; o += CT
        self.o_clg = o; o += CT
        self.o_clb = o; o += CT
        self.o_slg = o; o += CT
        self.o_slb = o; o += CT
        self.NP128 = o
        NH = self.NHRc
        o = 0
        self.q_mu = o; o += 3 * NH + 5
        self.q_ka = o; o += NH
        self.q_w0 = o; o += NH
        self.q_a0 = o; o += NH
        self.q_kk = o; o += NH
        self.q_rk = o; o += NH
        self.q_lg = o; o += NH
        self.q_lb = o; o += NH
        self.q_wup = o; o += NH * 64
        self.q_aup = o; o += NH * 64
        self.q_gup = o; o += 3 * NH * 64
        self.NP64 = o
        self.NC128 = 4 * 128 + 4 * NH * 64
        self.YOWN = 2 * DG
        self.YALL = self.NHRc * 64 + self.NHSc * 128
        self.TR = 256 if D < 4096 else 128


def consts_np(cfg):
    NH = cfg.NHRc
    c = np.zeros((128, cfg.NC128), np.float32)
    i = np.arange(128)
    c[:, 0:128] = np.eye(128)
    c[:, 128:256] = (i[:, None] < i[None, :])
    c[:, 256:384] = -(i[:, None] > i[None, :]).astype(np.float32)
    c[:, 384:512] = -1.0
    j = np.arange(64)
    e8 = np.tile(np.eye(64, dtype=np.float32), (1, NH))
    mS = np.tile((j[:, None] < j[None, :]).astype(np.float32), (1, NH))
    mI = np.tile((j[:, None] <= j[None, :]).astype(np.float32), (1, NH))
    mL = np.tile((j[:, None] > j[None, :]).astype(np.float32), (1, NH))
    o = 512
    for m in (e8, mS, mI, mL):
        c[0:64, o:o + NH * 64] = m
        o += NH * 64
    return c


def tile128(v, nt):
    return np.ascontiguousarray(np.asarray(v).reshape(nt, 128).T)


def mixer_inputs(cfg, L, role, xn_full_T, own_T):
    DG, NH, NHS, CT = cfg.DG, cfg.NHRc, cfg.NHSc, cfg.CT
    W = L["mix_w_in"]
    R0 = 2 * DG
    S0 = 2 * DG + cfg.NRW
    B0 = S0 + 2 * DG
    hr = role * NH * 64
    hs = role * NHS * 128
    colsel = np.concatenate([
        np.arange(0, 2 * DG), np.arange(S0, S0 + 2 * DG),
        R0 + hr + np.arange(NH * 64), R0 + DG + hr + np.arange(NH * 64), R0 + 2 * DG + hr + np.arange(NH * 64),
        R0 + 3 * DG + np.arange(LORA),
        B0 + hs + np.arange(NHS * 128), B0 + DG + hs + np.arange(NHS * 128), B0 + 2 * DG + hs + np.arange(NHS * 128)])
    assert len(colsel) == cfg.NC
    wmix = np.ascontiguousarray(W[:, colsel])
    p128 = np.zeros((128, cfg.NP128), np.float32)
    cw = L["conv_w"]
    p128[:, cfg.o_cw:cfg.o_cw + CT * CONV_K] = cw.T.reshape(CT, 128, CONV_K).transpose(1, 0, 2).reshape(128, CT * CONV_K)
    p128[:, cfg.o_cb:cfg.o_cb + CT] = tile128(L["conv_b"], CT)
    p128[:, cfg.o_clg:cfg.o_clg + CT] = tile128(L["conv_ln_g"], CT)
    p128[:, cfg.o_clb:cfg.o_clb + CT] = tile128(L["conv_ln_b"], CT)
    p128[:, cfg.o_slg:cfg.o_slg + CT] = tile128(L["sgu_ln_g"], CT)
    p128[:, cfg.o_slb:cfg.o_slb + CT] = tile128(L["sgu_ln_b"], CT)
    ws = L["sgu_w_s"]
    wsT = np.ascontiguousarray(ws.transpose(2, 0, 1).reshape(128, CT * 128))
    bsb = np.ascontiguousarray(np.broadcast_to(L["sgu_b_s"].reshape(1, CT * 128), (128, CT * 128)))
    p64 = np.zeros((64, cfg.NP64), np.float32)
    mu = L["rwkv_mu"]

    def h64(v, h0):
        return np.asarray(v).reshape(-1, 64)[h0:h0 + NH].T
    hh = role * NH
    o = cfg.q_mu
    p64[:, o:o + NH] = h64(mu[0:DG], hh); o += NH
    p64[:, o:o + NH] = h64(mu[DG:2 * DG], hh); o += NH
    p64[:, o:o + NH] = h64(mu[2 * DG:3 * DG], hh); o += NH
    ml = mu[3 * DG:3 * DG + LORA]
    for b, (s, n) in enumerate(((0, 64), (64, 64), (128, 64), (192, 64), (256, 32))):
        p64[0:n, o + b] = ml[s:s + n]
    p64[:, cfg.q_ka:cfg.q_ka + NH] = h64(L["rwkv_k_a"], hh)
    p64[:, cfg.q_w0:cfg.q_w0 + NH] = h64(L["rwkv_w0"], hh)
    p64[:, cfg.q_a0:cfg.q_a0 + NH] = h64(L["rwkv_a0"], hh)
    p64[:, cfg.q_kk:cfg.q_kk + NH] = h64(L["rwkv_k_k"], hh)
    p64[:, cfg.q_rk:cfg.q_rk + NH] = h64(L["rwkv_r_k"].reshape(-1), hh)
    p64[:, cfg.q_lg:cfg.q_lg + NH] = h64(L["rwkv_lnx_g"], hh)
    p64[:, cfg.q_lb:cfg.q_lb + NH] = h64(L["rwkv_lnx_b"], hh)
    p64[:, cfg.q_wup:cfg.q_wup + NH * 64] = L["rwkv_w_up"][:, hr:hr + NH * 64]
    p64[:, cfg.q_aup:cfg.q_aup + NH * 64] = L["rwkv_a_up"][:, hr:hr + NH * 64]
    gu = L["rwkv_g_up"][:, hr:hr + NH * 64]
    for b, (s, n) in enumerate(((0, 64), (64, 64), (128, 32))):
        p64[0:n, cfg.q_gup + b * NH * 64: cfg.q_gup + (b + 1) * NH * 64] = gu[s:s + n]
    return {"xno": own_T, "xn": xn_full_T, "wmix": wmix, "p128": p128, "wsT": wsT, "bsb": bsb, "p64": p64, "c128": consts_np(cfg)}


def mixer_body(P, C, cfg, T, AR, do=("conv", "sgu", "rwkv", "sb"), halo_mask=None, b_hm=None):
    CT, NH = cfg.CT, cfg.NHRc
    AF32 = ArenaView(AR, F32)
    ABF = ArenaView(AR, BF16)
    p128 = AF32.alloc(cfg.NP128); bp128 = Buf()
    P.dma("sp", p128, T["p128"], writes=[bp128])
    c128 = AF32.alloc(cfg.NC128); bc = Buf()
    P.dma("sp", c128, T["c128"], writes=[bc])
    p64 = AF32.alloc(cfg.NP64); bp64 = Buf()
    P.dma("sp", p64[0:64, :], T["p64"], writes=[bp64])
    om = AF32.alloc(4 * NH + 5)
    P.op("dve", lambda E: E.tensor_scalar(out=om[0:64, :], in0=p64[0:64, cfg.q_mu:cfg.q_mu + 4 * NH + 5], scalar1=-1.0, scalar2=1.0, op0=ALU.mult, op1=ALU.add), reads=[bp64], writes=[bp64])
    wsT = ABF.alloc(CT, 128)
    P.dma("pool", wsT, T["wsT"].rearrange("p (h i) -> p h i", i=128), writes=[bp128])
    P.op("dve", lambda E: E.memset(wsT[64:128, :, 0:64], 0.0), reads=[], writes=[bp128])
    bsb = AF32.alloc(CT, 128)
    P.dma("sp", bsb, T["bsb"].rearrange("p (h i) -> p h i", i=128), writes=[bp128])
    mSb = ABF.alloc(128)
    P.op("dve", lambda E: E.tensor_copy(out=mSb, in_=c128[:, 128:256]), reads=[bc], writes=[bc])
    C.ident = c128[:, 0:128]
    C.b_ident = bc
    cst = {"b": bc, "mS": c128[:, 128:256], "ntri": c128[:, 256:384], "nones": c128[:, 384:512], "mSb": mSb,
           "ident8": c128[:, 512:512 + NH * 64], "mS8": c128[:, 512 + NH * 64:512 + 2 * NH * 64],
           "mI8": c128[:, 512 + 2 * NH * 64:512 + 3 * NH * 64], "mL8": c128[:, 512 + 3 * NH * 64:512 + 4 * NH * 64]}
    K = CONV_K
    par_c = {"b": bp128, "cw": p128[:, cfg.o_cw:cfg.o_cw + CT * K].rearrange("p (c k) -> p c k", k=K), "cb": p128[:, cfg.o_cb:cfg.o_cb + CT],
             "lg": p128[:, cfg.o_clg:cfg.o_clg + CT], "lb": p128[:, cfg.o_clb:cfg.o_clb + CT]}
    par_s = {"b": bp128, "lg": p128[:, cfg.o_slg:cfg.o_slg + CT], "lb": p128[:, cfg.o_slb:cfg.o_slb + CT], "wsT": wsT, "bsb": bsb}
    q = cfg.q_mu
    par_r = {"b": bp64, "mu_r": p64[:, q:q + NH], "mu_k": p64[:, q + NH:q + 2 * NH], "mu_v": p64[:, q + 2 * NH:q + 3 * NH], "mu_l": p64[:, q + 3 * NH:q + 3 * NH + 5],
             "omu_r": om[:, 0:NH], "omu_k": om[:, NH:2 * NH], "omu_v": om[:, 2 * NH:3 * NH], "omu_l": om[:, 3 * NH:3 * NH + 5],
             "ka": p64[:, cfg.q_ka:cfg.q_ka + NH], "oka": om[:, 3 * NH + 5:4 * NH + 5],
             "w0": p64[:, cfg.q_w0:cfg.q_w0 + NH], "a0": p64[:, cfg.q_a0:cfg.q_a0 + NH], "kk": p64[:, cfg.q_kk:cfg.q_kk + NH], "rk": p64[:, cfg.q_rk:cfg.q_rk + NH],
             "lg": p64[:, cfg.q_lg:cfg.q_lg + NH], "lb": p64[:, cfg.q_lb:cfg.q_lb + NH],
             "wup": p64[:, cfg.q_wup:cfg.q_wup + NH * 64], "aup": p64[:, cfg.q_aup:cfg.q_aup + NH * 64],
             "gup": p64[:, cfg.q_gup:cfg.q_gup + 3 * NH * 64].rearrange("p (b n) -> p b n", b=3)}
    base = AR.off
    toks = []
    if "conv" in do:
        toks += conv_mixer(C, AF32, ABF, T["xno"], T["wmix"], cfg.c_ca, cfg.c_cg, cfg.KT, cfg.NT, CT, par_c, T["yown"], 0, halo_mask=halo_mask, b_hm=b_hm)
        P.barrier(); AR.reset(base)
    if "sgu" in do:
        toks += sgu_mixer(C, AF32, ABF, T["xno"], T["wmix"], cfg.c_su, cfg.c_sv, cfg.KT, cfg.NT, CT, par_s, T["yown"], cfg.DG)
        P.barrier(); AR.reset(base)
    if "rwkv" in do:
        toks += rwkv_mixer(C, AF32, ABF, T["xn"], T["wmix"], {"r": cfg.c_r, "k": cfg.c_k, "v": cfg.c_v, "lora": cfg.c_l}, cfg.KT, cfg.SEQ, NH, par_r, cst, T["yall"], 0, TR=cfg.TR, DBG=T.get("dbg"))
        P.barrier(); AR.reset(base)
    if "sb" in do:
        toks += sb_mixer(C, AF32, ABF, T["xn"], T["wmix"], {"q": cfg.c_q, "k": cfg.c_sk, "v": cfg.c_svv}, cfg.KT, cfg.SEQ, cfg.NHSc, cst, T["yall"], NH * 64)
        P.barrier(); AR.reset(base)
    return toks


def build_mixer(cfg, do=("conv", "sgu", "rwkv", "sb"), ar_bytes=170 * 1024, dbg=False):
    nc = bass.Bass("TRN2", target_bir_lowering=False)
    P = PB(nc)
    D = cfg.D
    T = {}
    T["xno"] = DV(nc.dram_tensor("xno", [D, 32 + cfg.NT], BF16, kind="ExternalInput").ap())
    T["xn"] = DV(nc.dram_tensor("xn", [D, cfg.SEQ], BF16, kind="ExternalInput").ap())
    T["wmix"] = nc.dram_tensor("wmix", [D, cfg.NC], F32, kind="ExternalInput").ap()
    T["p128"] = nc.dram_tensor("p128", [128, cfg.NP128], F32, kind="ExternalInput").ap()
    T["wsT"] = nc.dram_tensor("wsT", [128, cfg.CT * 128], F32, kind="ExternalInput").ap()
    T["bsb"] = nc.dram_tensor("bsb", [128, cfg.CT * 128], F32, kind="ExternalInput").ap()
    T["p64"] = nc.dram_tensor("p64", [64, cfg.NP64], F32, kind="ExternalInput").ap()
    T["c128"] = nc.dram_tensor("c128", [128, cfg.NC128], F32, kind="ExternalInput").ap()
    T["yown"] = nc.dram_tensor("yown", [cfg.YOWN, cfg.NT], BF16, kind="ExternalOutput").ap()
    T["yall"] = RV(nc.dram_tensor("yall", [cfg.YALL, cfg.SEQ], BF16, kind="ExternalOutput").ap())
    if dbg:
        T["dbg"] = nc.dram_tensor("dbg", [13, 64, cfg.NHRc * cfg.TR], F32, kind="ExternalOutput").ap()
    C = Ctx(P, cfg.NT)
    AR = Arena(P, "arena", ar_bytes)
    toks = mixer_body(P, C, cfg, T, AR, do)
    print("mixer arena peak KB", AR.peak * 4 / 1024)
    P.finish(toks)
    print("mixer n_inst", P.n_inst)
    return nc

import numpy as np, ml_dtypes

BF = ml_dtypes.bfloat16


def t_phase_body(P, C, cfg, AR, T, mode, xres):
    D, DFF, NT, KT = cfg.D, cfg.DFF, cfg.NT, cfg.KT
    base = AR.off
    XR = T["xres"]
    NG = {"first": 2, "mid": 3, "last": 2}[mode]
    g_all = AR.alloc((NG * KT,), F32)
    nt = norm_tmp(P, AR, KT)
    P.dma("sp", g_all, T["gn"], writes=[nt["b_g"]])
    KM = (4 * cfg.DG) // 128
    xn = AR.alloc((max(KT, KM), NT), BF16)
    b_xn = Buf()
    FT = DFF // 128
    chmax = max(4, -(-FT // 4))
    gT = AR.alloc((chmax, NT), BF16)
    b_gT = Buf()
    rt = rmw_tmp(P, AR)
    stmp = [AR.alloc((512,), F32) for _ in range(2)]
    b_stmp = [Buf(), Buf()]
    toks = []
    gi = 0
    if mode in ("mid", "last"):
        if "yT" in T:
            for k in range(KM):
                P.dma("sp", xn[:, k, :], T["yT"][k * 128:(k + 1) * 128, :], writes=[b_xn])
        else:
            load_y_fused(P, cfg, gT, T, xn, b_xn)
        proj_rmw(C, xn, b_xn, 0, KM, T["wmo"], 0, D, XR, xres, 1.0, rt)
        rmsnorm(C, XR, g_all[:, gi * KT:(gi + 1) * KT], D, xn_sb=xn, b_xn=b_xn, xres=xres, tmp=nt)
        gi += 1
        ffn(C, xn, b_xn, T["win2"], T["wout2"], D, DFF, XR, xres, rt, gT, b_gT, stmp, b_stmp)
    if mode in ("first", "mid"):
        rmsnorm(C, XR, g_all[:, gi * KT:(gi + 1) * KT], D, xn_sb=xn, b_xn=b_xn, xres=xres, tmp=nt)
        gi += 1
        ffn(C, xn, b_xn, T["win1"], T["wout1"], D, DFF, XR, xres, rt, gT, b_gT, stmp, b_stmp)
        rmsnorm(C, XR, g_all[:, gi * KT:(gi + 1) * KT], D, xn_sb=xn, b_xn=b_xn, xres=xres, tmp=nt)
        for k in range(KT):
            toks.append(P.dma("sp", T["xn2"].tile(k, 0, NT), xn[:, k, :], reads=[b_xn], writes=[T.get("b_xn2", Buf())]))
    else:
        toks += rmsnorm(C, XR, g_all[:, gi * KT:(gi + 1) * KT], D, out_f32=T["out"], xres=xres, tmp=nt)
    P.barrier()
    AR.reset(base)
    return toks


def load_y_fused(P, cfg, gT, T, xn, b_xn):
    NT, DG = cfg.NT, cfg.DG
    CT = cfg.CT
    nr, ns = cfg.NHRc * 64, cfg.NHSc * 128
    YA = cfg.YALL
    tA = [gT[:, 0, :], gT[:, 1, :]]
    tB = [gT[:, 2, :], gT[:, 3, :]]
    bA = [Buf(), Buf()]
    bB = [Buf(), Buf()]
    rm = T["rolemask"]
    k = 0
    it = 0
    def own(row0, ntile):
        nonlocal k
        for j in range(ntile):
            P.dma("sp", xn[:, k, :], T["yown"][row0 + j * 128:row0 + (j + 1) * 128, :], reads=[T["b_yown"]], writes=[b_xn])
            k += 1
    def gathered(row0, ntile):
        nonlocal k, it
        for r in range(2):
            for j in range(ntile):
                i = it % 2
                it += 1
                rr = r * YA + row0 + j * 128
                P.dma("sp", tA[i], T["yallg"](r, row0 + j * 128, 0, NT), reads=[T["b_yallg"]], writes=[bA[i]])
                P.dma("sp", tB[i], T["yallg"](r, row0 + j * 128, NT, NT), reads=[T["b_yallg"]], writes=[bB[i]])
                P.op("dve", lambda E, i=i: E.tensor_scalar(out=tA[i], in0=tA[i], scalar1=rm[:, 0:1], scalar2=None, op0=ALU.mult), reads=[T["b_rm"]], writes=[bA[i]])
                P.op("dve", lambda E, i=i, kk=k: E.scalar_tensor_tensor(out=xn[:, kk, :], in0=tB[i], scalar=rm[:, 1:2], in1=tA[i], op0=ALU.mult, op1=ALU.add), reads=[bA[i], bB[i], T["b_rm"]], writes=[b_xn])
                k += 1
    own(0, CT)
    gathered(0, nr // 128)
    own(DG, CT)
    gathered(nr, ns // 128)


def build_T(cfg, mode):
    nc = bass.Bass("TRN2", target_bir_lowering=False)
    P = PB(nc)
    D, DFF, NT, KT = cfg.D, cfg.DFF, cfg.NT, cfg.KT
    T = {}
    NG = {"first": 2, "mid": 3, "last": 2}[mode]
    T["xin"] = nc.dram_tensor("xin", [D, NT], F32, kind="ExternalInput").ap()
    T["gn"] = nc.dram_tensor("gn", [128, NG * KT], F32, kind="ExternalInput").ap()
    if mode in ("mid", "last"):
        T["yT"] = nc.dram_tensor("yT", [4 * cfg.DG, NT], BF16, kind="ExternalInput").ap()
        T["wmo"] = nc.dram_tensor("wmo", [4 * cfg.DG, D], F32, kind="ExternalInput").ap()
        T["win2"] = nc.dram_tensor("win2", [D, 2 * DFF], F32, kind="ExternalInput").ap()
        T["wout2"] = nc.dram_tensor("wout2", [DFF, D], F32, kind="ExternalInput").ap()
    if mode in ("first", "mid"):
        T["win1"] = nc.dram_tensor("win1", [D, 2 * DFF], F32, kind="ExternalInput").ap()
        T["wout1"] = nc.dram_tensor("wout1", [DFF, D], F32, kind="ExternalInput").ap()
        T["xn2"] = DV(nc.dram_tensor("xn2", [D, NT], BF16, kind="ExternalOutput").ap())
        T["xres"] = nc.dram_tensor("xres", [D, NT], F32, kind="ExternalOutput").ap()
    else:
        T["xres"] = nc.dram_tensor("xres", [D, NT], F32, kind="Internal").ap()
        T["out"] = nc.dram_tensor("out", [D, NT], F32, kind="ExternalOutput").ap()
    C = Ctx(P, NT)
    AR = Arena(P, "arena", 170 * 1024)
    H = NT // 512
    xres = {(f, h): Buf() for f in range(KT) for h in range(H)}
    for f in range(KT):
        for h in range(H):
            P.dma("sp", T["xres"][f * 128:(f + 1) * 128, h * 512:(h + 1) * 512], T["xin"][f * 128:(f + 1) * 128, h * 512:(h + 1) * 512], writes=[xres[(f, h)]])
    toks = t_phase_body(P, C, cfg, AR, T, mode, xres)
    toks += [b.w for b in xres.values()]
    P.finish(toks)
    return nc


def gn_pack(cfg, vecs):
    return np.ascontiguousarray(np.concatenate([tile128(v, cfg.KT) for v in vecs], axis=1))


_PROGS = {}


def _prog(cfg, key, builder):
    k = (cfg.D, cfg.DFF, cfg.SEQ, cfg.DG, key)
    if k not in _PROGS:
        _PROGS[k] = builder()
    return _PROGS[k]


def run_module(cfg, inp):
    B, SEQ, D, NT = cfg.B, cfg.SEQ, cfg.D, cfg.NT
    NCORE = 2 * B
    depth = inp["ffn1_w_in"].shape[0]
    cores = list(range(NCORE))
    x = inp["x"]
    xres = [np.ascontiguousarray(x[c // 2, (c % 2) * NT:(c % 2 + 1) * NT, :].T) for c in cores]
    yT = None
    out = None
    for l in range(depth + 1):
        mode = "first" if l == 0 else ("last" if l == depth else "mid")
        nc = _prog(cfg, "T" + mode, lambda: build_T(cfg, mode))
        common = {}
        if mode == "first":
            common["gn"] = gn_pack(cfg, [inp["ffn1_norm"][0], inp["mix_norm"][0]])
        elif mode == "mid":
            common["gn"] = gn_pack(cfg, [inp["ffn2_norm"][l - 1], inp["ffn1_norm"][l], inp["mix_norm"][l]])
        else:
            common["gn"] = gn_pack(cfg, [inp["ffn2_norm"][l - 1], inp["final_norm"]])
        if mode in ("mid", "last"):
            common["wmo"] = inp["mix_w_out"][l - 1]
            common["win2"] = inp["ffn2_w_in"][l - 1]
            common["wout2"] = inp["ffn2_w_out"][l - 1]
        if mode in ("first", "mid"):
            common["win1"] = inp["ffn1_w_in"][l]
            common["wout1"] = inp["ffn1_w_out"][l]
        maps = []
        for c in cores:
            m = dict(common)
            m["xin"] = xres[c]
            if mode in ("mid", "last"):
                m["yT"] = yT[c]
            maps.append(m)
        res = run_bass_kernel_spmd(nc, maps, core_ids=cores)
        if mode == "last":
            out = np.empty((B, SEQ, D), np.float32)
            for c in cores:
                out[c // 2, (c % 2) * NT:(c % 2 + 1) * NT, :] = res.results[c]["out"].T
            return out
        xres = [res.results[c]["xres"] for c in cores]
        xn2 = [res.results[c]["xn2"] for c in cores]
        L = {k: inp[k][l] for k in ("mix_w_in", "conv_w", "conv_b", "conv_ln_g", "conv_ln_b", "rwkv_mu", "rwkv_w0", "rwkv_w_up", "rwkv_a0",
                                    "rwkv_a_up", "rwkv_g_up", "rwkv_k_k", "rwkv_k_a", "rwkv_r_k", "rwkv_lnx_g", "rwkv_lnx_b",
                                    "sgu_ln_g", "sgu_ln_b", "sgu_w_s", "sgu_b_s")}
        ncm = _prog(cfg, "M", lambda: build_mixer(cfg))
        maps = []
        shared = {}
        for c in cores:
            b, role = c // 2, c % 2
            full = np.concatenate([xn2[2 * b], xn2[2 * b + 1]], axis=1)
            own = np.zeros((D, 32 + NT), BF)
            own[:, 32:] = xn2[c]
            if role == 1:
                own[:, 0:32] = xn2[2 * b][:, NT - 32:NT]
            if role not in shared:
                shared[role] = mixer_inputs(cfg, L, role, None, None)
            m = dict(shared[role])
            m["xn"] = full
            m["xno"] = own
            maps.append(m)
        res = run_bass_kernel_spmd(ncm, maps, core_ids=cores)
        DG = cfg.DG
        nr, ns = cfg.NHRc * 64, cfg.NHSc * 128
        yT = []
        for c in cores:
            b, role = c // 2, c % 2
            ts = slice(role * NT, (role + 1) * NT)
            yo = res.results[c]["yown"]
            ya = [res.results[2 * b]["yall"], res.results[2 * b + 1]["yall"]]
            yT.append(np.ascontiguousarray(np.concatenate([
                yo[0:DG], ya[0][0:nr, ts], ya[1][0:nr, ts], yo[DG:2 * DG], ya[0][nr:nr + ns, ts], ya[1][nr:nr + ns, ts]], axis=0)))
    return out


CC_BYTES = 2 * 2 ** 20
MIX_KEYS = ("mix_w_in", "conv_w", "conv_b", "conv_ln_g", "conv_ln_b", "rwkv_mu", "rwkv_w0", "rwkv_w_up", "rwkv_a0",
            "rwkv_a_up", "rwkv_g_up", "rwkv_k_k", "rwkv_k_a", "rwkv_r_k", "rwkv_lnx_g", "rwkv_lnx_b",
            "sgu_ln_g", "sgu_ln_b", "sgu_w_s", "sgu_b_s")


def build_fused(cfg, depth):
    nc = bass.Bass("TRN2", target_bir_lowering=False)
    P = PB(nc)
    D, DFF, NT, KT, SEQ = cfg.D, cfg.DFF, cfg.NT, cfg.KT, cfg.SEQ
    ext = lambda n, sh, dt: nc.dram_tensor(n, list(sh), dt, kind="ExternalInput").ap()
    X = {}
    X["xin"] = ext("xin", [D, NT], F32)
    NGT = 3 * depth + 1
    X["gn"] = ext("gn", [128, NGT * KT], F32)
    X["rolemask"] = ext("rolemask", [128, 2], F32)
    X["c128"] = ext("c128", [128, cfg.NC128], F32)
    for l in range(depth):
        X[f"win1_{l}"] = ext(f"win1_{l}", [D, 2 * DFF], F32)
        X[f"wout1_{l}"] = ext(f"wout1_{l}", [DFF, D], F32)
        X[f"win2_{l}"] = ext(f"win2_{l}", [D, 2 * DFF], F32)
        X[f"wout2_{l}"] = ext(f"wout2_{l}", [DFF, D], F32)
        X[f"wmo_{l}"] = ext(f"wmo_{l}", [4 * cfg.DG, D], F32)
        X[f"wmix_{l}"] = ext(f"wmix_{l}", [D, cfg.NC], F32)
        X[f"p128_{l}"] = ext(f"p128_{l}", [128, cfg.NP128], F32)
        X[f"wsT_{l}"] = ext(f"wsT_{l}", [128, cfg.CT * 128], F32)
        X[f"bsb_{l}"] = ext(f"bsb_{l}", [128, cfg.CT * 128], F32)
        X[f"p64_{l}"] = ext(f"p64_{l}", [64, cfg.NP64], F32)
    OUT = nc.dram_tensor("out", [D, NT], F32, kind="ExternalOutput").ap()
    XRES = nc.dram_tensor("xres", [D, NT], F32, kind="Internal").ap()
    groups = [[2 * b, 2 * b + 1] for b in range(cfg.B)]
    C = Ctx(P, NT)
    AR = Arena(P, "arena", 170 * 1024)
    rm = AR.alloc((2,), F32)
    b_rm = Buf()
    P.dma("sp", rm, X["rolemask"], writes=[b_rm])
    base0 = AR.off
    H = NT // 512
    xres = {(f, h): Buf() for f in range(KT) for h in range(H)}
    for f in range(KT):
        for h in range(H):
            P.dma("sp", XRES[f * 128:(f + 1) * 128, h * 512:(h + 1) * 512], X["xin"][f * 128:(f + 1) * 128, h * 512:(h + 1) * 512], writes=[xres[(f, h)]])
    toks = []
    gi = 0
    prev = None
    for l in range(depth + 1):
        mode = "first" if l == 0 else ("last" if l == depth else "mid")
        NG = {"first": 2, "mid": 3, "last": 2}[mode]
        T = {"xres": XRES, "gn": X["gn"][:, gi * KT:(gi + NG) * KT], "rolemask": rm, "b_rm": b_rm}
        gi += NG
        if mode in ("mid", "last"):
            T["yown"], T["b_yown"], T["yallg"], T["b_yallg"] = prev
            T["wmo"] = X[f"wmo_{l - 1}"]
            T["win2"] = X[f"win2_{l - 1}"]
            T["wout2"] = X[f"wout2_{l - 1}"]
        if mode in ("first", "mid"):
            T["win1"] = X[f"win1_{l}"]
            T["wout1"] = X[f"wout1_{l}"]
            CRX = min(D, CC_BYTES // (NT * 2))
            xn2_ts = [nc.dram_tensor(f"xn2_{l}_{j}", [CRX, NT], BF16) for j in range(D // CRX)]
            xng_ts = [nc.dram_tensor(f"xng_{l}_{j}", [2 * CRX, NT], BF16) for j in range(D // CRX)]

            def xn2_w(k, c0, n, xn2_ts=xn2_ts, CRX=CRX):
                d0 = k * 128
                return xn2_ts[d0 // CRX].ap()[d0 % CRX:d0 % CRX + 128, c0:c0 + n]
            T["xn2"] = DV(fn=xn2_w)
            T["b_xn2"] = Buf()
        else:
            T["out"] = OUT
        toks += t_phase_body(P, C, cfg, AR, T, mode, xres)
        if mode == "last":
            break
        b_xng = Buf()
        for j in range(len(xn2_ts)):
            P.collective("AllGather", [xn2_ts[j].ap().opt()], [xng_ts[j].ap().opt()], groups, reads=[T["b_xn2"]], writes=[b_xng])
        P.barrier()

        def xn_full(k, c0, n, xng_ts=xng_ts, CRX=CRX):
            r = c0 // NT
            assert (c0 + n - 1) // NT == r
            d0 = k * 128
            i = d0 % CRX
            return xng_ts[d0 // CRX].ap()[r * CRX + i:r * CRX + i + 128, c0 - r * NT:c0 - r * NT + n]

        def xn_own(k, c0, n, xng_ts=xng_ts, xn2_ts=xn2_ts, CRX=CRX):
            d0 = k * 128
            i = d0 % CRX
            if c0 == 0:
                assert n == 32
                return xng_ts[d0 // CRX].ap()[i:i + 128, NT - 32:NT]
            return xn2_ts[d0 // CRX].ap()[i:i + 128, c0 - 32:c0 - 32 + n]
        yown_t = nc.dram_tensor(f"yown_{l}", [cfg.YOWN, NT], BF16, kind="Internal")
        CRY = min(cfg.YALL, CC_BYTES // (SEQ * 2))
        yall_ts = [nc.dram_tensor(f"yall_{l}_{j}", [CRY, SEQ], BF16) for j in range(cfg.YALL // CRY)]
        yallg_ts = [nc.dram_tensor(f"yallg_{l}_{j}", [2 * CRY, SEQ], BF16) for j in range(cfg.YALL // CRY)]

        def yall_w(r0, nr_, c0, n, yall_ts=yall_ts, CRY=CRY):
            assert r0 // CRY == (r0 + nr_ - 1) // CRY
            return yall_ts[r0 // CRY].ap()[r0 % CRY:r0 % CRY + nr_, c0:c0 + n]

        def yallg_r(r, row, c0, n, yallg_ts=yallg_ts, CRY=CRY):
            i = row % CRY
            return yallg_ts[row // CRY].ap()[r * CRY + i:r * CRY + i + 128, c0:c0 + n]
        TM = {"xno": DV(fn=xn_own), "xn": DV(fn=xn_full), "wmix": X[f"wmix_{l}"], "p128": X[f"p128_{l}"], "wsT": X[f"wsT_{l}"],
              "bsb": X[f"bsb_{l}"], "p64": X[f"p64_{l}"], "c128": X["c128"], "yown": yown_t.ap(), "yall": RV(fn=yall_w)}
        mtoks = mixer_body(P, C, cfg, TM, AR, halo_mask=rm[:, 1:2], b_hm=b_rm)
        AR.reset(base0)
        b_yown = Buf()
        b_yall = Buf()
        P.wait_all("pool", mtoks)
        P.wait_all("sp", mtoks)
        b_yallg = Buf()
        for j in range(len(yall_ts)):
            P.collective("AllGather", [yall_ts[j].ap().opt()], [yallg_ts[j].ap().opt()], groups, reads=[], writes=[b_yallg])
        P.barrier()
        prev = (yown_t.ap(), b_yown, yallg_r, b_yallg)
    toks += [b.w for b in xres.values() if b.w is not None]
    P.finish(toks)
    print("fused n_inst", P.n_inst, "arena peak KB", AR.peak * 4 / 1024)
    return nc


def run_fused(cfg, inp):
    B, SEQ, D, NT = cfg.B, cfg.SEQ, cfg.D, cfg.NT
    NCORE = 2 * B
    depth = inp["ffn1_w_in"].shape[0]
    cores = list(range(NCORE))
    nc = _prog(cfg, ("F", depth), lambda: build_fused(cfg, depth))
    x = inp["x"]
    gvecs = []
    for l in range(depth + 1):
        if l > 0:
            gvecs.append(inp["ffn2_norm"][l - 1])
        if l < depth:
            gvecs += [inp["ffn1_norm"][l], inp["mix_norm"][l]]
        else:
            gvecs.append(inp["final_norm"])
    common = {"gn": gn_pack(cfg, gvecs), "c128": consts_np(cfg)}
    for l in range(depth):
        common[f"win1_{l}"] = inp["ffn1_w_in"][l]
        common[f"wout1_{l}"] = inp["ffn1_w_out"][l]
        common[f"win2_{l}"] = inp["ffn2_w_in"][l]
        common[f"wout2_{l}"] = inp["ffn2_w_out"][l]
        common[f"wmo_{l}"] = inp["mix_w_out"][l]
    per_role = {0: {}, 1: {}}
    for l in range(depth):
        L = {k: inp[k][l] for k in MIX_KEYS}
        for role in (0, 1):
            mi = mixer_inputs(cfg, L, role, None, None)
            for k in ("wmix", "p128", "wsT", "bsb", "p64"):
                per_role[role][f"{k}_{l}"] = mi[k]
    maps = []
    for c in cores:
        role = c % 2
        m = dict(common)
        m.update(per_role[role])
        m["xin"] = np.ascontiguousarray(x[c // 2, role * NT:(role + 1) * NT, :].T)
        rmk = np.zeros((128, 2), np.float32)
        rmk[:, 0] = 1 - role
        rmk[:, 1] = role
        m["rolemask"] = rmk
        maps.append(m)
    res = run_bass_kernel_spmd(nc, maps, core_ids=cores)
    out = np.empty((B, SEQ, D), np.float32)
    for c in cores:
        out[c // 2, (c % 2) * NT:(c % 2 + 1) * NT, :] = res.results[c]["out"].T
    return out


def kernel(**inputs):
    inp = {k: np.asarray(v) for k, v in inputs.items()}
    cfg = Cfg()
    return run_fused(cfg, inp)
```

```python
import numpy as np
from contextlib import ExitStack
import concourse.bass as bass
import concourse.mybir as mybir
from concourse.bass_utils import run_bass_kernel_spmd

F32 = mybir.dt.float32
BF16 = mybir.dt.bfloat16
AF = mybir.ActivationFunctionType
ALU = mybir.AluOpType
AX = mybir.AxisListType

SEM_LIMIT = 30000
N_DMA_SEMS = 12


class Buf:
    __slots__ = ("name", "w", "r")

    def __init__(self, name=""):
        self.name = name
        self.w = None
        self.r = []


class PB:
    ENG = ("pe", "act", "dve", "pool", "sp")

    def __init__(self, nc):
        self.nc = nc
        self.es = ExitStack()
        self.eng = {"pe": nc.tensor, "act": nc.scalar, "dve": nc.vector, "pool": nc.gpsimd, "sp": nc.sync}
        self.q = {e: [] for e in self.ENG}
        self.sem = {}
        self.cnt = {}
        self.nsem = 0
        for e in ("pe", "act", "dve", "pool"):
            self._new_sem(e)
        self.seen = {e: {} for e in self.ENG}
        self.dsem = {}
        for qn in ("sp", "act", "pool"):
            self.dsem[qn] = [[self._alloc_sem(f"d{qn}{i}"), 0, None] for i in range(N_DMA_SEMS)]
        self.dnext = {qn: 0 for qn in self.dsem}
        self.pe_pend_r = []
        self.pe_pend_w = []
        self.n_inst = 0

    def _alloc_sem(self, name):
        self.nsem += 1
        return self.es.enter_context(self.nc.semaphore(f"{name}_{self.nsem}"))

    def _new_sem(self, e):
        self.sem[e] = self._alloc_sem(f"s{e}")
        self.cnt[e] = 0

    def sbuf(self, name, shape, dt):
        return self.es.enter_context(self.nc.sbuf_tensor(name, list(shape), dt))

    def psum(self, name, shape, dt):
        return self.es.enter_context(self.nc.psum_tensor(name, list(shape), dt))

    def _need(self, e, reads, writes):
        toks = []
        for b in reads:
            if b.w is not None:
                toks.append(b.w)
        for b in writes:
            if b.w is not None:
                toks.append(b.w)
            toks.extend(b.r)
        best = {}
        for (s, v) in toks:
            k = id(s)
            if k not in best or best[k][1] < v:
                best[k] = (s, v)
        out = []
        for k, (s, v) in best.items():
            if e == "pe" and s is self.sem["pe"]:
                continue
            if self.seen[e].get(k, 0) >= v:
                continue
            self.seen[e][k] = v
            out.append((s, v))
        return out

    def _emit_waits(self, e, waits):
        for (s, v) in waits:
            self.q[e].append(lambda E, s=s, v=v: E.wait_ge(s, v))

    def _commit(self, tok, reads, writes):
        for b in reads:
            b.r.append(tok)
        for b in writes:
            b.w = tok
            b.r = []

    def op(self, e, fn, reads=(), writes=()):
        waits = self._need(e, reads, writes)
        self._emit_waits(e, waits)
        if self.cnt[e] >= SEM_LIMIT:
            self._new_sem(e)
        self.cnt[e] += 1
        s = self.sem[e]
        self.q[e].append(lambda E, fn=fn, s=s: fn(E).then_inc(s, 1))
        tok = (s, self.cnt[e])
        self._commit(tok, reads, writes)
        self.n_inst += 1
        return tok

    def mm(self, fn, reads=(), writes=(), last=True):
        e = "pe"
        waits = self._need(e, reads, writes)
        self._emit_waits(e, waits)
        self.n_inst += 1
        if not last:
            self.q[e].append(lambda E, fn=fn: fn(E))
            self.pe_pend_r.extend(reads)
            self.pe_pend_w.extend(writes)
            return None
        if self.cnt[e] >= SEM_LIMIT:
            self._new_sem(e)
        self.cnt[e] += 1
        s = self.sem[e]
        self.q[e].append(lambda E, fn=fn, s=s: fn(E).then_inc(s, 1))
        tok = (s, self.cnt[e])
        rs = {id(b): b for b in list(self.pe_pend_r) + list(reads)}
        ws = {id(b): b for b in list(self.pe_pend_w) + list(writes)}
        self.pe_pend_r = []
        self.pe_pend_w = []
        self._commit(tok, [b for k, b in rs.items() if k not in ws], list(ws.values()))
        return tok

    def barrier(self):
        toks = [(self.sem[e], self.cnt[e]) for e in ("pe", "act", "dve", "pool") if self.cnt[e] > 0]
        for qn, slots in self.dsem.items():
            for (s, n, prev) in slots:
                if prev is not None:
                    toks.append(prev)
        if getattr(self, "ccnt", 0) > 0:
            toks.append((self.csem, self.ccnt))
        for e in self.ENG:
            for (s, v) in toks:
                k = id(s)
                if self.seen[e].get(k, 0) >= v:
                    continue
                if e == "pe" and s is self.sem["pe"]:
                    continue
                self.seen[e][k] = v
                self.q[e].append(lambda E, s=s, v=v: E.wait_ge(s, v))

    def dma(self, qn, out, in_, reads=(), writes=(), **kw):
        slots = self.dsem[qn]
        i = self.dnext[qn]
        self.dnext[qn] = (i + 1) % len(slots)
        slot = slots[i]
        s, n, prev = slot
        waits = self._need(qn, reads, writes)
        if prev is not None:
            k = id(prev[0])
            if self.seen[qn].get(k, 0) < prev[1]:
                self.seen[qn][k] = prev[1]
                waits.append(prev)
        self._emit_waits(qn, waits)
        n += 1
        tok = (s, 16 * n)
        slot[1] = n
        slot[2] = tok
        self.q[qn].append(lambda E, out=out, in_=in_, s=s, kw=kw: E.dma_start(out=out, in_=in_, **kw).then_inc(s, 16))
        self._commit(tok, reads, writes)
        self.n_inst += 1
        return tok

    def wait_all(self, e, toks):
        for (s, v) in toks:
            self.q[e].append(lambda E, s=s, v=v: E.wait_ge(s, v))

    def finish(self, final_toks):
        self.wait_all("sp", final_toks)
        nc = self.nc
        q = self.q
        with nc.Block() as block:
            @block.sync
            def _(E):
                for f in q["sp"]:
                    f(E)

            @block.scalar
            def _(E):
                for f in q["act"]:
                    f(E)

            @block.vector
            def _(E):
                for f in q["dve"]:
                    f(E)

            @block.gpsimd
            def _(E):
                for f in q["pool"]:
                    f(E)

            @block.tensor
            def _(E):
                for f in q["pe"]:
                    f(E)
        self.es.close()


def pb_collective(self, kind, ins, outs, groups, reads=(), writes=()):
    qn = "pool"
    if not hasattr(self, "csem"):
        self.csem = self._alloc_sem("cc")
        self.ccnt = 0
    waits = self._need(qn, reads, writes)
    self._emit_waits(qn, waits)
    self.ccnt += 1
    s = self.csem
    tok = (s, self.ccnt)
    self.q[qn].append(lambda E, s=s: E.collective_compute(kind, ALU.bypass, replica_groups=groups, ins=ins, outs=outs).then_inc(s))
    self._commit(tok, reads, writes)
    return tok


PB.collective = pb_collective


EPS = 1e-6


class Ctx:
    def __init__(self, P, NT):
        self.P = P
        self.NT = NT
        self.ps = [P.psum(f"ps{i}", [128, 512], F32) for i in range(8)]
        self.psb = [Buf(f"ps{i}") for i in range(8)]
        self.ones = P.sbuf("ones_f", [128, 128], F32)
        self.b_ones = Buf("ones")
        P.op("dve", lambda E: E.memset(self.ones[:], 1.0), writes=[self.b_ones])
        self.NW = 4
        self.wb = [P.sbuf(f"wb{i}", [128, 4, 512], BF16) for i in range(self.NW)]
        self.wbb = [Buf(f"wb{i}") for i in range(self.NW)]
        self.wi = 0

    def next_w(self):
        i = self.wi
        self.wi = (i + 1) % self.NW
        return self.wb[i], self.wbb[i]


def split_chunks(n, nch):
    base, rem = divmod(n, nch)
    out = []
    s = 0
    for i in range(nch):
        c = base + (1 if i < rem else 0)
        out.append((s, c))
        s += c
    return out


def rmsnorm(C, xT, g_sb, D, xn_sb=None, b_xn=None, xres=None, out_f32=None, tmp=None):
    P = C.P
    NT = C.NT
    KT = D // 128
    Q = 256
    xf, b_xf, sq, b_sq, rs, b_rs, rstd, b_rstd = tmp["xf"], tmp["b_xf"], tmp["sq"], tmp["b_sq"], tmp["rs"], tmp["b_rs"], tmp["rstd"], tmp["b_rstd"]
    toks = []
    for q in range(NT // Q):
        c0 = q * Q
        h = c0 // 512
        for k in range(KT):
            rd = [xres[(k, h)]] if xres is not None else []
            P.dma("sp", xf[:, k, :], xT[k * 128:(k + 1) * 128, c0:c0 + Q], reads=rd, writes=[b_xf[k]])
        bank = q % 2
        for k in range(KT):
            P.op("act", lambda E, k=k, i=k % 2: E.activation(out=sq[i][:], in_=xf[:, k, :], func=AF.Square),
                 reads=[b_xf[k]], writes=[b_sq[k % 2]])
            P.mm(lambda E, k=k, i=k % 2, bank=bank: E.matmul(C.ps[bank][:, 0:Q], lhsT=C.ones[:], rhs=sq[i][:], start=(k == 0), stop=(k == KT - 1)),
                 reads=[b_sq[k % 2], C.b_ones], writes=[C.psb[bank]], last=True)
        P.op("act", lambda E, bank=bank: E.activation(out=rs[:], in_=C.ps[bank][:, 0:Q], func=AF.Sqrt, scale=1.0 / D, bias=tmp["eps"][:]),
             reads=[C.psb[bank], tmp["b_eps"]], writes=[b_rs])
        P.op("dve", lambda E: E.reciprocal(out=rstd[:], in_=rs[:]), reads=[b_rs], writes=[b_rstd])
        for k in range(KT):
            if xn_sb is not None:
                P.op("dve", lambda E, k=k, c0=c0: E.scalar_tensor_tensor(out=xn_sb[:, k, c0:c0 + Q], in0=xf[:, k, :], scalar=g_sb[:, k:k + 1], in1=rstd[:], op0=ALU.mult, op1=ALU.mult),
                     reads=[b_xf[k], b_rstd, tmp["b_g"]], writes=[b_xn])
            else:
                P.op("dve", lambda E, k=k: E.scalar_tensor_tensor(out=xf[:, k, :], in0=xf[:, k, :], scalar=g_sb[:, k:k + 1], in1=rstd[:], op0=ALU.mult, op1=ALU.mult),
                     reads=[b_xf[k], b_rstd, tmp["b_g"]], writes=[b_xf[k]])
                toks.append(P.dma("sp", out_f32[k * 128:(k + 1) * 128, c0:c0 + Q], xf[:, k, :], reads=[b_xf[k]], writes=[Buf()]))
    return toks


def norm_tmp(P, AR, KT):
    t = {}
    t["xf"] = AR.alloc((KT, 256), F32)
    t["b_xf"] = [Buf() for _ in range(KT)]
    t["sq"] = [AR.alloc((256,), F32) for i in range(2)]
    t["b_sq"] = [Buf(), Buf()]
    t["rs"] = AR.alloc((256,), F32)
    t["b_rs"] = Buf()
    t["rstd"] = AR.alloc((256,), F32)
    t["b_rstd"] = Buf()
    t["eps"] = AR.alloc((1,), F32)
    t["b_eps"] = Buf()
    P.op("dve", lambda E: E.memset(t["eps"][:, :], EPS), writes=[t["b_eps"]])
    t["b_g"] = Buf()
    return t


def proj_rmw(C, act_sb, b_act, kt0, nk, W, wrow0, D, xT, xres, scale, rt):
    P = C.P
    NT = C.NT
    H = NT // 512
    NG = D // 512 if D >= 512 else 1
    JW = min(4, D // 128)
    assert JW * H <= 8
    for g in range(NG):
        for j in range(JW):
            for h in range(H):
                f = g * JW + j
                xo, b_xo = rt["xold"][j * H + h], rt["b_xold"][j * H + h]
                P.dma("sp", xo[:], xT[f * 128:(f + 1) * 128, h * 512:(h + 1) * 512], reads=[xres[(f, h)]], writes=[b_xo])
        kb = 0
        while kb < nk:
            nkk = min(4, nk - kb)
            wb, bwb = C.next_w()
            r0 = wrow0 + kb * 128
            P.dma("pool", wb[:, 0:nkk, 0:JW * 128], W[r0:r0 + nkk * 128, g * JW * 128:(g + 1) * JW * 128].rearrange("(kk p) n -> p kk n", p=128), writes=[bwb])
            for kk in range(nkk):
                k = kb + kk
                for j in range(JW):
                    for h in range(H):
                        lastk = (kk == nkk - 1 and j == JW - 1 and h == H - 1)
                        P.mm(lambda E, wb=wb, kk=kk, j=j, h=h, k=k: E.matmul(C.ps[j * H + h][:, :], lhsT=wb[:, kk, j * 128:(j + 1) * 128], rhs=act_sb[:, kt0 + k, h * 512:(h + 1) * 512], start=(k == 0), stop=(k == nk - 1)),
                             reads=[bwb, b_act], writes=[C.psb[j * H + h]], last=lastk)
            kb += nkk
        for j in range(JW):
            for h in range(H):
                f = g * JW + j
                i = j * H + h
                xo, b_xo = rt["xold"][i], rt["b_xold"][i]
                P.op("dve", lambda E, i=i, xo=xo: E.scalar_tensor_tensor(out=xo[:], in0=C.ps[i][:, :], scalar=float(scale), in1=xo[:], op0=ALU.mult, op1=ALU.add),
                     reads=[C.psb[i], b_xo], writes=[b_xo])
                P.dma("sp", xT[f * 128:(f + 1) * 128, h * 512:(h + 1) * 512], xo[:], reads=[b_xo], writes=[xres[(f, h)]])


def rmw_tmp(P, AR):
    rt = {}
    rt["xold"] = [AR.alloc((512,), F32) for i in range(8)]
    rt["b_xold"] = [Buf() for _ in range(8)]
    return rt


def ffn(C, xn_sb, b_xn, W_in, W_out, D, DFF, xT, xres, rt, gT, b_gT, stmp, b_stmp, nch=4):
    P = C.P
    NT = C.NT
    H = NT // 512
    KT = D // 128
    FT = DFF // 128
    Win4 = W_in.rearrange("(kt p) (two f) -> p kt two f", p=128, two=2)
    for (c0, cn) in split_chunks(FT, nch):
        t = 0
        while t < cn:
            gs = min(2, cn - t)
            n0 = (c0 + t) * 128
            for k4 in range(0, KT, 4):
                nkk = min(4, KT - k4)
                wb, bwb = C.next_w()
                wv = wb[:].rearrange("p k (two f) -> p k two f", two=2)
                for gu in range(2):
                    P.dma("pool", wv[:, 0:nkk, gu, 0:gs * 128], Win4[:, k4:k4 + nkk, gu, n0:n0 + gs * 128], writes=[bwb])
                for kk in range(nkk):
                    k = k4 + kk
                    for j in range(gs):
                        for gu in range(2):
                            for h in range(H):
                                bank = gu * 4 + j * H + h
                                lastk = (kk == nkk - 1 and j == gs - 1 and gu == 1 and h == H - 1)
                                P.mm(lambda E, wv=wv, kk=kk, gu=gu, j=j, h=h, k=k, bank=bank: E.matmul(C.ps[bank][:, :], lhsT=wv[:, kk, gu, j * 128:(j + 1) * 128], rhs=xn_sb[:, k, h * 512:(h + 1) * 512], start=(k == 0), stop=(k == KT - 1)),
                                     reads=[bwb, b_xn], writes=[C.psb[bank]], last=lastk)
            for j in range(gs):
                for h in range(H):
                    bg = j * H + h
                    bu = 4 + j * H + h
                    si = (j * H + h) % 2
                    P.op("act", lambda E, bg=bg, si=si: E.activation(out=stmp[si][:], in_=C.ps[bg][:, :], func=AF.Silu),
                         reads=[C.psb[bg]], writes=[b_stmp[si]])
                    P.op("dve", lambda E, bu=bu, si=si, tt=t + j, h=h: E.tensor_tensor(out=gT[:, tt, h * 512:(h + 1) * 512], in0=stmp[si][:], in1=C.ps[bu][:, :], op=ALU.mult),
                         reads=[b_stmp[si], C.psb[bu]], writes=[b_gT])
            t += gs
        proj_rmw(C, gT, b_gT, 0, cn, W_out, c0 * 128, D, xT, xres, 0.5, rt)

import math


class Arena:
    def __init__(self, P, name, nbytes):
        self.n = nbytes // 4
        self.t = P.sbuf(name, [128, self.n], F32)
        self.off = 0
        self.peak = 0

    def reset(self, off=0):
        self.off = off

    def alloc(self, shape, dt):
        sz = 1
        for s in shape:
            sz *= s
        words = sz if dt == F32 else (sz + 1) // 2
        a = self.t[:, self.off:self.off + words]
        self.off += words
        self.peak = max(self.peak, self.off)
        assert self.off <= self.n, ("arena overflow", self.off, self.n)
        if dt != F32:
            a = a.bitcast(dt)
            if sz % 2:
                a = a[:, 0:sz]
        if len(shape) == 1:
            return a
        if len(shape) == 2:
            return a.rearrange("p (a b) -> p a b", a=shape[0])
        if len(shape) == 3:
            return a.rearrange("p (a b c) -> p a b c", a=shape[0], b=shape[1])
        raise ValueError


class DV:
    def __init__(self, ap=None, fn=None, fn_group=None, kg=None):
        self.ap = ap
        self.fn = fn
        self.fn_group = fn_group
        self.kg = kg

    def tile(self, k, c0, n):
        if self.fn is not None:
            return self.fn(k, c0, n)
        return self.ap[k * 128:(k + 1) * 128, c0:c0 + n]

    def group(self, k0, nk, c0, n):
        if self.fn_group is not None:
            return self.fn_group(k0, nk, c0, n)
        return self.ap[k0 * 128:(k0 + nk) * 128, c0:c0 + n].rearrange("(k p) n -> p k n", p=128)


def load_xn(P, xn, b_xn, X, KT, c0, n):
    kg = X.kg if X.kg is not None else KT
    if X.fn is not None and X.fn_group is None:
        for k in range(KT):
            P.dma("sp", xn[:, k, 0:n], X.tile(k, c0, n), writes=[b_xn])
        return
    for k0 in range(0, KT, kg):
        nk = min(kg, KT - k0)
        P.dma("sp", xn[:, k0:k0 + nk, 0:n], X.group(k0, nk, c0, n), writes=[b_xn])


class RV:
    def __init__(self, ap=None, fn=None):
        self.ap = ap
        self.fn = fn

    def rows(self, r0, nr, c0, n):
        if self.fn is not None:
            return self.fn(r0, nr, c0, n)
        return self.ap[r0:r0 + nr, c0:c0 + n]


class ArenaView:
    def __init__(self, ar, dt):
        self.ar = ar
        self.dt = dt

    def alloc(self, *shape):
        return self.ar.alloc(shape, self.dt)


def proj_blocks(C, xn, b_xn, chunks, W, KT, segs, blocks, evac, bank0=None):
    P = C.P
    nb = len(blocks) * len(chunks)
    if bank0 is None:
        if nb <= 4:
            C.pbt = 4 - getattr(C, "pbt", 4)
            bank0 = C.pbt
        else:
            bank0 = 0
    assert bank0 + nb <= 8
    soff = []
    o = 0
    for (c0_, nc_) in segs:
        soff.append(o)
        o += nc_
    assert o <= 512
    for k4 in range(0, KT, 4):
        nkk = min(4, KT - k4)
        wb, bwb = C.next_w()
        for si, (col0, ncols) in enumerate(segs):
            P.dma("pool", wb[:, 0:nkk, soff[si]:soff[si] + ncols],
                  W[k4 * 128:(k4 + nkk) * 128, col0:col0 + ncols].rearrange("(kk p) n -> p kk n", p=128), writes=[bwb])
        for kk in range(nkk):
            k = k4 + kk
            for bi, (si, off, M) in enumerate(blocks):
                for ci, (c0, n) in enumerate(chunks):
                    bank = bank0 + bi * len(chunks) + ci
                    lastk = (kk == nkk - 1 and bi == len(blocks) - 1 and ci == len(chunks) - 1)
                    P.mm(lambda E, wb=wb, kk=kk, a=soff[si] + off, M=M, c0=c0, n=n, k=k, bank=bank:
                         E.matmul(C.ps[bank][0:M, 0:n], lhsT=wb[:, kk, a:a + M], rhs=xn[:, k, c0:c0 + n], start=(k == 0), stop=(k == KT - 1)),
                         reads=[bwb, b_xn], writes=[C.psb[bank]], last=lastk)
    for bi, (si, off, M) in enumerate(blocks):
        for ci, (c0, n) in enumerate(chunks):
            bank = bank0 + bi * len(chunks) + ci
            evac(bi, ci, C.ps[bank][0:M, 0:n], C.psb[bank])


def layernorm_fm(C, x, b_x, nt, n, ncol, gcol, bcol, b_par, eps, out_fn, tmp):
    P = C.P
    CH = nt * 128
    bs, bq = 0, 1
    bxl = b_x if isinstance(b_x, list) else [b_x] * nt
    for k in range(nt):
        P.mm(lambda E, k=k: E.matmul(C.ps[bs][:, 0:n], lhsT=C.ones[:], rhs=x[:, k, ncol:ncol + n], start=(k == 0), stop=(k == nt - 1)),
             reads=[bxl[k], C.b_ones], writes=[C.psb[bs]], last=(k == nt - 1))
    for k in range(nt):
        i = k % 2
        P.op("act", lambda E, k=k, i=i: E.activation(out=tmp["sq"][i][:, 0:n], in_=x[:, k, ncol:ncol + n], func=AF.Square),
             reads=[bxl[k]], writes=[tmp["b_sq"][i]])
        P.mm(lambda E, k=k, i=i: E.matmul(C.ps[bq][:, 0:n], lhsT=C.ones[:], rhs=tmp["sq"][i][:, 0:n], start=(k == 0), stop=(k == nt - 1)),
             reads=[tmp["b_sq"][i], C.b_ones], writes=[C.psb[bq]], last=True)
    mean, rstd = tmp["mean"], tmp["rstd"]
    P.op("act", lambda E: E.activation(out=mean[:, 0:n], in_=C.ps[bs][:, 0:n], func=AF.Copy, scale=1.0 / CH), reads=[C.psb[bs]], writes=[tmp["b_mean"]])
    P.op("dve", lambda E: E.tensor_tensor(out=rstd[:, 0:n], in0=mean[:, 0:n], in1=mean[:, 0:n], op=ALU.mult), reads=[tmp["b_mean"]], writes=[tmp["b_rstd"]])
    P.op("dve", lambda E: E.scalar_tensor_tensor(out=rstd[:, 0:n], in0=C.ps[bq][:, 0:n], scalar=1.0 / CH, in1=rstd[:, 0:n], op0=ALU.mult, op1=ALU.subtract),
         reads=[C.psb[bq], tmp["b_rstd"]], writes=[tmp["b_rstd"]])
    P.op("dve", lambda E: E.tensor_scalar(out=rstd[:, 0:n], in0=rstd[:, 0:n], scalar1=0.0, scalar2=float(eps), op0=ALU.max, op1=ALU.add), reads=[tmp["b_rstd"]], writes=[tmp["b_rstd"]])
    P.op("act", lambda E: E.activation(out=rstd[:, 0:n], in_=rstd[:, 0:n], func=AF.Sqrt), reads=[tmp["b_rstd"]], writes=[tmp["b_rstd"]])
    P.op("dve", lambda E: E.reciprocal(out=rstd[:, 0:n], in_=rstd[:, 0:n]), reads=[tmp["b_rstd"]], writes=[tmp["b_rstd"]])
    for k in range(nt):
        i = k % 2
        t = tmp["t"][i]
        P.op("dve", lambda E, k=k, t=t: E.tensor_tensor(out=t[:, 0:n], in0=x[:, k, ncol:ncol + n], in1=mean[:, 0:n], op=ALU.subtract),
             reads=[bxl[k], tmp["b_mean"]], writes=[tmp["b_t"][i]])
        P.op("dve", lambda E, t=t: E.tensor_tensor(out=t[:, 0:n], in0=t[:, 0:n], in1=rstd[:, 0:n], op=ALU.mult),
             reads=[tmp["b_rstd"]], writes=[tmp["b_t"][i]])
        out_fn(k, t[:, 0:n], tmp["b_t"][i])


def ln_tmp(AR):
    t = {}
    t["sq"] = [AR.alloc(512), AR.alloc(512)]
    t["b_sq"] = [Buf(), Buf()]
    t["mean"] = AR.alloc(512)
    t["rstd"] = AR.alloc(512)
    t["b_mean"] = Buf()
    t["b_rstd"] = Buf()
    t["t"] = [AR.alloc(512), AR.alloc(512)]
    t["b_t"] = [Buf(), Buf()]
    return t


def conv_mixer(C, AF32, ABF, XNO, W, col_a, col_g, KT, NT, CT, par, YOUT, yrow0, K=31, halo_mask=None, b_hm=None):
    P = C.P
    HALO = 32
    NTT = HALO + NT
    xn = ABF.alloc(KT, 512)
    b_xn = Buf()
    yglu = AF32.alloc(CT, NTT)
    b_yglu = Buf()
    sg = [AF32.alloc(512), AF32.alloc(512)]
    b_sg = [Buf(), Buf()]
    chunks_all = [(0, HALO)] + [(HALO + i * 512, 512) for i in range(NT // 512)]
    for (c0, n) in chunks_all:
        load_xn(P, xn, b_xn, XNO, KT, c0, n)
        if halo_mask is not None and c0 == 0:
            P.op("dve", lambda E: E.tensor_scalar(out=xn[:, :, 0:HALO], in0=xn[:, :, 0:HALO], scalar1=halo_mask, scalar2=None, op0=ALU.mult), reads=[b_hm], writes=[b_xn])
        for g in range(CT // 2):
            def evac(bi, ci, ps, psb, g=g, c0=c0, n=n):
                pass
            res = {}

            def evac2(bi, ci, ps, psb, res=res, g=g, c0=c0, n=n):
                res[bi] = (ps, psb)
                if bi == 3:
                    for j in range(2):
                        pa, ba = res[j]
                        pg, bg = res[2 + j]
                        i = j % 2
                        P.op("act", lambda E, pg=pg, i=i, n=n: E.activation(out=sg[i][:, 0:n], in_=pg, func=AF.Sigmoid), reads=[bg], writes=[b_sg[i]])
                        P.op("dve", lambda E, pa=pa, i=i, n=n, ct=g * 2 + j, c0=c0: E.tensor_tensor(out=yglu[:, ct, c0:c0 + n], in0=sg[i][:, 0:n], in1=pa, op=ALU.mult),
                             reads=[b_sg[i], ba], writes=[b_yglu])
            proj_blocks(C, xn, b_xn, [(0, n)], W, KT, [(col_a + g * 256, 256), (col_g + g * 256, 256)],
                        [(0, 0, 128), (0, 128, 128), (1, 0, 128), (1, 128, 128)], evac2)
    cv = AF32.alloc(CT, NT)
    b_cv = [Buf() for _ in range(CT)]
    cw, cb = par["cw"], par["cb"]
    for ct in range(CT):
        e = "dve"
        bct = b_cv[ct]
        off = HALO - (K - 1)
        P.op(e, lambda E, ct=ct: E.tensor_scalar(out=cv[:, ct, :], in0=yglu[:, ct, off:off + NT], scalar1=cw[:, ct, 0:1], scalar2=cb[:, ct:ct + 1], op0=ALU.mult, op1=ALU.add),
             reads=[b_yglu, par["b"]], writes=[bct])
        for j in range(1, K):
            P.op(e, lambda E, ct=ct, j=j: E.scalar_tensor_tensor(out=cv[:, ct, :], in0=yglu[:, ct, off + j:off + j + NT], scalar=cw[:, ct, j:j + 1], in1=cv[:, ct, :], op0=ALU.mult, op1=ALU.add),
                 reads=[b_yglu, par["b"]], writes=[bct])
    tmp = ln_tmp(AF32)
    yo = [ABF.alloc(512), ABF.alloc(512)]
    b_yo = [Buf(), Buf()]
    toks = []
    for hh in range(NT // 512):
        def out_fn(k, xh, b_xh, hh=hh):
            i = k % 2
            P.op("act", lambda E, k=k, xh=xh, i=i: E.activation(out=yo[i][:], in_=xh, func=AF.Silu, scale=par["lg"][:, k:k + 1], bias=par["lb"][:, k:k + 1]),
                 reads=[b_xh, par["b"]], writes=[b_yo[i]])
            toks.append(P.dma("sp", YOUT[yrow0 + k * 128:yrow0 + (k + 1) * 128, hh * 512:(hh + 1) * 512], yo[i][:], reads=[b_yo[i]], writes=[Buf()]))
        layernorm_fm(C, cv, b_cv, CT, 512, hh * 512, None, None, None, EPS, out_fn, tmp)
    return toks


def gelu_tanh(P, eng_v, out_ap, in_ps, b_in, n, t1, b_t1, t2, b_t2, writes):
    c2 = 2.0 * math.sqrt(2.0 / math.pi)
    P.op("act", lambda E: E.activation(out=t1, in_=in_ps, func=AF.Copy), reads=[b_in], writes=[b_t1])
    P.op("act", lambda E: E.activation(out=t2, in_=in_ps, func=AF.Square), reads=[b_in], writes=[b_t2])
    P.op(eng_v, lambda E: E.tensor_scalar(out=t2, in0=t2, scalar1=0.044715, scalar2=1.0, op0=ALU.mult, op1=ALU.add), reads=[b_t2], writes=[b_t2])
    P.op(eng_v, lambda E: E.tensor_tensor(out=t2, in0=t2, in1=t1, op=ALU.mult), reads=[b_t1, b_t2], writes=[b_t2])
    P.op("act", lambda E: E.activation(out=t2, in_=t2, func=AF.Sigmoid, scale=c2), reads=[b_t2], writes=[b_t2])
    P.op(eng_v, lambda E: E.tensor_tensor(out=out_ap, in0=t2, in1=t1, op=ALU.mult), reads=[b_t1, b_t2], writes=writes)


def sgu_mixer(C, AF32, ABF, XNO, W, col_u, col_v, KT, NT, CT, par, YOUT, yrow0):
    P = C.P
    HALO = 32
    xn = ABF.alloc(KT, 512)
    b_xn = Buf()
    u = AF32.alloc(CT, NT)
    b_u = Buf()
    v = AF32.alloc(CT, NT)
    b_v = Buf()
    t1 = [AF32.alloc(512), AF32.alloc(512)]
    t2 = [AF32.alloc(512), AF32.alloc(512)]
    b_t1 = [Buf(), Buf()]
    b_t2 = [Buf(), Buf()]
    for hh in range(NT // 512):
        c0 = HALO + hh * 512
        load_xn(P, xn, b_xn, XNO, KT, c0, 512)
        for (col, dst, b_dst) in ((col_u, u, b_u), (col_v, v, b_v)):
            gsz = min(4, CT)
            for g in range(CT // gsz):
                def evac(bi, ci, ps, psb, g=g, dst=dst, b_dst=b_dst, hh=hh):
                    i = bi % 2
                    gelu_tanh(P, "dve", dst[:, g * gsz + bi, hh * 512:(hh + 1) * 512], ps, psb, 512, t1[i][:], b_t1[i], t2[i][:], b_t2[i], [b_dst])
                proj_blocks(C, xn, b_xn, [(0, 512)], W, KT, [(col + g * gsz * 128, gsz * 128)], [(0, j * 128, 128) for j in range(gsz)], evac)
    tmp = ln_tmp(AF32)
    vn = v
    b_vn = b_v
    for hh in range(NT // 512):
        def out_fn(k, xh, b_xh, hh=hh):
            P.op("act", lambda E, k=k, xh=xh: E.activation(out=vn[:, k, hh * 512:(hh + 1) * 512], in_=xh, func=AF.Identity, scale=par["lg"][:, k:k + 1], bias=par["lb"][:, k:k + 1]),
                 reads=[b_xh, par["b"]], writes=[b_vn])
        layernorm_fm(C, v, b_v, CT, 512, hh * 512, None, None, None, EPS, out_fn, tmp)
    vt = [ABF.alloc(128), ABF.alloc(128)]
    b_vt = [Buf(), Buf()]
    yo = [ABF.alloc(128), ABF.alloc(128)]
    b_yo = [Buf(), Buf()]
    sv = [AF32.alloc(128), AF32.alloc(128)]
    b_sv = [Buf(), Buf()]
    toks = []
    it = 0
    for h in range(CT):
        for nb in range(NT // 128):
            i = it % 2
            it += 1
            tb = 4 + i
            P.mm(lambda E, h=h, nb=nb, tb=tb: E.matmul(C.ps[tb][:, 0:128], lhsT=vn[:, h, nb * 128:(nb + 1) * 128], rhs=C.ident[:], start=True, stop=True),
                 reads=[b_vn, C.b_ident], writes=[C.psb[tb]], last=True)
            P.op("act", lambda E, i=i, tb=tb: E.activation(out=vt[i][:], in_=C.ps[tb][:, 0:128], func=AF.Copy), reads=[C.psb[tb]], writes=[b_vt[i]])
            bank = 2 + i
            P.mm(lambda E, h=h, i=i, bank=bank: E.matmul(C.ps[bank][:, 0:128], lhsT=vt[i][:], rhs=par["wsT"][:, h, :], start=True, stop=True),
                 reads=[b_vt[i], par["b"]], writes=[C.psb[bank]], last=True)
            P.op("dve", lambda E, h=h, i=i, bank=bank: E.tensor_tensor(out=sv[i][:], in0=C.ps[bank][:, 0:128], in1=par["bsb"][:, h, :], op=ALU.add),
                 reads=[C.psb[bank], par["b"]], writes=[b_sv[i]])
            P.op("dve", lambda E, h=h, nb=nb, i=i: E.tensor_tensor(out=yo[i][:], in0=sv[i][:], in1=u[:, h, nb * 128:(nb + 1) * 128], op=ALU.mult),
                 reads=[b_sv[i], b_u], writes=[b_yo[i]])
            toks.append(P.dma("sp", YOUT[yrow0 + h * 128:yrow0 + (h + 1) * 128, nb * 128:(nb + 1) * 128], yo[i][:], reads=[b_yo[i]], writes=[Buf()]))
    return toks


def rwkv_mixer(C, AF32, ABF, XN, W, cols, KT, S, NH, par, cst, YOUT, yrow0, TR=256, DBG=None):
    P = C.P
    L = 64
    NCH = TR // L
    NT_ = S // TR
    HW = NH * 64
    ones64 = C.ones[0:64, 0:64]
    id64 = C.ident[0:64, 0:64]
    xn = ABF.alloc(KT, TR)
    b_xn = Buf()
    R = AF32.alloc(NH, TR); Kb = AF32.alloc(NH, TR); V = AF32.alloc(NH, TR)
    b_R, b_K, b_V = Buf(), Buf(), Buf()
    XL = AF32.alloc(5, TR); b_XL = Buf()
    LW = AF32.alloc(NH, TR); b_LW = Buf()
    CS = [AF32.alloc(NH, TR), AF32.alloc(NH, TR)]; b_CS = [Buf(), Buf()]
    A = AF32.alloc(NH, TR); b_A = Buf()
    G = AF32.alloc(NH, TR); b_G = Buf()
    KK = AF32.alloc(NH, TR); b_KK = Buf()
    BON = AF32.alloc(NH, TR); b_BON = Buf()
    PX = AF32.alloc(NH, TR); b_PX = Buf()
    YB, b_YB = CS[1], b_CS[1]
    PL = AF32.alloc(NH, NCH); b_PL = Buf()
    carry = AF32.alloc(3 * NH + 5); b_carry = Buf()
    ST = [AF32.alloc(NH, 64), AF32.alloc(NH, 64)]; b_ST = [Buf(), Buf()]
    t64 = [AF32.alloc(TR), AF32.alloc(TR)]; b_t64 = [Buf(), Buf()]
    def cb():
        return [AF32.alloc(NH, 64), AF32.alloc(NH, 64)], [Buf(), Buf()]
    def cb1():
        a = AF32.alloc(NH, 64)
        b = Buf()
        return [a, a], [b, b]
    VT, b_VT = cb(); KtT, b_KtT = cb(); BtT, b_BtT = cb()
    M1, b_M1 = cb1(); N1, b_N1 = cb(); N2, b_N2 = cb()
    Xm, b_X = cb(); Xt, b_Xt = cb1()
    Aa, b_Aa = cb1(); At, b_At = cb1()
    RH, b_RH = cb1(); NU, b_NU = cb1()
    tmpS = AF32.alloc(NH, 64); b_tmpS = Buf()
    yo = [ABF.alloc(TR), ABF.alloc(TR)]; b_yo = [Buf(), Buf()]
    P.op("dve", lambda E: E.memset(carry[:, :], 0.0), writes=[b_carry])
    P.op("dve", lambda E: E.memset(ST[0][:, :, :], 0.0), writes=[b_ST[0]])
    bp = par["b"]
    toks = []
    sti = 0
    gch = 0
    for tt in range(NT_):
        c0 = tt * TR
        load_xn(P, xn, b_xn, XN, KT, c0, TR)

        def shift_evac(dst, b_dst, hidx, cidx, mu, omu, M=64):
            def ev(bi, ci, ps, psb):
                h = hidx(bi)
                cc = cidx(bi)
                P.op("dve", lambda E: E.tensor_scalar(out=dst[0:M, h, :], in0=ps, scalar1=omu[0:M, h:h + 1], scalar2=None, op0=ALU.mult), reads=[psb, bp], writes=[b_dst])
                P.op("dve", lambda E: E.scalar_tensor_tensor(out=dst[0:M, h, 1:TR], in0=ps[:, 0:TR - 1], scalar=mu[0:M, h:h + 1], in1=dst[0:M, h, 1:TR], op0=ALU.mult, op1=ALU.add), reads=[psb, bp], writes=[b_dst])
                P.op("dve", lambda E: E.scalar_tensor_tensor(out=dst[0:M, h, 0:1], in0=carry[0:M, cc:cc + 1], scalar=mu[0:M, h:h + 1], in1=dst[0:M, h, 0:1], op0=ALU.mult, op1=ALU.add), reads=[b_carry, bp], writes=[b_dst])
                P.op("dve", lambda E: E.tensor_copy(out=carry[0:M, cc:cc + 1], in_=ps[:, TR - 1:TR]), reads=[psb], writes=[b_carry])
            return ev
        for qi, (nm, dst, b_dst) in enumerate((("r", R, b_R), ("k", Kb, b_K), ("v", V, b_V))):
            for h0 in range(0, NH, 8):
                nh = min(8, NH - h0)
                proj_blocks(C, xn, b_xn, [(0, TR)], W, KT, [(cols[nm] + h0 * 64, nh * 64)], [(0, j * 64, 64) for j in range(nh)],
                            shift_evac(dst, b_dst, lambda bi, h0=h0: h0 + bi, lambda bi, h0=h0, qi=qi: qi * NH + h0 + bi, par["mu_" + nm], par["omu_" + nm]))
        lb = [(0, 0, 64), (0, 64, 64), (0, 128, 64), (0, 192, 64), (0, 256, 32)]

        def ev_l(bi, ci, ps, psb):
            M = lb[bi][2]
            shift_evac(XL, b_XL, lambda b: b, lambda b: 3 * NH + b, par["mu_l"], par["omu_l"], M=M)(bi, ci, ps, psb)
        proj_blocks(C, xn, b_xn, [(0, TR)], W, KT, [(cols["lora"], 288)], lb, ev_l)

        P.op("act", lambda E: E.activation(out=XL[0:64, 0, :], in_=XL[0:64, 0, :], func=AF.Tanh), reads=[], writes=[b_XL])
        for j in (2, 3):
            P.op("act", lambda E, j=j: E.activation(out=XL[0:64, j, :], in_=XL[0:64, j, :], func=AF.Sigmoid), reads=[], writes=[b_XL])
        P.op("act", lambda E: E.activation(out=XL[0:32, 4, :], in_=XL[0:32, 4, :], func=AF.Sigmoid), reads=[], writes=[b_XL])
        for h in range(NH):
            bk = h % 2
            P.mm(lambda E, h=h, bk=bk: E.matmul(C.ps[bk][0:64, 0:TR], lhsT=par["wup"][0:64, h * 64:(h + 1) * 64], rhs=XL[0:64, 0, :], start=True, stop=True),
                 reads=[b_XL, bp], writes=[C.psb[bk]], last=True)
            P.op("act", lambda E, h=h, bk=bk: E.activation(out=LW[0:64, h, :], in_=C.ps[bk][0:64, 0:TR], func=AF.Sigmoid, bias=par["w0"][0:64, h:h + 1]), reads=[C.psb[bk], bp], writes=[b_LW])
            bk2 = 2 + h % 2
            P.mm(lambda E, h=h, bk2=bk2: E.matmul(C.ps[bk2][0:64, 0:TR], lhsT=par["aup"][0:64, h * 64:(h + 1) * 64], rhs=XL[0:64, 1, :], start=True, stop=True),
                 reads=[b_XL, bp], writes=[C.psb[bk2]], last=True)
            P.op("act", lambda E, h=h, bk2=bk2: E.activation(out=A[0:64, h, :], in_=C.ps[bk2][0:64, 0:TR], func=AF.Sigmoid, bias=par["a0"][0:64, h:h + 1]), reads=[C.psb[bk2], bp], writes=[b_A])
            bk3 = 4 + h % 2
            for j, M in ((0, 64), (1, 64), (2, 32)):
                P.mm(lambda E, h=h, bk3=bk3, j=j, M=M: E.matmul(C.ps[bk3][0:64, 0:TR], lhsT=par["gup"][0:M, j, h * 64:(h + 1) * 64], rhs=XL[0:M, 2 + j, :], start=(j == 0), stop=(j == 2)),
                     reads=[b_XL, bp], writes=[C.psb[bk3]], last=(j == 2))
            P.op("act", lambda E, h=h, bk3=bk3: E.activation(out=G[0:64, h, :], in_=C.ps[bk3][0:64, 0:TR], func=AF.Copy), reads=[C.psb[bk3]], writes=[b_G])
        P.op("dve", lambda E: E.tensor_scalar(out=LW[0:64, :, :], in0=LW[0:64, :, :], scalar1=-math.exp(-0.5), scalar2=None, op0=ALU.mult), reads=[], writes=[b_LW])

        for h in range(NH):
            P.op("dve", lambda E, h=h: E.tensor_scalar(out=KK[0:64, h, :], in0=Kb[0:64, h, :], scalar1=par["kk"][0:64, h:h + 1], scalar2=None, op0=ALU.mult), reads=[b_K, bp], writes=[b_KK])
        for h in range(NH):
            i = h % 2
            bk = 6 + i
            P.op("act", lambda E, h=h, i=i: E.activation(out=t64[i][0:64, :], in_=KK[0:64, h, :], func=AF.Square), reads=[b_KK], writes=[b_t64[i]])
            P.mm(lambda E, i=i, bk=bk: E.matmul(C.ps[bk][0:64, 0:TR], lhsT=ones64, rhs=t64[i][0:64, :], start=True, stop=True), reads=[b_t64[i], C.b_ones], writes=[C.psb[bk]], last=True)
            P.op("act", lambda E, i=i, bk=bk: E.activation(out=t64[i][0:64, :], in_=C.ps[bk][0:64, 0:TR], func=AF.Sqrt), reads=[C.psb[bk]], writes=[b_t64[i]])
            P.op("dve", lambda E, i=i: E.tensor_scalar(out=t64[i][0:64, :], in0=t64[i][0:64, :], scalar1=1e-12, scalar2=None, op0=ALU.max), reads=[], writes=[b_t64[i]])
            P.op("dve", lambda E, i=i: E.reciprocal(out=t64[i][0:64, :], in_=t64[i][0:64, :]), reads=[], writes=[b_t64[i]])
            P.op("dve", lambda E, h=h, i=i: E.tensor_tensor(out=KK[0:64, h, :], in0=KK[0:64, h, :], in1=t64[i][0:64, :], op=ALU.mult), reads=[b_t64[i]], writes=[b_KK])
        for h in range(NH):
            P.op("dve", lambda E, h=h: E.tensor_scalar(out=PX[0:64, h, :], in0=A[0:64, h, :], scalar1=par["ka"][0:64, h:h + 1], scalar2=par["oka"][0:64, h:h + 1], op0=ALU.mult, op1=ALU.add), reads=[b_A, bp], writes=[b_PX])
        P.op("dve", lambda E: E.tensor_tensor(out=Kb[0:64, :, :], in0=Kb[0:64, :, :], in1=PX[0:64, :, :], op=ALU.mult), reads=[b_PX], writes=[b_K])
        P.op("dve", lambda E: E.tensor_tensor(out=A[0:64, :, :], in0=A[0:64, :, :], in1=KK[0:64, :, :], op=ALU.mult), reads=[b_KK], writes=[b_A])
        P.op("dve", lambda E: E.tensor_tensor(out=PX[0:64, :, :], in0=R[0:64, :, :], in1=Kb[0:64, :, :], op=ALU.mult), reads=[b_R, b_K], writes=[b_PX])
        for h in range(NH):
            i = h % 2
            bk = 6 + i
            P.op("dve", lambda E, h=h, i=i: E.tensor_scalar(out=t64[i][0:64, :], in0=PX[0:64, h, :], scalar1=par["rk"][0:64, h:h + 1], scalar2=None, op0=ALU.mult), reads=[b_PX, bp], writes=[b_t64[i]])
            P.mm(lambda E, i=i, bk=bk: E.matmul(C.ps[bk][0:64, 0:TR], lhsT=ones64, rhs=t64[i][0:64, :], start=True, stop=True), reads=[b_t64[i], C.b_ones], writes=[C.psb[bk]], last=True)
            P.op("dve", lambda E, h=h, bk=bk: E.tensor_tensor(out=BON[0:64, h, :], in0=C.ps[bk][0:64, 0:TR], in1=V[0:64, h, :], op=ALU.mult), reads=[C.psb[bk], b_V], writes=[b_BON])

        if DBG is not None and tt == 0:
            for di, (buf_, b_) in enumerate(((R, b_R), (Kb, b_K), (V, b_V), (LW, b_LW), (A, b_A), (G, b_G), (KK, b_KK), (BON, b_BON))):
                toks.append(P.dma("sp", DBG[di, :, :].rearrange("p (h t) -> p h t", h=NH), buf_[0:64, :, :], reads=[b_], writes=[Buf()]))
        def v4(ap):
            return ap[0:64, :, :].rearrange("p h (c l) -> p h c l", l=L)
        src, b_src = LW, b_LW
        pi = 0
        d = 1
        while d < L:
            dst, b_dst = CS[pi], b_CS[pi]
            for h in range(NH):
                s4 = src[0:64, h, :].rearrange("p (c l) -> p c l", l=L)
                d4 = dst[0:64, h, :].rearrange("p (c l) -> p c l", l=L)
                P.op("dve", lambda E, s4=s4, d4=d4, d=d: E.tensor_copy(out=d4[:, :, 0:d], in_=s4[:, :, 0:d]), reads=[b_src], writes=[b_dst])
                P.op("dve", lambda E, s4=s4, d4=d4, d=d: E.tensor_tensor(out=d4[:, :, d:L], in0=s4[:, :, d:L], in1=s4[:, :, 0:L - d], op=ALU.add), reads=[b_src], writes=[b_dst])
            src, b_src = dst, b_dst
            pi = 1 - pi
            d *= 2
        CUM, b_CUM = src, b_src
        OTH, b_OTH = CS[pi], b_CS[pi]
        P.op("act", lambda E: E.activation(out=PX[0:64, :, :], in_=CUM[0:64, :, :], func=AF.Exp), reads=[b_CUM], writes=[b_PX])
        P.op("dve", lambda E: E.tensor_tensor(out=R[0:64, :, :], in0=R[0:64, :, :], in1=PX[0:64, :, :], op=ALU.mult), reads=[b_PX], writes=[b_R])
        for h in range(NH):
            p4 = PX[0:64, h, :].rearrange("p (c l) -> p c l", l=L)
            P.op("dve", lambda E, h=h, p4=p4: E.tensor_copy(out=PL[0:64, h, :], in_=p4[:, :, L - 1]), reads=[b_PX], writes=[b_PL])
        P.op("act", lambda E: E.activation(out=PX[0:64, :, :], in_=CUM[0:64, :, :], func=AF.Exp, scale=-1.0), reads=[b_CUM, b_PL, b_R], writes=[b_PX])
        P.op("dve", lambda E: E.tensor_tensor(out=Kb[0:64, :, :], in0=Kb[0:64, :, :], in1=PX[0:64, :, :], op=ALU.mult), reads=[b_PX], writes=[b_K])
        P.op("dve", lambda E: E.tensor_tensor(out=A[0:64, :, :], in0=A[0:64, :, :], in1=PX[0:64, :, :], op=ALU.mult), reads=[b_PX], writes=[b_A])
        P.op("dve", lambda E: E.tensor_tensor(out=OTH[0:64, :, :], in0=CUM[0:64, :, :], in1=LW[0:64, :, :], op=ALU.subtract), reads=[b_CUM, b_LW, b_K, b_A], writes=[b_OTH])
        P.op("act", lambda E: E.activation(out=OTH[0:64, :, :], in_=OTH[0:64, :, :], func=AF.Exp), reads=[], writes=[b_OTH])
        P.op("dve", lambda E: E.tensor_tensor(out=KK[0:64, :, :], in0=KK[0:64, :, :], in1=OTH[0:64, :, :], op=ALU.mult), reads=[b_OTH], writes=[b_KK])

        def do_chunk(c, q, cs, So, b_So, Sn, b_Sn):

            def hv(buf, h):
                return buf[0:64, h, :]
            for (src_, b_s, dstl, b_dl, bank) in ((V, b_V, VT, b_VT, 0), (Kb, b_K, KtT, b_KtT, 1), (A, b_A, BtT, b_BtT, 2)):
                for h in range(NH):
                    P.mm(lambda E, src_=src_, h=h, bank=bank: E.matmul(C.ps[bank][0:64, h * 64:(h + 1) * 64], lhsT=src_[0:64, h, cs], rhs=id64, start=True, stop=True),
                         reads=[b_s, C.b_ident], writes=[C.psb[bank]], last=(h == NH - 1))
                P.op("act", lambda E, dstl=dstl, bank=bank: E.activation(out=dstl[q][0:64, :, :], in_=C.ps[bank][0:64, 0:HW].rearrange("p (h l) -> p h l", l=64), func=AF.Copy),
                     reads=[C.psb[bank]], writes=[b_dl[q]])
            specs = ((Kb, b_K, KK, b_KK, M1, b_M1, "mS8", 3), (A, b_A, KK, b_KK, Aa, b_Aa, "mS8", 4), (KK, b_KK, A, b_A, At, b_At, "mL8", 5),
                     (Kb, b_K, R, b_R, N1, b_N1, "mI8", 6), (A, b_A, R, b_R, N2, b_N2, "mI8", 7))
            for (l_, b_l, r_, b_r, dstl, b_dl, mk, bank) in specs:
                for h in range(NH):
                    P.mm(lambda E, l_=l_, r_=r_, h=h, bank=bank: E.matmul(C.ps[bank][0:64, h * 64:(h + 1) * 64], lhsT=l_[0:64, h, cs], rhs=r_[0:64, h, cs], start=True, stop=True),
                         reads=[b_l, b_r], writes=[C.psb[bank]], last=(h == NH - 1))
                P.op("dve", lambda E, dstl=dstl, bank=bank, mk=mk: E.tensor_tensor(out=dstl[q][0:64, :, :].rearrange("p h l -> p (h l)"), in0=C.ps[bank][0:64, 0:HW], in1=cst[mk][0:64, 0:HW], op=ALU.mult),
                     reads=[C.psb[bank], cst["b"]], writes=[b_dl[q]])
            fl = lambda t: t[0:64, :, :].rearrange("p h l -> p (h l)")
            P.op("dve", lambda E: E.tensor_tensor(out=fl(Xm[q]), in0=cst["ident8"][0:64, 0:HW], in1=fl(Aa[q]), op=ALU.subtract), reads=[b_Aa[q], cst["b"]], writes=[b_X[q]])
            P.op("dve", lambda E: E.tensor_tensor(out=fl(Xt[q]), in0=cst["ident8"][0:64, 0:HW], in1=fl(At[q]), op=ALU.subtract), reads=[b_At[q], cst["b"]], writes=[b_Xt[q]])
            NIT = 5
            for it in range(NIT):
                lastit = (it == NIT - 1)
                for h in range(NH):
                    P.mm(lambda E, h=h: E.matmul(C.ps[0][0:64, h * 64:(h + 1) * 64], lhsT=At[q][0:64, h, :], rhs=Aa[q][0:64, h, :], start=True, stop=True),
                         reads=[b_At[q], b_Aa[q]], writes=[C.psb[0]], last=(h == NH - 1))
                if not lastit:
                    for h in range(NH):
                        P.mm(lambda E, h=h: E.matmul(C.ps[1][0:64, h * 64:(h + 1) * 64], lhsT=Aa[q][0:64, h, :], rhs=At[q][0:64, h, :], start=True, stop=True),
                             reads=[b_At[q], b_Aa[q]], writes=[C.psb[1]], last=(h == NH - 1))
                P.op("act", lambda E: E.activation(out=fl(Aa[q]), in_=C.ps[0][0:64, 0:HW], func=AF.Copy), reads=[C.psb[0], C.psb[1] if not lastit else C.psb[0]], writes=[b_Aa[q]])
                if not lastit:
                    P.op("act", lambda E: E.activation(out=fl(At[q]), in_=C.ps[1][0:64, 0:HW], func=AF.Copy), reads=[C.psb[1]], writes=[b_At[q]])
                for h in range(NH):
                    P.mm(lambda E, h=h: E.matmul(C.ps[2][0:64, h * 64:(h + 1) * 64], lhsT=Xt[q][0:64, h, :], rhs=Aa[q][0:64, h, :], start=True, stop=True),
                         reads=[b_Xt[q], b_Aa[q]], writes=[C.psb[2]], last=(h == NH - 1))
                if not lastit:
                    for h in range(NH):
                        P.mm(lambda E, h=h: E.matmul(C.ps[3][0:64, h * 64:(h + 1) * 64], lhsT=Aa[q][0:64, h, :], rhs=Xt[q][0:64, h, :], start=True, stop=True),
                             reads=[b_Xt[q], b_Aa[q]], writes=[C.psb[3]], last=(h == NH - 1))
                P.op("dve", lambda E: E.tensor_tensor(out=fl(Xm[q]), in0=C.ps[2][0:64, 0:HW], in1=fl(Xm[q]), op=ALU.add), reads=[C.psb[2], C.psb[3] if not lastit else C.psb[2]], writes=[b_X[q]])
                if not lastit:
                    P.op("dve", lambda E: E.tensor_tensor(out=fl(Xt[q]), in0=C.ps[3][0:64, 0:HW], in1=fl(Xt[q]), op=ALU.add), reads=[C.psb[3]], writes=[b_Xt[q]])
            for h in range(NH):
                P.mm(lambda E, h=h: E.matmul(C.ps[4][0:64, h * 64:(h + 1) * 64], lhsT=KK[0:64, h, cs], rhs=So[0:64, h, :], start=True, stop=False),
                     reads=[b_KK, b_So], writes=[C.psb[4]], last=False)
                P.mm(lambda E, h=h: E.matmul(C.ps[4][0:64, h * 64:(h + 1) * 64], lhsT=M1[q][0:64, h, :], rhs=VT[q][0:64, h, :], start=False, stop=True),
                     reads=[b_M1[q], b_VT[q]], writes=[C.psb[4]], last=(h == NH - 1))
            P.op("act", lambda E: E.activation(out=fl(RH[q]), in_=C.ps[4][0:64, 0:HW], func=AF.Copy), reads=[C.psb[4]], writes=[b_RH[q]])
            for h in range(NH):
                P.mm(lambda E, h=h: E.matmul(C.ps[5][0:64, h * 64:(h + 1) * 64], lhsT=Xm[q][0:64, h, :], rhs=RH[q][0:64, h, :], start=True, stop=True),
                     reads=[b_X[q], b_RH[q]], writes=[C.psb[5]], last=(h == NH - 1))
            P.op("act", lambda E: E.activation(out=fl(NU[q]), in_=C.ps[5][0:64, 0:HW], func=AF.Copy, scale=-1.0), reads=[C.psb[5]], writes=[b_NU[q]])
            for h in range(NH):
                P.mm(lambda E, h=h: E.matmul(C.ps[6][0:64, h * 64:(h + 1) * 64], lhsT=So[0:64, h, :], rhs=R[0:64, h, cs], start=True, stop=False),
                     reads=[b_So, b_R], writes=[C.psb[6]], last=False)
                P.mm(lambda E, h=h: E.matmul(C.ps[6][0:64, h * 64:(h + 1) * 64], lhsT=VT[q][0:64, h, :], rhs=N1[q][0:64, h, :], start=False, stop=False),
                     reads=[b_VT[q], b_N1[q]], writes=[C.psb[6]], last=False)
                P.mm(lambda E, h=h: E.matmul(C.ps[6][0:64, h * 64:(h + 1) * 64], lhsT=NU[q][0:64, h, :], rhs=N2[q][0:64, h, :], start=False, stop=True),
                     reads=[b_NU[q], b_N2[q]], writes=[C.psb[6]], last=(h == NH - 1))
            P.op("act", lambda E: E.activation(out=YB[0:64, :, cs], in_=C.ps[6][0:64, 0:HW].rearrange("p (h l) -> p h l", l=64), func=AF.Copy), reads=[C.psb[6]], writes=[b_YB])
            for h in range(NH):
                P.mm(lambda E, h=h: E.matmul(C.ps[7][0:64, h * 64:(h + 1) * 64], lhsT=KtT[q][0:64, h, :], rhs=VT[q][0:64, h, :], start=True, stop=False),
                     reads=[b_KtT[q], b_VT[q]], writes=[C.psb[7]], last=False)
                P.mm(lambda E, h=h: E.matmul(C.ps[7][0:64, h * 64:(h + 1) * 64], lhsT=BtT[q][0:64, h, :], rhs=NU[q][0:64, h, :], start=False, stop=True),
                     reads=[b_BtT[q], b_NU[q]], writes=[C.psb[7]], last=(h == NH - 1))
            P.op("dve", lambda E: E.tensor_tensor(out=fl(tmpS), in0=C.ps[7][0:64, 0:HW], in1=fl(So), op=ALU.add), reads=[C.psb[7], b_So], writes=[b_tmpS])
            for h in range(NH):
                P.op("dve", lambda E, h=h, c=c: E.tensor_scalar(out=Sn[0:64, h, :], in0=tmpS[0:64, h, :], scalar1=PL[0:64, h, c:c + 1], scalar2=None, op0=ALU.mult), reads=[b_tmpS, b_PL], writes=[b_Sn])

        for c in range(NCH):
            q = gch % 2
            gch += 1
            do_chunk(c, q, slice(c * L, (c + 1) * L), ST[sti], b_ST[sti], ST[1 - sti], b_ST[1 - sti])
            sti = 1 - sti
        if DBG is not None and tt == 0:
            for di, (buf_, b_) in enumerate(((R, b_R), (Kb, b_K), (A, b_A), (KK, b_KK), (YB, b_YB))):
                toks.append(P.dma("sp", DBG[8 + di, :, :].rearrange("p (h t) -> p h t", h=NH), buf_[0:64, :, :], reads=[b_], writes=[Buf()]))
        for h in range(NH):
            i = h % 2
            P.mm(lambda E, h=h: E.matmul(C.ps[0][0:64, 0:TR], lhsT=ones64, rhs=YB[0:64, h, :], start=True, stop=True), reads=[b_YB, C.b_ones], writes=[C.psb[0]], last=True)
            P.op("act", lambda E, h=h, i=i: E.activation(out=t64[i][0:64, :], in_=YB[0:64, h, :], func=AF.Square), reads=[b_YB], writes=[b_t64[i]])
            P.mm(lambda E, i=i: E.matmul(C.ps[1][0:64, 0:TR], lhsT=ones64, rhs=t64[i][0:64, :], start=True, stop=True), reads=[b_t64[i], C.b_ones], writes=[C.psb[1]], last=True)
            mean = PX[0:64, 0, :]
            var = PX[0:64, 1, :]
            P.op("act", lambda E: E.activation(out=mean, in_=C.ps[0][0:64, 0:TR], func=AF.Copy, scale=1.0 / 64), reads=[C.psb[0]], writes=[b_PX])
            P.op("dve", lambda E: E.tensor_tensor(out=var, in0=mean, in1=mean, op=ALU.mult), reads=[], writes=[b_PX])
            P.op("dve", lambda E: E.scalar_tensor_tensor(out=var, in0=C.ps[1][0:64, 0:TR], scalar=1.0 / 64, in1=var, op0=ALU.mult, op1=ALU.subtract), reads=[C.psb[1]], writes=[b_PX])
            P.op("dve", lambda E: E.tensor_scalar(out=var, in0=var, scalar1=0.0, scalar2=64e-5, op0=ALU.max, op1=ALU.add), reads=[], writes=[b_PX])
            P.op("act", lambda E: E.activation(out=var, in_=var, func=AF.Sqrt), reads=[], writes=[b_PX])
            P.op("dve", lambda E: E.reciprocal(out=var, in_=var), reads=[], writes=[b_PX])
            P.op("dve", lambda E, h=h: E.tensor_tensor(out=YB[0:64, h, :], in0=YB[0:64, h, :], in1=mean, op=ALU.subtract), reads=[b_PX], writes=[b_YB])
            P.op("dve", lambda E, h=h: E.tensor_tensor(out=YB[0:64, h, :], in0=YB[0:64, h, :], in1=var, op=ALU.mult), reads=[b_PX], writes=[b_YB])
            P.op("dve", lambda E, h=h: E.tensor_scalar(out=YB[0:64, h, :], in0=YB[0:64, h, :], scalar1=par["lg"][0:64, h:h + 1], scalar2=par["lb"][0:64, h:h + 1], op0=ALU.mult, op1=ALU.add), reads=[bp], writes=[b_YB])
            P.op("dve", lambda E, h=h: E.tensor_tensor(out=YB[0:64, h, :], in0=YB[0:64, h, :], in1=BON[0:64, h, :], op=ALU.add), reads=[b_BON], writes=[b_YB])
            P.op("dve", lambda E, h=h, i=i: E.tensor_tensor(out=yo[i][0:64, :], in0=YB[0:64, h, :], in1=G[0:64, h, :], op=ALU.mult), reads=[b_G, b_YB], writes=[b_yo[i]])
            toks.append(P.dma("sp", YOUT.rows(yrow0 + h * 64, 64, c0, TR), yo[i][0:64, :], reads=[b_yo[i]], writes=[Buf()]))
    return toks


def sb_mixer(C, AF32, ABF, XN, W, cols, KT, S, NHS, cst, YOUT, yrow0, TT=512):
    P = C.P
    NB = S // 128
    scale = 128 ** -0.5
    xn = ABF.alloc(KT, TT)
    b_xn = Buf()
    QT = ABF.alloc(NHS, S); b_QT = Buf()
    KTt = ABF.alloc(NHS, S); b_KT = Buf()
    VT = ABF.alloc(NB, NHS, 128); b_VT = Buf()
    vf = [AF32.alloc(TT), AF32.alloc(TT)]; b_vf = [Buf(), Buf()]
    nvf = 0
    for tt in range(S // TT):
        c0 = tt * TT
        load_xn(P, xn, b_xn, XN, KT, c0, TT)
        for h0 in range(0, NHS, 4):
            nh = min(4, NHS - h0)
            blocks = [(0, j * 128, 128) for j in range(nh)]

            def ev_q(bi, ci, ps, psb, h0=h0, c0=c0):
                P.op("act", lambda E: E.activation(out=QT[:, h0 + bi, c0:c0 + TT], in_=ps, func=AF.Copy, scale=scale), reads=[psb], writes=[b_QT])

            def ev_k(bi, ci, ps, psb, h0=h0, c0=c0):
                P.op("act", lambda E: E.activation(out=KTt[:, h0 + bi, c0:c0 + TT], in_=ps, func=AF.Copy), reads=[psb], writes=[b_KT])
            proj_blocks(C, xn, b_xn, [(0, TT)], W, KT, [(cols["q"] + h0 * 128, nh * 128)], blocks, ev_q, bank0=0)
            proj_blocks(C, xn, b_xn, [(0, TT)], W, KT, [(cols["k"] + h0 * 128, nh * 128)], blocks, ev_k, bank0=4)
            for j in range(nh):
                h = h0 + j

                def ev_v(bi, ci, ps, psb, h=h, c0=c0):
                    nonlocal nvf
                    i = nvf % 2
                    nvf += 1
                    P.op("act", lambda E, i=i: E.activation(out=vf[i][:, :], in_=ps, func=AF.Copy), reads=[psb], writes=[b_vf[i]])
                    for bb in range(TT // 128):
                        bank = 4 + (bb % 4)
                        P.mm(lambda E, i=i, bb=bb, bank=bank: E.matmul(C.ps[bank][:, 0:128], lhsT=vf[i][:, bb * 128:(bb + 1) * 128], rhs=C.ident[:], start=True, stop=True),
                             reads=[b_vf[i], C.b_ident], writes=[C.psb[bank]], last=True)
                        P.op("dve", lambda E, bb=bb, bank=bank: E.tensor_copy(out=VT[:, c0 // 128 + bb, h, :], in_=C.ps[bank][:, 0:128]), reads=[C.psb[bank]], writes=[b_VT])
                proj_blocks(C, xn, b_xn, [(0, TT)], W, KT, [(cols["v"] + h * 128, 128)], [(0, 0, 128)], ev_v, bank0=0)
    NP = 3
    E1 = [AF32.alloc(128) for _ in range(NP)]; b_E1 = [Buf() for _ in range(NP)]
    SP = [AF32.alloc(128) for _ in range(NP)]; b_SP = [Buf() for _ in range(NP)]
    ARG = [AF32.alloc(128) for _ in range(NP)]; b_ARG = [Buf() for _ in range(NP)]
    WT = [ABF.alloc(128) for _ in range(NP)]; b_WT = [Buf() for _ in range(NP)]
    SPS = AF32.alloc(128); b_SPS = Buf()
    yo = [ABF.alloc(128), ABF.alloc(128)]; b_yo = [Buf(), Buf()]
    toks = []
    it = 0
    nq = 0
    for h in range(NHS):
        for qb in range(NB):
            bo = 6 + nq % 2
            nq += 1
            for kb in range(qb, -1, -1):
                i = it % NP
                it += 1
                bz = i
                bl = 3 + i
                diag = (kb == qb)
                P.mm(lambda E, h=h, kb=kb, qb=qb, bz=bz: E.matmul(C.ps[bz][:, 0:128], lhsT=KTt[:, h, kb * 128:(kb + 1) * 128], rhs=QT[:, h, qb * 128:(qb + 1) * 128], start=True, stop=True),
                     reads=[b_KT, b_QT], writes=[C.psb[bz]], last=True)
                P.op("act", lambda E, i=i, bz=bz: E.activation(out=E1[i][:, :], in_=C.ps[bz][:, 0:128], func=AF.Exp), reads=[C.psb[bz]], writes=[b_E1[i]])
                P.op("act", lambda E, i=i: E.activation(out=SP[i][:, :], in_=E1[i][:, :], func=AF.Ln, bias=1.0), reads=[b_E1[i]], writes=[b_SP[i]])
                P.op("dve", lambda E, i=i, bz=bz: E.tensor_tensor(out=ARG[i][:, :], in0=C.ps[bz][:, 0:128], in1=SP[i][:, :], op=ALU.subtract), reads=[C.psb[bz], b_SP[i]], writes=[b_ARG[i]])
                if diag:
                    P.op("dve", lambda E, i=i: E.tensor_tensor(out=SP[i][:, :], in0=SP[i][:, :], in1=cst["mS"][:, :], op=ALU.mult), reads=[cst["b"], b_ARG[i]], writes=[b_SP[i]])
                P.mm(lambda E, i=i, bl=bl, diag=diag: E.matmul(C.ps[bl][:, 0:128], lhsT=cst["ntri"][:, :], rhs=SP[i][:, :], start=True, stop=diag),
                     reads=[b_SP[i], cst["b"]], writes=[C.psb[bl]], last=diag)
                if not diag:
                    P.mm(lambda E, bl=bl: E.matmul(C.ps[bl][:, 0:128], lhsT=cst["nones"][:, :], rhs=SPS[:, :], start=False, stop=True),
                         reads=[b_SPS, cst["b"]], writes=[C.psb[bl]], last=True)
                P.op("dve", lambda E, i=i, bl=bl: E.tensor_tensor(out=ARG[i][:, :], in0=ARG[i][:, :], in1=C.ps[bl][:, 0:128], op=ALU.add), reads=[C.psb[bl]], writes=[b_ARG[i]])
                P.op("act", lambda E, i=i: E.activation(out=WT[i][:, :], in_=ARG[i][:, :], func=AF.Exp), reads=[b_ARG[i]], writes=[b_WT[i]])
                if diag:
                    P.op("dve", lambda E, i=i: E.tensor_tensor(out=WT[i][:, :], in0=WT[i][:, :], in1=cst["mSb"][:, :], op=ALU.mult), reads=[cst["b"]], writes=[b_WT[i]])
                if kb > 0:
                    if diag:
                        P.op("pool", lambda E, i=i: E.tensor_copy(out=SPS[:, :], in_=SP[i][:, :]), reads=[b_SP[i]], writes=[b_SPS])
                    else:
                        P.op("pool", lambda E, i=i: E.tensor_tensor(out=SPS[:, :], in0=SPS[:, :], in1=SP[i][:, :], op=ALU.add), reads=[b_SP[i]], writes=[b_SPS])
                P.mm(lambda E, i=i, h=h, kb=kb, qb=qb, bo=bo: E.matmul(C.ps[bo][:, 0:128], lhsT=VT[:, kb, h, :], rhs=WT[i][:, :], start=(kb == qb), stop=(kb == 0)),
                     reads=[b_VT, b_WT[i]], writes=[C.psb[bo]], last=True)
            j = nq % 2
            P.op("act", lambda E, j=j, bo=bo: E.activation(out=yo[j][:, :], in_=C.ps[bo][:, 0:128], func=AF.Copy), reads=[C.psb[bo]], writes=[b_yo[j]])
            toks.append(P.dma("sp", YOUT.rows(yrow0 + h * 128, 128, qb * 128, 128), yo[j][:, :], reads=[b_yo[j]], writes=[Buf()]))
    return toks

import numpy as np, ml_dtypes

LORA = 288
CONV_K = 31


class Cfg:
    def __init__(self, D=4096, DFF=11008, SEQ=2048, B=4, DG=1024):
        self.D, self.DFF, self.SEQ, self.B, self.DG = D, DFF, SEQ, B, DG
        self.KT = D // 128
        self.NT = SEQ // 2
        self.CT = DG // 128
        self.NHR = DG // 64
        self.NHS = DG // 128
        self.NHRc = self.NHR // 2
        self.NHSc = self.NHS // 2
        self.NRW = 3 * DG + LORA
        self.NIN = 2 * DG + self.NRW + 2 * DG + 3 * DG
        c = 0
        self.c_ca = c; c += DG
        self.c_cg = c; c += DG
        self.c_su = c; c += DG
        self.c_sv = c; c += DG
        self.c_r = c; c += self.NHRc * 64
        self.c_k = c; c += self.NHRc * 64
        self.c_v = c; c += self.NHRc * 64
        self.c_l = c; c += LORA
        self.c_q = c; c += self.NHSc * 128
        self.c_sk = c; c += self.NHSc * 128
        self.c_svv = c; c += self.NHSc * 128
        self.NC = c
        CT = self.CT
        o = 0
        self.o_cw = o; o += CT * CONV_K
        self.o_cb = o; o += CT
        self.o_clg = o; o += CT
        self.o_clb = o; o += CT
        self.o_slg = o; o += CT
        self.o_slb = o; o += CT
        self.NP128 = o
        NH = self.NHRc
        o = 0
        self.q_mu = o; o += 3 * NH + 5
        self.q_ka = o; o += NH
        self.q_w0 = o; o += NH
        self.q_a0 = o; o += NH
        self.q_kk = o; o += NH
        self.q_rk = o; o += NH
        self.q_lg = o; o += NH
        self.q_lb = o; o += NH
        self.q_wup = o; o += NH * 64
        self.q_aup = o; o += NH * 64
        self.q_gup = o; o += 3 * NH * 64
        self.NP64 = o
        self.NC128 = 4 * 128 + 4 * NH * 64
        self.YOWN = 2 * DG
        self.YALL = self.NHRc * 64 + self.NHSc * 128
        self.TR = 256


def consts_np(cfg):
    NH = cfg.NHRc
    c = np.zeros((128, cfg.NC128), np.float32)
    i = np.arange(128)
    c[:, 0:128] = np.eye(128)
    c[:, 128:256] = (i[:, None] < i[None, :])
    c[:, 256:384] = -(i[:, None] > i[None, :]).astype(np.float32)
    c[:, 384:512] = -1.0
    j = np.arange(64)
    e8 = np.tile(np.eye(64, dtype=np.float32), (1, NH))
    mS = np.tile((j[:, None] < j[None, :]).astype(np.float32), (1, NH))
    mI = np.tile((j[:, None] <= j[None, :]).astype(np.float32), (1, NH))
    mL = np.tile((j[:, None] > j[None, :]).astype(np.float32), (1, NH))
    o = 512
    for m in (e8, mS, mI, mL):
        c[0:64, o:o + NH * 64] = m
        o += NH * 64
    return c


def tile128(v, nt):
    return np.ascontiguousarray(np.asarray(v).reshape(nt, 128).T)


def mixer_inputs(cfg, L, role, xn_full_T, own_T):
    DG, NH, NHS, CT = cfg.DG, cfg.NHRc, cfg.NHSc, cfg.CT
    W = L["mix_w_in"]
    R0 = 2 * DG
    S0 = 2 * DG + cfg.NRW
    B0 = S0 + 2 * DG
    hr = role * NH * 64
    hs = role * NHS * 128
    colsel = np.concatenate([
        np.arange(0, 2 * DG), np.arange(S0, S0 + 2 * DG),
        R0 + hr + np.arange(NH * 64), R0 + DG + hr + np.arange(NH * 64), R0 + 2 * DG + hr + np.arange(NH * 64),
        R0 + 3 * DG + np.arange(LORA),
        B0 + hs + np.arange(NHS * 128), B0 + DG + hs + np.arange(NHS * 128), B0 + 2 * DG + hs + np.arange(NHS * 128)])
    assert len(colsel) == cfg.NC
    wmix = np.ascontiguousarray(W[:, colsel])
    p128 = np.zeros((128, cfg.NP128), np.float32)
    cw = L["conv_w"]
    p128[:, cfg.o_cw:cfg.o_cw + CT * CONV_K] = cw.T.reshape(CT, 128, CONV_K).transpose(1, 0, 2).reshape(128, CT * CONV_K)
    p128[:, cfg.o_cb:cfg.o_cb + CT] = tile128(L["conv_b"], CT)
    p128[:, cfg.o_clg:cfg.o_clg + CT] = tile128(L["conv_ln_g"], CT)
    p128[:, cfg.o_clb:cfg.o_clb + CT] = tile128(L["conv_ln_b"], CT)
    p128[:, cfg.o_slg:cfg.o_slg + CT] = tile128(L["sgu_ln_g"], CT)
    p128[:, cfg.o_slb:cfg.o_slb + CT] = tile128(L["sgu_ln_b"], CT)
    ws = L["sgu_w_s"]
    wsT = np.ascontiguousarray(ws.transpose(2, 0, 1).reshape(128, CT * 128))
    bsb = np.ascontiguousarray(np.broadcast_to(L["sgu_b_s"].reshape(1, CT * 128), (128, CT * 128)))
    p64 = np.zeros((64, cfg.NP64), np.float32)
    mu = L["rwkv_mu"]

    def h64(v, h0):
        return np.asarray(v).reshape(-1, 64)[h0:h0 + NH].T
    hh = role * NH
    o = cfg.q_mu
    p64[:, o:o + NH] = h64(mu[0:DG], hh); o += NH
    p64[:, o:o + NH] = h64(mu[DG:2 * DG], hh); o += NH
    p64[:, o:o + NH] = h64(mu[2 * DG:3 * DG], hh); o += NH
    ml = mu[3 * DG:3 * DG + LORA]
    for b, (s, n) in enumerate(((0, 64), (64, 64), (128, 64), (192, 64), (256, 32))):
        p64[0:n, o + b] = ml[s:s + n]
    p64[:, cfg.q_ka:cfg.q_ka + NH] = h64(L["rwkv_k_a"], hh)
    p64[:, cfg.q_w0:cfg.q_w0 + NH] = h64(L["rwkv_w0"], hh)
    p64[:, cfg.q_a0:cfg.q_a0 + NH] = h64(L["rwkv_a0"], hh)
    p64[:, cfg.q_kk:cfg.q_kk + NH] = h64(L["rwkv_k_k"], hh)
    p64[:, cfg.q_rk:cfg.q_rk + NH] = h64(L["rwkv_r_k"].reshape(-1), hh)
    p64[:, cfg.q_lg:cfg.q_lg + NH] = h64(L["rwkv_lnx_g"], hh)
    p64[:, cfg.q_lb:cfg.q_lb + NH] = h64(L["rwkv_lnx_b"], hh)
    p64[:, cfg.q_wup:cfg.q_wup + NH * 64] = L["rwkv_w_up"][:, hr:hr + NH * 64]
    p64[:, cfg.q_aup:cfg.q_aup + NH * 64] = L["rwkv_a_up"][:, hr:hr + NH * 64]
    gu = L["rwkv_g_up"][:, hr:hr + NH * 64]
    for b, (s, n) in enumerate(((0, 64), (64, 64), (128, 32))):
        p64[0:n, cfg.q_gup + b * NH * 64: cfg.q_gup + (b + 1) * NH * 64] = gu[s:s + n]
    return {"xno": own_T, "xn": xn_full_T, "wmix": wmix, "p128": p128, "wsT": wsT, "bsb": bsb, "p64": p64, "c128": consts_np(cfg)}


def mixer_body(P, C, cfg, T, AR, do=("conv", "sgu", "rwkv", "sb"), halo_mask=None, b_hm=None):
    CT, NH = cfg.CT, cfg.NHRc
    AF32 = ArenaView(AR, F32)
    ABF = ArenaView(AR, BF16)
    p128 = AF32.alloc(cfg.NP128); bp128 = Buf()
    P.dma("sp", p128, T["p128"], writes=[bp128])
    c128 = AF32.alloc(cfg.NC128); bc = Buf()
    P.dma("sp", c128, T["c128"], writes=[bc])
    p64 = AF32.alloc(cfg.NP64); bp64 = Buf()
    P.dma("sp", p64[0:64, :], T["p64"], writes=[bp64])
    om = AF32.alloc(4 * NH + 5)
    P.op("dve", lambda E: E.tensor_scalar(out=om[0:64, :], in0=p64[0:64, cfg.q_mu:cfg.q_mu + 4 * NH + 5], scalar1=-1.0, scalar2=1.0, op0=ALU.mult, op1=ALU.add), reads=[bp64], writes=[bp64])
    wsT = ABF.alloc(CT, 128)
    P.dma("pool", wsT, T["wsT"].rearrange("p (h i) -> p h i", i=128), writes=[bp128])
    P.op("dve", lambda E: E.memset(wsT[64:128, :, 0:64], 0.0), reads=[], writes=[bp128])
    mSb = ABF.alloc(128)
    P.op("dve", lambda E: E.tensor_copy(out=mSb, in_=c128[:, 128:256]), reads=[bc], writes=[bc])
    C.ident = c128[:, 0:128]
    C.b_ident = bc
    cst = {"b": bc, "mS": c128[:, 128:256], "ntri": c128[:, 256:384], "nones": c128[:, 384:512], "mSb": mSb,
           "ident8": c128[:, 512:512 + NH * 64], "mS8": c128[:, 512 + NH * 64:512 + 2 * NH * 64],
           "mI8": c128[:, 512 + 2 * NH * 64:512 + 3 * NH * 64], "mL8": c128[:, 512 + 3 * NH * 64:512 + 4 * NH * 64]}
    K = CONV_K
    par_c = {"b": bp128, "cw": p128[:, cfg.o_cw:cfg.o_cw + CT * K].rearrange("p (c k) -> p c k", k=K), "cb": p128[:, cfg.o_cb:cfg.o_cb + CT],
             "lg": p128[:, cfg.o_clg:cfg.o_clg + CT], "lb": p128[:, cfg.o_clb:cfg.o_clb + CT]}
    par_s = {"b": bp128, "lg": p128[:, cfg.o_slg:cfg.o_slg + CT], "lb": p128[:, cfg.o_slb:cfg.o_slb + CT], "wsT": wsT}
    q = cfg.q_mu
    par_r = {"b": bp64, "mu_r": p64[:, q:q + NH], "mu_k": p64[:, q + NH:q + 2 * NH], "mu_v": p64[:, q + 2 * NH:q + 3 * NH], "mu_l": p64[:, q + 3 * NH:q + 3 * NH + 5],
             "omu_r": om[:, 0:NH], "omu_k": om[:, NH:2 * NH], "omu_v": om[:, 2 * NH:3 * NH], "omu_l": om[:, 3 * NH:3 * NH + 5],
             "ka": p64[:, cfg.q_ka:cfg.q_ka + NH], "oka": om[:, 3 * NH + 5:4 * NH + 5],
             "w0": p64[:, cfg.q_w0:cfg.q_w0 + NH], "a0": p64[:, cfg.q_a0:cfg.q_a0 + NH], "kk": p64[:, cfg.q_kk:cfg.q_kk + NH], "rk": p64[:, cfg.q_rk:cfg.q_rk + NH],
             "lg": p64[:, cfg.q_lg:cfg.q_lg + NH], "lb": p64[:, cfg.q_lb:cfg.q_lb + NH],
             "wup": p64[:, cfg.q_wup:cfg.q_wup + NH * 64], "aup": p64[:, cfg.q_aup:cfg.q_aup + NH * 64],
             "gup": p64[:, cfg.q_gup:cfg.q_gup + 3 * NH * 64].rearrange("p (b n) -> p b n", b=3)}
    base = AR.off
    toks = []
    if "conv" in do:
        toks += conv_mixer(C, AF32, ABF, T["xno"], T["wmix"], cfg.c_ca, cfg.c_cg, cfg.KT, cfg.NT, CT, par_c, T["yown"], 0, halo_mask=halo_mask, b_hm=b_hm)
        P.barrier(); AR.reset(base)
    if "sgu" in do:
        bsb = AF32.alloc(CT, 128)
        P.dma("sp", bsb, T["bsb"].rearrange("p (h i) -> p h i", i=128), writes=[bp128])
        par_s["bsb"] = bsb
        toks += sgu_mixer(C, AF32, ABF, T["xno"], T["wmix"], cfg.c_su, cfg.c_sv, cfg.KT, cfg.NT, CT, par_s, T["yown"], cfg.DG)
        P.barrier(); AR.reset(base)
    if "rwkv" in do:
        toks += rwkv_mixer(C, AF32, ABF, T["xn"], T["wmix"], {"r": cfg.c_r, "k": cfg.c_k, "v": cfg.c_v, "lora": cfg.c_l}, cfg.KT, cfg.SEQ, NH, par_r, cst, T["yall"], 0, TR=cfg.TR, DBG=T.get("dbg"))
        P.barrier(); AR.reset(base)
    if "sb" in do:
        toks += sb_mixer(C, AF32, ABF, T["xn"], T["wmix"], {"q": cfg.c_q, "k": cfg.c_sk, "v": cfg.c_svv}, cfg.KT, cfg.SEQ, cfg.NHSc, cst, T["yall"], NH * 64)
        P.barrier(); AR.reset(base)
    return toks


def build_mixer(cfg, do=("conv", "sgu", "rwkv", "sb"), ar_bytes=190 * 1024, dbg=False):
    nc = bass.Bass("TRN2", target_bir_lowering=False)
    P = PB(nc)
    D = cfg.D
    T = {}
    T["xno"] = DV(nc.dram_tensor("xno", [D, 32 + cfg.NT], BF16, kind="ExternalInput").ap())
    T["xn"] = DV(nc.dram_tensor("xn", [D, cfg.SEQ], BF16, kind="ExternalInput").ap())
    T["wmix"] = nc.dram_tensor("wmix", [D, cfg.NC], F32, kind="ExternalInput").ap()
    T["p128"] = nc.dram_tensor("p128", [128, cfg.NP128], F32, kind="ExternalInput").ap()
    T["wsT"] = nc.dram_tensor("wsT", [128, cfg.CT * 128], F32, kind="ExternalInput").ap()
    T["bsb"] = nc.dram_tensor("bsb", [128, cfg.CT * 128], F32, kind="ExternalInput").ap()
    T["p64"] = nc.dram_tensor("p64", [64, cfg.NP64], F32, kind="ExternalInput").ap()
    T["c128"] = nc.dram_tensor("c128", [128, cfg.NC128], F32, kind="ExternalInput").ap()
    T["yown"] = nc.dram_tensor("yown", [cfg.YOWN, cfg.NT], BF16, kind="ExternalOutput").ap()
    T["yall"] = RV(nc.dram_tensor("yall", [cfg.YALL, cfg.SEQ], BF16, kind="ExternalOutput").ap())
    if dbg:
        T["dbg"] = nc.dram_tensor("dbg", [13, 64, cfg.NHRc * cfg.TR], F32, kind="ExternalOutput").ap()
    C = Ctx(P, cfg.NT)
    AR = Arena(P, "arena", ar_bytes)
    toks = mixer_body(P, C, cfg, T, AR, do)
    print("mixer arena peak KB", AR.peak * 4 / 1024)
    P.finish(toks)
    print("mixer n_inst", P.n_inst)
    return nc

import numpy as np, ml_dtypes

BF = ml_dtypes.bfloat16


def t_phase_body(P, C, cfg, AR, T, mode, xres):
    D, DFF, NT, KT = cfg.D, cfg.DFF, cfg.NT, cfg.KT
    base = AR.off
    XR = T["xres"]
    NG = {"first": 2, "mid": 3, "last": 2}[mode]
    g_all = AR.alloc((NG * KT,), F32)
    nt = norm_tmp(P, AR, KT)
    P.dma("sp", g_all, T["gn"], writes=[nt["b_g"]])
    KM = (4 * cfg.DG) // 128
    xn = AR.alloc((max(KT, KM), NT), BF16)
    b_xn = Buf()
    FT = DFF // 128
    chmax = max(4, -(-FT // 4))
    gT = AR.alloc((chmax, NT), BF16)
    b_gT = Buf()
    rt = rmw_tmp(P, AR)
    stmp = [AR.alloc((512,), F32) for _ in range(2)]
    b_stmp = [Buf(), Buf()]
    toks = []
    gi = 0
    if mode in ("mid", "last"):
        if "yT" in T:
            for k in range(KM):
                P.dma("sp", xn[:, k, :], T["yT"][k * 128:(k + 1) * 128, :], writes=[b_xn])
        else:
            load_y_fused(P, cfg, gT, T, xn, b_xn)
        proj_rmw(C, xn, b_xn, 0, KM, T["wmo"], 0, D, XR, xres, 1.0, rt)
        rmsnorm(C, XR, g_all[:, gi * KT:(gi + 1) * KT], D, xn_sb=xn, b_xn=b_xn, xres=xres, tmp=nt)
        gi += 1
        ffn(C, xn, b_xn, T["win2"], T["wout2"], D, DFF, XR, xres, rt, gT, b_gT, stmp, b_stmp)
    if mode in ("first", "mid"):
        rmsnorm(C, XR, g_all[:, gi * KT:(gi + 1) * KT], D, xn_sb=xn, b_xn=b_xn, xres=xres, tmp=nt)
        gi += 1
        ffn(C, xn, b_xn, T["win1"], T["wout1"], D, DFF, XR, xres, rt, gT, b_gT, stmp, b_stmp)
        rmsnorm(C, XR, g_all[:, gi * KT:(gi + 1) * KT], D, xn_sb=xn, b_xn=b_xn, xres=xres, tmp=nt)
        for k in range(KT):
            toks.append(P.dma("sp", T["xn2"].tile(k, 0, NT), xn[:, k, :], reads=[b_xn], writes=[T.get("b_xn2", Buf())]))
    else:
        toks += rmsnorm(C, XR, g_all[:, gi * KT:(gi + 1) * KT], D, out_f32=T["out"], xres=xres, tmp=nt)
    P.barrier()
    AR.reset(base)
    return toks


def load_y_fused(P, cfg, gT, T, xn, b_xn):
    NT, DG = cfg.NT, cfg.DG
    CT = cfg.CT
    nr, ns = cfg.NHRc * 64, cfg.NHSc * 128
    YA = cfg.YALL
    tA = [gT[:, 0, :], gT[:, 1, :]]
    tB = [gT[:, 2, :], gT[:, 3, :]]
    bA = [Buf(), Buf()]
    bB = [Buf(), Buf()]
    rm = T["rolemask"]
    k = 0
    it = 0
    def own(row0, ntile):
        nonlocal k
        for j in range(ntile):
            P.dma("sp", xn[:, k, :], T["yown"][row0 + j * 128:row0 + (j + 1) * 128, :], reads=[T["b_yown"]], writes=[b_xn])
            k += 1
    def gathered(row0, ntile):
        nonlocal k, it
        for r in range(2):
            for j in range(ntile):
                i = it % 2
                it += 1
                rr = r * YA + row0 + j * 128
                P.dma("sp", tA[i], T["yallg"](r, row0 + j * 128, 0, NT), reads=[T["b_yallg"]], writes=[bA[i]])
                P.dma("sp", tB[i], T["yallg"](r, row0 + j * 128, NT, NT), reads=[T["b_yallg"]], writes=[bB[i]])
                P.op("dve", lambda E, i=i: E.tensor_scalar(out=tA[i], in0=tA[i], scalar1=rm[:, 0:1], scalar2=None, op0=ALU.mult), reads=[T["b_rm"]], writes=[bA[i]])
                P.op("dve", lambda E, i=i, kk=k: E.scalar_tensor_tensor(out=xn[:, kk, :], in0=tB[i], scalar=rm[:, 1:2], in1=tA[i], op0=ALU.mult, op1=ALU.add), reads=[bA[i], bB[i], T["b_rm"]], writes=[b_xn])
                k += 1
    own(0, CT)
    gathered(0, nr // 128)
    own(DG, CT)
    gathered(nr, ns // 128)


def build_T(cfg, mode):
    nc = bass.Bass("TRN2", target_bir_lowering=False)
    P = PB(nc)
    D, DFF, NT, KT = cfg.D, cfg.DFF, cfg.NT, cfg.KT
    T = {}
    NG = {"first": 2, "mid": 3, "last": 2}[mode]
    T["xin"] = nc.dram_tensor("xin", [D, NT], F32, kind="ExternalInput").ap()
    T["gn"] = nc.dram_tensor("gn", [128, NG * KT], F32, kind="ExternalInput").ap()
    if mode in ("mid", "last"):
        T["yT"] = nc.dram_tensor("yT", [4 * cfg.DG, NT], BF16, kind="ExternalInput").ap()
        T["wmo"] = nc.dram_tensor("wmo", [4 * cfg.DG, D], F32, kind="ExternalInput").ap()
        T["win2"] = nc.dram_tensor("win2", [D, 2 * DFF], F32, kind="ExternalInput").ap()
        T["wout2"] = nc.dram_tensor("wout2", [DFF, D], F32, kind="ExternalInput").ap()
    if mode in ("first", "mid"):
        T["win1"] = nc.dram_tensor("win1", [D, 2 * DFF], F32, kind="ExternalInput").ap()
        T["wout1"] = nc.dram_tensor("wout1", [DFF, D], F32, kind="ExternalInput").ap()
        T["xn2"] = DV(nc.dram_tensor("xn2", [D, NT], BF16, kind="ExternalOutput").ap())
        T["xres"] = nc.dram_tensor("xres", [D, NT], F32, kind="ExternalOutput").ap()
    else:
        T["xres"] = nc.dram_tensor("xres", [D, NT], F32, kind="Internal").ap()
        T["out"] = nc.dram_tensor("out", [D, NT], F32, kind="ExternalOutput").ap()
    C = Ctx(P, NT)
    AR = Arena(P, "arena", 190 * 1024)
    H = NT // 512
    xres = {(f, h): Buf() for f in range(KT) for h in range(H)}
    for f in range(KT):
        for h in range(H):
            P.dma("sp", T["xres"][f * 128:(f + 1) * 128, h * 512:(h + 1) * 512], T["xin"][f * 128:(f + 1) * 128, h * 512:(h + 1) * 512], writes=[xres[(f, h)]])
    toks = t_phase_body(P, C, cfg, AR, T, mode, xres)
    toks += [b.w for b in xres.values()]
    P.finish(toks)
    return nc


def gn_pack(cfg, vecs):
    return np.ascontiguousarray(np.concatenate([tile128(v, cfg.KT) for v in vecs], axis=1))


_PROGS = {}


def _prog(cfg, key, builder):
    k = (cfg.D, cfg.DFF, cfg.SEQ, cfg.DG, key)
    if k not in _PROGS:
        _PROGS[k] = builder()
    return _PROGS[k]


def run_module(cfg, inp):
    B, SEQ, D, NT = cfg.B, cfg.SEQ, cfg.D, cfg.NT
    NCORE = 2 * B
    depth = inp["ffn1_w_in"].shape[0]
    cores = list(range(NCORE))
    x = inp["x"]
    xres = [np.ascontiguousarray(x[c // 2, (c % 2) * NT:(c % 2 + 1) * NT, :].T) for c in cores]
    yT = None
    out = None
    for l in range(depth + 1):
        mode = "first" if l == 0 else ("last" if l == depth else "mid")
        nc = _prog(cfg, "T" + mode, lambda: build_T(cfg, mode))
        common = {}
        if mode == "first":
            common["gn"] = gn_pack(cfg, [inp["ffn1_norm"][0], inp["mix_norm"][0]])
        elif mode == "mid":
            common["gn"] = gn_pack(cfg, [inp["ffn2_norm"][l - 1], inp["ffn1_norm"][l], inp["mix_norm"][l]])
        else:
            common["gn"] = gn_pack(cfg, [inp["ffn2_norm"][l - 1], inp["final_norm"]])
        if mode in ("mid", "last"):
            common["wmo"] = inp["mix_w_out"][l - 1]
            common["win2"] = inp["ffn2_w_in"][l - 1]
            common["wout2"] = inp["ffn2_w_out"][l - 1]
        if mode in ("first", "mid"):
            common["win1"] = inp["ffn1_w_in"][l]
            common["wout1"] = inp["ffn1_w_out"][l]
        maps = []
        for c in cores:
            m = dict(common)
            m["xin"] = xres[c]
            if mode in ("mid", "last"):
                m["yT"] = yT[c]
            maps.append(m)
        res = run_bass_kernel_spmd(nc, maps, core_ids=cores)
        if mode == "last":
            out = np.empty((B, SEQ, D), np.float32)
            for c in cores:
                out[c // 2, (c % 2) * NT:(c % 2 + 1) * NT, :] = res.results[c]["out"].T
            return out
        xres = [res.results[c]["xres"] for c in cores]
        xn2 = [res.results[c]["xn2"] for c in cores]
        L = {k: inp[k][l] for k in ("mix_w_in", "conv_w", "conv_b", "conv_ln_g", "conv_ln_b", "rwkv_mu", "rwkv_w0", "rwkv_w_up", "rwkv_a0",
                                    "rwkv_a_up", "rwkv_g_up", "rwkv_k_k", "rwkv_k_a", "rwkv_r_k", "rwkv_lnx_g", "rwkv_lnx_b",
                                    "sgu_ln_g", "sgu_ln_b", "sgu_w_s", "sgu_b_s")}
        ncm = _prog(cfg, "M", lambda: build_mixer(cfg))
        maps = []
        shared = {}
        for c in cores:
            b, role = c // 2, c % 2
            full = np.concatenate([xn2[2 * b], xn2[2 * b + 1]], axis=1)
            own = np.zeros((D, 32 + NT), BF)
            own[:, 32:] = xn2[c]
            if role == 1:
                own[:, 0:32] = xn2[2 * b][:, NT - 32:NT]
            if role not in shared:
                shared[role] = mixer_inputs(cfg, L, role, None, None)
            m = dict(shared[role])
            m["xn"] = full
            m["xno"] = own
            maps.append(m)
        res = run_bass_kernel_spmd(ncm, maps, core_ids=cores)
        DG = cfg.DG
        nr, ns = cfg.NHRc * 64, cfg.NHSc * 128
        yT = []
        for c in cores:
            b, role = c // 2, c % 2
            ts = slice(role * NT, (role + 1) * NT)
            yo = res.results[c]["yown"]
            ya = [res.results[2 * b]["yall"], res.results[2 * b + 1]["yall"]]
            yT.append(np.ascontiguousarray(np.concatenate([
                yo[0:DG], ya[0][0:nr, ts], ya[1][0:nr, ts], yo[DG:2 * DG], ya[0][nr:nr + ns, ts], ya[1][nr:nr + ns, ts]], axis=0)))
    return out


CC_BYTES = 2 * 2 ** 20
MIX_KEYS = ("mix_w_in", "conv_w", "conv_b", "conv_ln_g", "conv_ln_b", "rwkv_mu", "rwkv_w0", "rwkv_w_up", "rwkv_a0",
            "rwkv_a_up", "rwkv_g_up", "rwkv_k_k", "rwkv_k_a", "rwkv_r_k", "rwkv_lnx_g", "rwkv_lnx_b",
            "sgu_ln_g", "sgu_ln_b", "sgu_w_s", "sgu_b_s")


def build_fused(cfg, depth):
    nc = bass.Bass("TRN2", target_bir_lowering=False)
    P = PB(nc)
    D, DFF, NT, KT, SEQ = cfg.D, cfg.DFF, cfg.NT, cfg.KT, cfg.SEQ
    ext = lambda n, sh, dt: nc.dram_tensor(n, list(sh), dt, kind="ExternalInput").ap()
    X = {}
    X["xin"] = ext("xin", [D, NT], F32)
    NGT = 3 * depth + 1
    X["gn"] = ext("gn", [128, NGT * KT], F32)
    X["rolemask"] = ext("rolemask", [128, 2], F32)
    X["c128"] = ext("c128", [128, cfg.NC128], F32)
    for l in range(depth):
        X[f"win1_{l}"] = ext(f"win1_{l}", [D, 2 * DFF], F32)
        X[f"wout1_{l}"] = ext(f"wout1_{l}", [DFF, D], F32)
        X[f"win2_{l}"] = ext(f"win2_{l}", [D, 2 * DFF], F32)
        X[f"wout2_{l}"] = ext(f"wout2_{l}", [DFF, D], F32)
        X[f"wmo_{l}"] = ext(f"wmo_{l}", [4 * cfg.DG, D], F32)
        X[f"wmix_{l}"] = ext(f"wmix_{l}", [D, cfg.NC], F32)
        X[f"p128_{l}"] = ext(f"p128_{l}", [128, cfg.NP128], F32)
        X[f"wsT_{l}"] = ext(f"wsT_{l}", [128, cfg.CT * 128], F32)
        X[f"bsb_{l}"] = ext(f"bsb_{l}", [128, cfg.CT * 128], F32)
        X[f"p64_{l}"] = ext(f"p64_{l}", [64, cfg.NP64], F32)
    OUT = nc.dram_tensor("out", [D, NT], F32, kind="ExternalOutput").ap()
    XRES = nc.dram_tensor("xres", [D, NT], F32, kind="Internal").ap()
    groups = [[2 * b, 2 * b + 1] for b in range(cfg.B)]
    C = Ctx(P, NT)
    AR = Arena(P, "arena", 190 * 1024)
    rm = AR.alloc((2,), F32)
    b_rm = Buf()
    P.dma("sp", rm, X["rolemask"], writes=[b_rm])
    base0 = AR.off
    H = NT // 512
    xres = {(f, h): Buf() for f in range(KT) for h in range(H)}
    for f in range(KT):
        for h in range(H):
            P.dma("sp", XRES[f * 128:(f + 1) * 128, h * 512:(h + 1) * 512], X["xin"][f * 128:(f + 1) * 128, h * 512:(h + 1) * 512], writes=[xres[(f, h)]])
    toks = []
    gi = 0
    prev = None
    for l in range(depth + 1):
        mode = "first" if l == 0 else ("last" if l == depth else "mid")
        NG = {"first": 2, "mid": 3, "last": 2}[mode]
        T = {"xres": XRES, "gn": X["gn"][:, gi * KT:(gi + NG) * KT], "rolemask": rm, "b_rm": b_rm}
        gi += NG
        if mode in ("mid", "last"):
            T["yown"], T["b_yown"], T["yallg"], T["b_yallg"] = prev
            T["wmo"] = X[f"wmo_{l - 1}"]
            T["win2"] = X[f"win2_{l - 1}"]
            T["wout2"] = X[f"wout2_{l - 1}"]
        if mode in ("first", "mid"):
            T["win1"] = X[f"win1_{l}"]
            T["wout1"] = X[f"wout1_{l}"]
            CRX = min(D, CC_BYTES // (NT * 2))
            xn2_ts = [nc.dram_tensor(f"xn2_{l}_{j}", [CRX, NT], BF16) for j in range(D // CRX)]
            xng_ts = [nc.dram_tensor(f"xng_{l}_{j}", [2 * CRX, NT], BF16) for j in range(D // CRX)]

            def xn2_w(k, c0, n, xn2_ts=xn2_ts, CRX=CRX):
                d0 = k * 128
                return xn2_ts[d0 // CRX].ap()[d0 % CRX:d0 % CRX + 128, c0:c0 + n]
            T["xn2"] = DV(fn=xn2_w)
            T["b_xn2"] = Buf()
        else:
            T["out"] = OUT
        toks += t_phase_body(P, C, cfg, AR, T, mode, xres)
        if mode == "last":
            break
        b_xng = Buf()
        for j in range(len(xn2_ts)):
            P.collective("AllGather", [xn2_ts[j].ap().opt()], [xng_ts[j].ap().opt()], groups, reads=[T["b_xn2"]], writes=[b_xng])
        P.barrier()

        def xn_full(k, c0, n, xng_ts=xng_ts, CRX=CRX):
            r = c0 // NT
            assert (c0 + n - 1) // NT == r
            d0 = k * 128
            i = d0 % CRX
            return xng_ts[d0 // CRX].ap()[r * CRX + i:r * CRX + i + 128, c0 - r * NT:c0 - r * NT + n]

        def xn_full_g(k0, nk, c0, n, xng_ts=xng_ts, CRX=CRX):
            r = c0 // NT
            assert (c0 + n - 1) // NT == r
            d0 = k0 * 128
            i = d0 % CRX
            assert i + nk * 128 <= CRX
            return xng_ts[d0 // CRX].ap()[r * CRX + i:r * CRX + i + nk * 128, c0 - r * NT:c0 - r * NT + n].rearrange("(k p) n -> p k n", p=128)

        def xn_own_g(k0, nk, c0, n, xng_ts=xng_ts, xn2_ts=xn2_ts, CRX=CRX):
            d0 = k0 * 128
            i = d0 % CRX
            assert i + nk * 128 <= CRX
            if c0 == 0:
                assert n == 32
                return xng_ts[d0 // CRX].ap()[i:i + nk * 128, NT - 32:NT].rearrange("(k p) n -> p k n", p=128)
            return xn2_ts[d0 // CRX].ap()[i:i + nk * 128, c0 - 32:c0 - 32 + n].rearrange("(k p) n -> p k n", p=128)

        def xn_own(k, c0, n, xng_ts=xng_ts, xn2_ts=xn2_ts, CRX=CRX):
            d0 = k * 128
            i = d0 % CRX
            if c0 == 0:
                assert n == 32
                return xng_ts[d0 // CRX].ap()[i:i + 128, NT - 32:NT]
            return xn2_ts[d0 // CRX].ap()[i:i + 128, c0 - 32:c0 - 32 + n]
        yown_t = nc.dram_tensor(f"yown_{l}", [cfg.YOWN, NT], BF16, kind="Internal")
        CRY = min(cfg.YALL, CC_BYTES // (SEQ * 2))
        yall_ts = [nc.dram_tensor(f"yall_{l}_{j}", [CRY, SEQ], BF16) for j in range(cfg.YALL // CRY)]
        yallg_ts = [nc.dram_tensor(f"yallg_{l}_{j}", [2 * CRY, SEQ], BF16) for j in range(cfg.YALL // CRY)]

        def yall_w(r0, nr_, c0, n, yall_ts=yall_ts, CRY=CRY):
            assert r0 // CRY == (r0 + nr_ - 1) // CRY
            return yall_ts[r0 // CRY].ap()[r0 % CRY:r0 % CRY + nr_, c0:c0 + n]

        def yallg_r(r, row, c0, n, yallg_ts=yallg_ts, CRY=CRY):
            i = row % CRY
            return yallg_ts[row // CRY].ap()[r * CRY + i:r * CRY + i + 128, c0:c0 + n]
        TM = {"xno": DV(fn=xn_own, fn_group=xn_own_g, kg=CRX // 128), "xn": DV(fn=xn_full, fn_group=xn_full_g, kg=CRX // 128), "wmix": X[f"wmix_{l}"], "p128": X[f"p128_{l}"], "wsT": X[f"wsT_{l}"],
              "bsb": X[f"bsb_{l}"], "p64": X[f"p64_{l}"], "c128": X["c128"], "yown": yown_t.ap(), "yall": RV(fn=yall_w)}
        mtoks = mixer_body(P, C, cfg, TM, AR, halo_mask=rm[:, 1:2], b_hm=b_rm)
        AR.reset(base0)
        b_yown = Buf()
        b_yall = Buf()
        P.wait_all("pool", mtoks)
        P.wait_all("sp", mtoks)
        b_yallg = Buf()
        for j in range(len(yall_ts)):
            P.collective("AllGather", [yall_ts[j].ap().opt()], [yallg_ts[j].ap().opt()], groups, reads=[], writes=[b_yallg])
        P.barrier()
        prev = (yown_t.ap(), b_yown, yallg_r, b_yallg)
    toks += [b.w for b in xres.values() if b.w is not None]
    P.finish(toks)
    print("fused n_inst", P.n_inst, "arena peak KB", AR.peak * 4 / 1024)
    return nc


def run_fused(cfg, inp):
    B, SEQ, D, NT = cfg.B, cfg.SEQ, cfg.D, cfg.NT
    NCORE = 2 * B
    depth = inp["ffn1_w_in"].shape[0]
    cores = list(range(NCORE))
    nc = _prog(cfg, ("F", depth), lambda: build_fused(cfg, depth))
    x = inp["x"]
    gvecs = []
    for l in range(depth + 1):
        if l > 0:
            gvecs.append(inp["ffn2_norm"][l - 1])
        if l < depth:
            gvecs += [inp["ffn1_norm"][l], inp["mix_norm"][l]]
        else:
            gvecs.append(inp["final_norm"])
    common = {"gn": gn_pack(cfg, gvecs), "c128": consts_np(cfg)}
    for l in range(depth):
        common[f"win1_{l}"] = inp["ffn1_w_in"][l]
        common[f"wout1_{l}"] = inp["ffn1_w_out"][l]
        common[f"win2_{l}"] = inp["ffn2_w_in"][l]
        common[f"wout2_{l}"] = inp["ffn2_w_out"][l]
        common[f"wmo_{l}"] = inp["mix_w_out"][l]
    per_role = {0: {}, 1: {}}
    for l in range(depth):
        L = {k: inp[k][l] for k in MIX_KEYS}
        for role in (0, 1):
            mi = mixer_inputs(cfg, L, role, None, None)
            for k in ("wmix", "p128", "wsT", "bsb", "p64"):
                per_role[role][f"{k}_{l}"] = mi[k]
    maps = []
    for c in cores:
        role = c % 2
        m = dict(common)
        m.update(per_role[role])
        m["xin"] = np.ascontiguousarray(x[c // 2, role * NT:(role + 1) * NT, :].T)
        rmk = np.zeros((128, 2), np.float32)
        rmk[:, 0] = 1 - role
        rmk[:, 1] = role
        m["rolemask"] = rmk
        maps.append(m)
    res = run_bass_kernel_spmd(nc, maps, core_ids=cores)
    out = np.empty((B, SEQ, D), np.float32)
    for c in cores:
        out[c // 2, (c % 2) * NT:(c % 2 + 1) * NT, :] = res.results[c]["out"].T
    return out


def kernel(**inputs):
    inp = {k: np.asarray(v) for k, v in inputs.items()}
    cfg = Cfg()
    return run_fused(cfg, inp)
```

```python
import numpy as np
from contextlib import ExitStack
import concourse.bass as bass
import concourse.mybir as mybir
from concourse.bass_utils import run_bass_kernel_spmd

F32 = mybir.dt.float32
BF16 = mybir.dt.bfloat16
AF = mybir.ActivationFunctionType
ALU = mybir.AluOpType
AX = mybir.AxisListType

SEM_LIMIT = 30000
N_DMA_SEMS = 12


class Buf:
    __slots__ = ("name", "w", "r")

    def __init__(self, name=""):
        self.name = name
        self.w = None
        self.r = []


class PB:
    ENG = ("pe", "act", "dve", "pool", "sp")

    def __init__(self, nc):
        self.nc = nc
        self.es = ExitStack()
        self.eng = {"pe": nc.tensor, "act": nc.scalar, "dve": nc.vector, "pool": nc.gpsimd, "sp": nc.sync}
        self.q = {e: [] for e in self.ENG}
        self.sem = {}
        self.cnt = {}
        self.nsem = 0
        for e in ("pe", "act", "dve", "pool"):
            self._new_sem(e)
        self.seen = {e: {} for e in self.ENG}
        self.dsem = {}
        for qn in ("sp", "act", "pool"):
            self.dsem[qn] = [[self._alloc_sem(f"d{qn}{i}"), 0, None] for i in range(N_DMA_SEMS)]
        self.dnext = {qn: 0 for qn in self.dsem}
        self.pe_pend_r = []
        self.pe_pend_w = []
        self.n_inst = 0

    def _alloc_sem(self, name):
        self.nsem += 1
        return self.es.enter_context(self.nc.semaphore(f"{name}_{self.nsem}"))

    def _new_sem(self, e):
        self.sem[e] = self._alloc_sem(f"s{e}")
        self.cnt[e] = 0

    def sbuf(self, name, shape, dt):
        return self.es.enter_context(self.nc.sbuf_tensor(name, list(shape), dt))

    def psum(self, name, shape, dt):
        return self.es.enter_context(self.nc.psum_tensor(name, list(shape), dt))

    def _need(self, e, reads, writes):
        toks = []
        for b in reads:
            if b.w is not None:
                toks.append(b.w)
        for b in writes:
            if b.w is not None:
                toks.append(b.w)
            toks.extend(b.r)
        best = {}
        for (s, v) in toks:
            k = id(s)
            if k not in best or best[k][1] < v:
                best[k] = (s, v)
        out = []
        for k, (s, v) in best.items():
            if e == "pe" and s is self.sem["pe"]:
                continue
            if self.seen[e].get(k, 0) >= v:
                continue
            self.seen[e][k] = v
            out.append((s, v))
        return out

    def _emit_waits(self, e, waits):
        for (s, v) in waits:
            self.q[e].append(lambda E, s=s, v=v: E.wait_ge(s, v))

    def _commit(self, tok, reads, writes):
        for b in reads:
            b.r.append(tok)
        for b in writes:
            b.w = tok
            b.r = []

    def op(self, e, fn, reads=(), writes=()):
        waits = self._need(e, reads, writes)
        self._emit_waits(e, waits)
        if self.cnt[e] >= SEM_LIMIT:
            self._new_sem(e)
        self.cnt[e] += 1
        s = self.sem[e]
        self.q[e].append(lambda E, fn=fn, s=s: fn(E).then_inc(s, 1))
        tok = (s, self.cnt[e])
        self._commit(tok, reads, writes)
        self.n_inst += 1
        return tok

    def mm(self, fn, reads=(), writes=(), last=True):
        e = "pe"
        waits = self._need(e, reads, writes)
        self._emit_waits(e, waits)
        self.n_inst += 1
        if not last:
            self.q[e].append(lambda E, fn=fn: fn(E))
            self.pe_pend_r.extend(reads)
            self.pe_pend_w.extend(writes)
            return None
        if self.cnt[e] >= SEM_LIMIT:
            self._new_sem(e)
        self.cnt[e] += 1
        s = self.sem[e]
        self.q[e].append(lambda E, fn=fn, s=s: fn(E).then_inc(s, 1))
        tok = (s, self.cnt[e])
        rs = {id(b): b for b in list(self.pe_pend_r) + list(reads)}
        ws = {id(b): b for b in list(self.pe_pend_w) + list(writes)}
        self.pe_pend_r = []
        self.pe_pend_w = []
        self._commit(tok, [b for k, b in rs.items() if k not in ws], list(ws.values()))
        return tok

    def barrier(self):
        toks = [(self.sem[e], self.cnt[e]) for e in ("pe", "act", "dve", "pool") if self.cnt[e] > 0]
        for qn, slots in self.dsem.items():
            for (s, n, prev) in slots:
                if prev is not None:
                    toks.append(prev)
        if getattr(self, "ccnt", 0) > 0:
            toks.append((self.csem, self.ccnt))
        for e in self.ENG:
            for (s, v) in toks:
                k = id(s)
                if self.seen[e].get(k, 0) >= v:
                    continue
                if e == "pe" and s is self.sem["pe"]:
                    continue
                self.seen[e][k] = v
                self.q[e].append(lambda E, s=s, v=v: E.wait_ge(s, v))

    def dma(self, qn, out, in_, reads=(), writes=(), **kw):
        slots = self.dsem[qn]
        i = self.dnext[qn]
        self.dnext[qn] = (i + 1) % len(slots)
        slot = slots[i]
        s, n, prev = slot
        waits = self._need(qn, reads, writes)
        if prev is not None:
            k = id(prev[0])
            if self.seen[qn].get(k, 0) < prev[1]:
                self.seen[qn][k] = prev[1]
                waits.append(prev)
        self._emit_waits(qn, waits)
        n += 1
        tok = (s, 16 * n)
        slot[1] = n
        slot[2] = tok
        self.q[qn].append(lambda E, out=out, in_=in_, s=s, kw=kw: E.dma_start(out=out, in_=in_, **kw).then_inc(s, 16))
        self._commit(tok, reads, writes)
        self.n_inst += 1
        return tok

    def wait_all(self, e, toks):
        for (s, v) in toks:
            self.q[e].append(lambda E, s=s, v=v: E.wait_ge(s, v))

    def finish(self, final_toks):
        self.wait_all("sp", final_toks)
        nc = self.nc
        q = self.q
        with nc.Block() as block:
            @block.sync
            def _(E):
                for f in q["sp"]:
                    f(E)

            @block.scalar
            def _(E):
                for f in q["act"]:
                    f(E)

            @block.vector
            def _(E):
                for f in q["dve"]:
                    f(E)

            @block.gpsimd
            def _(E):
                for f in q["pool"]:
                    f(E)

            @block.tensor
            def _(E):
                for f in q["pe"]:
                    f(E)
        self.es.close()


def pb_collective(self, kind, ins, outs, groups, reads=(), writes=()):
    qn = "pool"
    if not hasattr(self, "csem"):
        self.csem = self._alloc_sem("cc")
        self.ccnt = 0
    waits = self._need(qn, reads, writes)
    self._emit_waits(qn, waits)
    self.ccnt += 1
    s = self.csem
    tok = (s, self.ccnt)
    self.q[qn].append(lambda E, s=s: E.collective_compute(kind, ALU.bypass, replica_groups=groups, ins=ins, outs=outs).then_inc(s))
    self._commit(tok, reads, writes)
    return tok


PB.collective = pb_collective


EPS = 1e-6


class Ctx:
    def __init__(self, P, NT):
        self.P = P
        self.NT = NT
        self.ps = [P.psum(f"ps{i}", [128, 512], F32) for i in range(8)]
        self.psb = [Buf(f"ps{i}") for i in range(8)]
        self.ones = P.sbuf("ones_f", [128, 128], F32)
        self.b_ones = Buf("ones")
        P.op("dve", lambda E: E.memset(self.ones[:], 1.0), writes=[self.b_ones])
        self.NW = 4
        self.wb = [P.sbuf(f"wb{i}", [128, 4, 512], BF16) for i in range(self.NW)]
        self.wbb = [Buf(f"wb{i}") for i in range(self.NW)]
        self.wi = 0

    def next_w(self):
        i = self.wi
        self.wi = (i + 1) % self.NW
        return self.wb[i], self.wbb[i]


def split_chunks(n, nch):
    base, rem = divmod(n, nch)
    out = []
    s = 0
    for i in range(nch):
        c = base + (1 if i < rem else 0)
        out.append((s, c))
        s += c
    return out


def rmsnorm(C, xT, g_sb, D, xn_sb=None, b_xn=None, xres=None, out_f32=None, tmp=None):
    P = C.P
    NT = C.NT
    KT = D // 128
    Q = 256
    xf, b_xf, sq, b_sq, rs, b_rs, rstd, b_rstd = tmp["xf"], tmp["b_xf"], tmp["sq"], tmp["b_sq"], tmp["rs"], tmp["b_rs"], tmp["rstd"], tmp["b_rstd"]
    toks = []
    for q in range(NT // Q):
        c0 = q * Q
        h = c0 // 512
        for k in range(KT):
            rd = [xres[(k, h)]] if xres is not None else []
            P.dma("sp", xf[:, k, :], xT[k * 128:(k + 1) * 128, c0:c0 + Q], reads=rd, writes=[b_xf[k]])
        bank = q % 2
        for k in range(KT):
            P.op("act", lambda E, k=k, i=k % 2: E.activation(out=sq[i][:], in_=xf[:, k, :], func=AF.Square),
                 reads=[b_xf[k]], writes=[b_sq[k % 2]])
            P.mm(lambda E, k=k, i=k % 2, bank=bank: E.matmul(C.ps[bank][:, 0:Q], lhsT=C.ones[:], rhs=sq[i][:], start=(k == 0), stop=(k == KT - 1)),
                 reads=[b_sq[k % 2], C.b_ones], writes=[C.psb[bank]], last=True)
        P.op("act", lambda E, bank=bank: E.activation(out=rs[:], in_=C.ps[bank][:, 0:Q], func=AF.Sqrt, scale=1.0 / D, bias=tmp["eps"][:]),
             reads=[C.psb[bank], tmp["b_eps"]], writes=[b_rs])
        P.op("dve", lambda E: E.reciprocal(out=rstd[:], in_=rs[:]), reads=[b_rs], writes=[b_rstd])
        for k in range(KT):
            if xn_sb is not None:
                P.op("dve", lambda E, k=k, c0=c0: E.scalar_tensor_tensor(out=xn_sb[:, k, c0:c0 + Q], in0=xf[:, k, :], scalar=g_sb[:, k:k + 1], in1=rstd[:], op0=ALU.mult, op1=ALU.mult),
                     reads=[b_xf[k], b_rstd, tmp["b_g"]], writes=[b_xn])
            else:
                P.op("dve", lambda E, k=k: E.scalar_tensor_tensor(out=xf[:, k, :], in0=xf[:, k, :], scalar=g_sb[:, k:k + 1], in1=rstd[:], op0=ALU.mult, op1=ALU.mult),
                     reads=[b_xf[k], b_rstd, tmp["b_g"]], writes=[b_xf[k]])
                toks.append(P.dma("sp", out_f32[k * 128:(k + 1) * 128, c0:c0 + Q], xf[:, k, :], reads=[b_xf[k]], writes=[Buf()]))
    return toks


def norm_tmp(P, AR, KT):
    t = {}
    t["xf"] = AR.alloc((KT, 256), F32)
    t["b_xf"] = [Buf() for _ in range(KT)]
    t["sq"] = [AR.alloc((256,), F32) for i in range(2)]
    t["b_sq"] = [Buf(), Buf()]
    t["rs"] = AR.alloc((256,), F32)
    t["b_rs"] = Buf()
    t["rstd"] = AR.alloc((256,), F32)
    t["b_rstd"] = Buf()
    t["eps"] = AR.alloc((1,), F32)
    t["b_eps"] = Buf()
    P.op("dve", lambda E: E.memset(t["eps"][:, :], EPS), writes=[t["b_eps"]])
    t["b_g"] = Buf()
    return t


def proj_rmw(C, act_sb, b_act, kt0, nk, W, wrow0, D, xT, xres, scale, rt):
    P = C.P
    NT = C.NT
    H = NT // 512
    NG = D // 512 if D >= 512 else 1
    JW = min(4, D // 128)
    assert JW * H <= 8
    for g in range(NG):
        for j in range(JW):
            for h in range(H):
                f = g * JW + j
                xo, b_xo = rt["xold"][j * H + h], rt["b_xold"][j * H + h]
                P.dma("sp", xo[:], xT[f * 128:(f + 1) * 128, h * 512:(h + 1) * 512], reads=[xres[(f, h)]], writes=[b_xo])
        kb = 0
        while kb < nk:
            nkk = min(4, nk - kb)
            wb, bwb = C.next_w()
            r0 = wrow0 + kb * 128
            P.dma("pool", wb[:, 0:nkk, 0:JW * 128], W[r0:r0 + nkk * 128, g * JW * 128:(g + 1) * JW * 128].rearrange("(kk p) n -> p kk n", p=128), writes=[bwb])
            for kk in range(nkk):
                k = kb + kk
                for j in range(JW):
                    for h in range(H):
                        lastk = (kk == nkk - 1 and j == JW - 1 and h == H - 1)
                        P.mm(lambda E, wb=wb, kk=kk, j=j, h=h, k=k: E.matmul(C.ps[j * H + h][:, :], lhsT=wb[:, kk, j * 128:(j + 1) * 128], rhs=act_sb[:, kt0 + k, h * 512:(h + 1) * 512], start=(k == 0), stop=(k == nk - 1)),
                             reads=[bwb, b_act], writes=[C.psb[j * H + h]], last=lastk)
            kb += nkk
        for j in range(JW):
            for h in range(H):
                f = g * JW + j
                i = j * H + h
                xo, b_xo = rt["xold"][i], rt["b_xold"][i]
                P.op("dve", lambda E, i=i, xo=xo: E.scalar_tensor_tensor(out=xo[:], in0=C.ps[i][:, :], scalar=float(scale), in1=xo[:], op0=ALU.mult, op1=ALU.add),
                     reads=[C.psb[i], b_xo], writes=[b_xo])
                P.dma("sp", xT[f * 128:(f + 1) * 128, h * 512:(h + 1) * 512], xo[:], reads=[b_xo], writes=[xres[(f, h)]])


def rmw_tmp(P, AR):
    rt = {}
    rt["xold"] = [AR.alloc((512,), F32) for i in range(8)]
    rt["b_xold"] = [Buf() for _ in range(8)]
    return rt


def ffn(C, xn_sb, b_xn, W_in, W_out, D, DFF, xT, xres, rt, gT, b_gT, stmp, b_stmp, nch=4):
    P = C.P
    NT = C.NT
    H = NT // 512
    KT = D // 128
    FT = DFF // 128
    Win4 = W_in.rearrange("(kt p) (two f) -> p kt two f", p=128, two=2)
    for (c0, cn) in split_chunks(FT, nch):
        t = 0
        while t < cn:
            gs = min(2, cn - t)
            n0 = (c0 + t) * 128
            for k4 in range(0, KT, 4):
                nkk = min(4, KT - k4)
                wb, bwb = C.next_w()
                wv = wb[:].rearrange("p k (two f) -> p k two f", two=2)
                for gu in range(2):
                    P.dma("pool", wv[:, 0:nkk, gu, 0:gs * 128], Win4[:, k4:k4 + nkk, gu, n0:n0 + gs * 128], writes=[bwb])
                for kk in range(nkk):
                    k = k4 + kk
                    for j in range(gs):
                        for gu in range(2):
                            for h in range(H):
                                bank = gu * 4 + j * H + h
                                lastk = (kk == nkk - 1 and j == gs - 1 and gu == 1 and h == H - 1)
                                P.mm(lambda E, wv=wv, kk=kk, gu=gu, j=j, h=h, k=k, bank=bank: E.matmul(C.ps[bank][:, :], lhsT=wv[:, kk, gu, j * 128:(j + 1) * 128], rhs=xn_sb[:, k, h * 512:(h + 1) * 512], start=(k == 0), stop=(k == KT - 1)),
                                     reads=[bwb, b_xn], writes=[C.psb[bank]], last=lastk)
            for j in range(gs):
                for h in range(H):
                    bg = j * H + h
                    bu = 4 + j * H + h
                    si = (j * H + h) % 2
                    P.op("act", lambda E, bg=bg, si=si: E.activation(out=stmp[si][:], in_=C.ps[bg][:, :], func=AF.Silu),
                         reads=[C.psb[bg]], writes=[b_stmp[si]])
                    P.op("dve", lambda E, bu=bu, si=si, tt=t + j, h=h: E.tensor_tensor(out=gT[:, tt, h * 512:(h + 1) * 512], in0=stmp[si][:], in1=C.ps[bu][:, :], op=ALU.mult),
                         reads=[b_stmp[si], C.psb[bu]], writes=[b_gT])
            t += gs
        proj_rmw(C, gT, b_gT, 0, cn, W_out, c0 * 128, D, xT, xres, 0.5, rt)

import math


class Arena:
    def __init__(self, P, name, nbytes):
        self.n = nbytes // 4
        self.t = P.sbuf(name, [128, self.n], F32)
        self.off = 0
        self.peak = 0

    def reset(self, off=0):
        self.off = off

    def alloc(self, shape, dt):
        sz = 1
        for s in shape:
            sz *= s
        words = sz if dt == F32 else (sz + 1) // 2
        a = self.t[:, self.off:self.off + words]
        self.off += words
        self.peak = max(self.peak, self.off)
        assert self.off <= self.n, ("arena overflow", self.off, self.n)
        if dt != F32:
            a = a.bitcast(dt)
            if sz % 2:
                a = a[:, 0:sz]
        if len(shape) == 1:
            return a
        if len(shape) == 2:
            return a.rearrange("p (a b) -> p a b", a=shape[0])
        if len(shape) == 3:
            return a.rearrange("p (a b c) -> p a b c", a=shape[0], b=shape[1])
        raise ValueError


class DV:
    def __init__(self, ap=None, fn=None, fn_group=None, kg=None):
        self.ap = ap
        self.fn = fn
        self.fn_group = fn_group
        self.kg = kg

    def tile(self, k, c0, n):
        if self.fn is not None:
            return self.fn(k, c0, n)
        return self.ap[k * 128:(k + 1) * 128, c0:c0 + n]

    def group(self, k0, nk, c0, n):
        if self.fn_group is not None:
            return self.fn_group(k0, nk, c0, n)
        return self.ap[k0 * 128:(k0 + nk) * 128, c0:c0 + n].rearrange("(k p) n -> p k n", p=128)


def load_xn(P, xn, b_xn, X, KT, c0, n):
    kg = X.kg if X.kg is not None else KT
    if X.fn is not None and X.fn_group is None:
        for k in range(KT):
            P.dma("sp", xn[:, k, 0:n], X.tile(k, c0, n), writes=[b_xn])
        return
    for k0 in range(0, KT, kg):
        nk = min(kg, KT - k0)
        P.dma("sp", xn[:, k0:k0 + nk, 0:n], X.group(k0, nk, c0, n), writes=[b_xn])


class RV:
    def __init__(self, ap=None, fn=None):
        self.ap = ap
        self.fn = fn

    def rows(self, r0, nr, c0, n):
        if self.fn is not None:
            return self.fn(r0, nr, c0, n)
        return self.ap[r0:r0 + nr, c0:c0 + n]


class ArenaView:
    def __init__(self, ar, dt):
        self.ar = ar
        self.dt = dt

    def alloc(self, *shape):
        return self.ar.alloc(shape, self.dt)


def proj_blocks(C, xn, b_xn, chunks, W, KT, segs, blocks, evac, bank0=None):
    P = C.P
    nb = len(blocks) * len(chunks)
    if bank0 is None:
        if nb <= 4:
            C.pbt = 4 - getattr(C, "pbt", 4)
            bank0 = C.pbt
        else:
            bank0 = 0
    assert bank0 + nb <= 8
    soff = []
    o = 0
    for (c0_, nc_) in segs:
        soff.append(o)
        o += nc_
    assert o <= 512
    for k4 in range(0, KT, 4):
        nkk = min(4, KT - k4)
        wb, bwb = C.next_w()
        for si, (col0, ncols) in enumerate(segs):
            P.dma("pool", wb[:, 0:nkk, soff[si]:soff[si] + ncols],
                  W[k4 * 128:(k4 + nkk) * 128, col0:col0 + ncols].rearrange("(kk p) n -> p kk n", p=128), writes=[bwb])
        for kk in range(nkk):
            k = k4 + kk
            for bi, (si, off, M) in enumerate(blocks):
                for ci, (c0, n) in enumerate(chunks):
                    bank = bank0 + bi * len(chunks) + ci
                    lastk = (kk == nkk - 1 and bi == len(blocks) - 1 and ci == len(chunks) - 1)
                    P.mm(lambda E, wb=wb, kk=kk, a=soff[si] + off, M=M, c0=c0, n=n, k=k, bank=bank:
                         E.matmul(C.ps[bank][0:M, 0:n], lhsT=wb[:, kk, a:a + M], rhs=xn[:, k, c0:c0 + n], start=(k == 0), stop=(k == KT - 1)),
                         reads=[bwb, b_xn], writes=[C.psb[bank]], last=lastk)
    for bi, (si, off, M) in enumerate(blocks):
        for ci, (c0, n) in enumerate(chunks):
            bank = bank0 + bi * len(chunks) + ci
            evac(bi, ci, C.ps[bank][0:M, 0:n], C.psb[bank])


def layernorm_fm(C, x, b_x, nt, n, ncol, gcol, bcol, b_par, eps, out_fn, tmp):
    P = C.P
    CH = nt * 128
    bs, bq = 0, 1
    bxl = b_x if isinstance(b_x, list) else [b_x] * nt
    for k in range(nt):
        P.mm(lambda E, k=k: E.matmul(C.ps[bs][:, 0:n], lhsT=C.ones[:], rhs=x[:, k, ncol:ncol + n], start=(k == 0), stop=(k == nt - 1)),
             reads=[bxl[k], C.b_ones], writes=[C.psb[bs]], last=(k == nt - 1))
    for k in range(nt):
        i = k % 2
        P.op("act", lambda E, k=k, i=i: E.activation(out=tmp["sq"][i][:, 0:n], in_=x[:, k, ncol:ncol + n], func=AF.Square),
             reads=[bxl[k]], writes=[tmp["b_sq"][i]])
        P.mm(lambda E, k=k, i=i: E.matmul(C.ps[bq][:, 0:n], lhsT=C.ones[:], rhs=tmp["sq"][i][:, 0:n], start=(k == 0), stop=(k == nt - 1)),
             reads=[tmp["b_sq"][i], C.b_ones], writes=[C.psb[bq]], last=True)
    mean, rstd = tmp["mean"], tmp["rstd"]
    P.op("act", lambda E: E.activation(out=mean[:, 0:n], in_=C.ps[bs][:, 0:n], func=AF.Copy, scale=1.0 / CH), reads=[C.psb[bs]], writes=[tmp["b_mean"]])
    P.op("dve", lambda E: E.tensor_tensor(out=rstd[:, 0:n], in0=mean[:, 0:n], in1=mean[:, 0:n], op=ALU.mult), reads=[tmp["b_mean"]], writes=[tmp["b_rstd"]])
    P.op("dve", lambda E: E.scalar_tensor_tensor(out=rstd[:, 0:n], in0=C.ps[bq][:, 0:n], scalar=1.0 / CH, in1=rstd[:, 0:n], op0=ALU.mult, op1=ALU.subtract),
         reads=[C.psb[bq], tmp["b_rstd"]], writes=[tmp["b_rstd"]])
    P.op("dve", lambda E: E.tensor_scalar(out=rstd[:, 0:n], in0=rstd[:, 0:n], scalar1=0.0, scalar2=float(eps), op0=ALU.max, op1=ALU.add), reads=[tmp["b_rstd"]], writes=[tmp["b_rstd"]])
    P.op("act", lambda E: E.activation(out=rstd[:, 0:n], in_=rstd[:, 0:n], func=AF.Sqrt), reads=[tmp["b_rstd"]], writes=[tmp["b_rstd"]])
    P.op("dve", lambda E: E.reciprocal(out=rstd[:, 0:n], in_=rstd[:, 0:n]), reads=[tmp["b_rstd"]], writes=[tmp["b_rstd"]])
    for k in range(nt):
        i = k % 2
        t = tmp["t"][i]
        P.op("dve", lambda E, k=k, t=t: E.tensor_tensor(out=t[:, 0:n], in0=x[:, k, ncol:ncol + n], in1=mean[:, 0:n], op=ALU.subtract),
             reads=[bxl[k], tmp["b_mean"]], writes=[tmp["b_t"][i]])
        P.op("dve", lambda E, t=t: E.tensor_tensor(out=t[:, 0:n], in0=t[:, 0:n], in1=rstd[:, 0:n], op=ALU.mult),
             reads=[tmp["b_rstd"]], writes=[tmp["b_t"][i]])
        out_fn(k, t[:, 0:n], tmp["b_t"][i])


def ln_tmp(AR):
    t = {}
    t["sq"] = [AR.alloc(512), AR.alloc(512)]
    t["b_sq"] = [Buf(), Buf()]
    t["mean"] = AR.alloc(512)
    t["rstd"] = AR.alloc(512)
    t["b_mean"] = Buf()
    t["b_rstd"] = Buf()
    t["t"] = [AR.alloc(512), AR.alloc(512)]
    t["b_t"] = [Buf(), Buf()]
    return t


def conv_mixer(C, AF32, ABF, XNO, W, col_a, col_g, KT, NT, CT, par, YOUT, yrow0, K=31, halo_mask=None, b_hm=None):
    P = C.P
    HALO = 32
    NTT = HALO + NT
    xn = ABF.alloc(KT, 512)
    b_xn = Buf()
    yglu = AF32.alloc(CT, NTT)
    b_yglu = Buf()
    sg = [AF32.alloc(512), AF32.alloc(512)]
    b_sg = [Buf(), Buf()]
    chunks_all = [(0, HALO)] + [(HALO + i * 512, 512) for i in range(NT // 512)]
    for (c0, n) in chunks_all:
        load_xn(P, xn, b_xn, XNO, KT, c0, n)
        if halo_mask is not None and c0 == 0:
            P.op("dve", lambda E: E.tensor_scalar(out=xn[:, :, 0:HALO], in0=xn[:, :, 0:HALO], scalar1=halo_mask, scalar2=None, op0=ALU.mult), reads=[b_hm], writes=[b_xn])
        for g in range(CT // 2):
            def evac(bi, ci, ps, psb, g=g, c0=c0, n=n):
                pass
            res = {}

            def evac2(bi, ci, ps, psb, res=res, g=g, c0=c0, n=n):
                res[bi] = (ps, psb)
                if bi == 3:
                    for j in range(2):
                        pa, ba = res[j]
                        pg, bg = res[2 + j]
                        i = j % 2
                        P.op("act", lambda E, pg=pg, i=i, n=n: E.activation(out=sg[i][:, 0:n], in_=pg, func=AF.Sigmoid), reads=[bg], writes=[b_sg[i]])
                        P.op("dve", lambda E, pa=pa, i=i, n=n, ct=g * 2 + j, c0=c0: E.tensor_tensor(out=yglu[:, ct, c0:c0 + n], in0=sg[i][:, 0:n], in1=pa, op=ALU.mult),
                             reads=[b_sg[i], ba], writes=[b_yglu])
            proj_blocks(C, xn, b_xn, [(0, n)], W, KT, [(col_a + g * 256, 256), (col_g + g * 256, 256)],
                        [(0, 0, 128), (0, 128, 128), (1, 0, 128), (1, 128, 128)], evac2)
    cv = AF32.alloc(CT, NT)
    b_cv = [Buf() for _ in range(CT)]
    cw, cb = par["cw"], par["cb"]
    for ct in range(CT):
        e = "dve"
        bct = b_cv[ct]
        off = HALO - (K - 1)
        P.op(e, lambda E, ct=ct: E.tensor_scalar(out=cv[:, ct, :], in0=yglu[:, ct, off:off + NT], scalar1=cw[:, ct, 0:1], scalar2=cb[:, ct:ct + 1], op0=ALU.mult, op1=ALU.add),
             reads=[b_yglu, par["b"]], writes=[bct])
        for j in range(1, K):
            P.op(e, lambda E, ct=ct, j=j: E.scalar_tensor_tensor(out=cv[:, ct, :], in0=yglu[:, ct, off + j:off + j + NT], scalar=cw[:, ct, j:j + 1], in1=cv[:, ct, :], op0=ALU.mult, op1=ALU.add),
                 reads=[b_yglu, par["b"]], writes=[bct])
    tmp = ln_tmp(AF32)
    yo = [ABF.alloc(512), ABF.alloc(512)]
    b_yo = [Buf(), Buf()]
    toks = []
    for hh in range(NT // 512):
        def out_fn(k, xh, b_xh, hh=hh):
            i = k % 2
            P.op("act", lambda E, k=k, xh=xh, i=i: E.activation(out=yo[i][:], in_=xh, func=AF.Silu, scale=par["lg"][:, k:k + 1], bias=par["lb"][:, k:k + 1]),
                 reads=[b_xh, par["b"]], writes=[b_yo[i]])
            toks.append(P.dma("sp", YOUT[yrow0 + k * 128:yrow0 + (k + 1) * 128, hh * 512:(hh + 1) * 512], yo[i][:], reads=[b_yo[i]], writes=[Buf()]))
        layernorm_fm(C, cv, b_cv, CT, 512, hh * 512, None, None, None, EPS, out_fn, tmp)
    return toks


def gelu_tanh(P, eng_v, out_ap, in_ps, b_in, n, t1, b_t1, t2, b_t2, writes):
    c2 = 2.0 * math.sqrt(2.0 / math.pi)
    P.op("act", lambda E: E.activation(out=t1, in_=in_ps, func=AF.Copy), reads=[b_in], writes=[b_t1])
    P.op("act", lambda E: E.activation(out=t2, in_=in_ps, func=AF.Square), reads=[b_in], writes=[b_t2])
    P.op(eng_v, lambda E: E.tensor_scalar(out=t2, in0=t2, scalar1=0.044715, scalar2=1.0, op0=ALU.mult, op1=ALU.add), reads=[b_t2], writes=[b_t2])
    P.op(eng_v, lambda E: E.tensor_tensor(out=t2, in0=t2, in1=t1, op=ALU.mult), reads=[b_t1, b_t2], writes=[b_t2])
    P.op("act", lambda E: E.activation(out=t2, in_=t2, func=AF.Sigmoid, scale=c2), reads=[b_t2], writes=[b_t2])
    P.op(eng_v, lambda E: E.tensor_tensor(out=out_ap, in0=t2, in1=t1, op=ALU.mult), reads=[b_t1, b_t2], writes=writes)


def sgu_mixer(C, AF32, ABF, XNO, W, col_u, col_v, KT, NT, CT, par, YOUT, yrow0):
    P = C.P
    HALO = 32
    xn = ABF.alloc(KT, 512)
    b_xn = Buf()
    u = AF32.alloc(CT, NT)
    b_u = Buf()
    v = AF32.alloc(CT, NT)
    b_v = Buf()
    t1 = [AF32.alloc(512), AF32.alloc(512)]
    t2 = [AF32.alloc(512), AF32.alloc(512)]
    b_t1 = [Buf(), Buf()]
    b_t2 = [Buf(), Buf()]
    for hh in range(NT // 512):
        c0 = HALO + hh * 512
        load_xn(P, xn, b_xn, XNO, KT, c0, 512)
        for (col, dst, b_dst) in ((col_u, u, b_u), (col_v, v, b_v)):
            gsz = min(4, CT)
            for g in range(CT // gsz):
                def evac(bi, ci, ps, psb, g=g, dst=dst, b_dst=b_dst, hh=hh):
                    i = bi % 2
                    gelu_tanh(P, "dve", dst[:, g * gsz + bi, hh * 512:(hh + 1) * 512], ps, psb, 512, t1[i][:], b_t1[i], t2[i][:], b_t2[i], [b_dst])
                proj_blocks(C, xn, b_xn, [(0, 512)], W, KT, [(col + g * gsz * 128, gsz * 128)], [(0, j * 128, 128) for j in range(gsz)], evac)
    tmp = ln_tmp(AF32)
    vn = v
    b_vn = b_v
    for hh in range(NT // 512):
        def out_fn(k, xh, b_xh, hh=hh):
            P.op("act", lambda E, k=k, xh=xh: E.activation(out=vn[:, k, hh * 512:(hh + 1) * 512], in_=xh, func=AF.Identity, scale=par["lg"][:, k:k + 1], bias=par["lb"][:, k:k + 1]),
                 reads=[b_xh, par["b"]], writes=[b_vn])
        layernorm_fm(C, v, b_v, CT, 512, hh * 512, None, None, None, EPS, out_fn, tmp)
    vt = [ABF.alloc(128), ABF.alloc(128)]
    b_vt = [Buf(), Buf()]
    yo = [ABF.alloc(128), ABF.alloc(128)]
    b_yo = [Buf(), Buf()]
    sv = [AF32.alloc(128), AF32.alloc(128)]
    b_sv = [Buf(), Buf()]
    toks = []
    it = 0
    for h in range(CT):
        for nb in range(NT // 128):
            i = it % 2
            it += 1
            tb = 4 + i
            P.mm(lambda E, h=h, nb=nb, tb=tb: E.matmul(C.ps[tb][:, 0:128], lhsT=vn[:, h, nb * 128:(nb + 1) * 128], rhs=C.ident[:], start=True, stop=True),
                 reads=[b_vn, C.b_ident], writes=[C.psb[tb]], last=True)
            P.op("act", lambda E, i=i, tb=tb: E.activation(out=vt[i][:], in_=C.ps[tb][:, 0:128], func=AF.Copy), reads=[C.psb[tb]], writes=[b_vt[i]])
            bank = 2 + i
            P.mm(lambda E, h=h, i=i, bank=bank: E.matmul(C.ps[bank][:, 0:128], lhsT=vt[i][:], rhs=par["wsT"][:, h, :], start=True, stop=True),
                 reads=[b_vt[i], par["b"]], writes=[C.psb[bank]], last=True)
            P.op("dve", lambda E, h=h, i=i, bank=bank: E.tensor_tensor(out=sv[i][:], in0=C.ps[bank][:, 0:128], in1=par["bsb"][:, h, :], op=ALU.add),
                 reads=[C.psb[bank], par["b"]], writes=[b_sv[i]])
            P.op("dve", lambda E, h=h, nb=nb, i=i: E.tensor_tensor(out=yo[i][:], in0=sv[i][:], in1=u[:, h, nb * 128:(nb + 1) * 128], op=ALU.mult),
                 reads=[b_sv[i], b_u], writes=[b_yo[i]])
            toks.append(P.dma("sp", YOUT[yrow0 + h * 128:yrow0 + (h + 1) * 128, nb * 128:(nb + 1) * 128], yo[i][:], reads=[b_yo[i]], writes=[Buf()]))
    return toks


def rwkv_mixer(C, AF32, ABF, XN, W, cols, KT, S, NH, par, cst, YOUT, yrow0, TR=256, DBG=None):
    P = C.P
    L = 64
    NCH = TR // L
    NT_ = S // TR
    HW = NH * 64
    ones64 = C.ones[0:64, 0:64]
    id64 = C.ident[0:64, 0:64]
    xn = ABF.alloc(KT, TR)
    b_xn = Buf()
    R = AF32.alloc(NH, TR); Kb = AF32.alloc(NH, TR); V = AF32.alloc(NH, TR)
    b_R, b_K, b_V = Buf(), Buf(), Buf()
    XL = AF32.alloc(5, TR); b_XL = Buf()
    LW = AF32.alloc(NH, TR); b_LW = Buf()
    CS = [AF32.alloc(NH, TR), AF32.alloc(NH, TR)]; b_CS = [Buf(), Buf()]
    A = AF32.alloc(NH, TR); b_A = Buf()
    G = AF32.alloc(NH, TR); b_G = Buf()
    KK = AF32.alloc(NH, TR); b_KK = Buf()
    BON = AF32.alloc(NH, TR); b_BON = Buf()
    PX = AF32.alloc(NH, TR); b_PX = Buf()
    YB, b_YB = CS[1], b_CS[1]
    PL = AF32.alloc(NH, NCH); b_PL = Buf()
    carry = AF32.alloc(3 * NH + 5); b_carry = Buf()
    ST = [AF32.alloc(NH, 64), AF32.alloc(NH, 64)]; b_ST = [Buf(), Buf()]
    t64 = [AF32.alloc(TR), AF32.alloc(TR)]; b_t64 = [Buf(), Buf()]
    def cb():
        return [ABF.alloc(NH, 64), ABF.alloc(NH, 64)], [Buf(), Buf()]
    def cb1():
        a = ABF.alloc(NH, 64)
        b = Buf()
        return [a, a], [b, b]
    VT, b_VT = cb(); KtT, b_KtT = cb(); BtT, b_BtT = cb()
    M1, b_M1 = cb(); N1, b_N1 = cb(); N2, b_N2 = cb()
    Xm, b_X = cb(); Xt, b_Xt = cb1()
    Aa, b_Aa = cb1(); At, b_At = cb1()
    RH, b_RH = cb1(); NU, b_NU = cb1()
    tmpS = AF32.alloc(NH, 64); b_tmpS = Buf()
    Rb = ABF.alloc(NH, TR); Kbb = ABF.alloc(NH, TR); Ab = ABF.alloc(NH, TR); KKb = ABF.alloc(NH, TR); Vb = ABF.alloc(NH, TR)
    b_Rb, b_Kbb, b_Ab, b_KKb, b_Vb = Buf(), Buf(), Buf(), Buf(), Buf()
    STb = [ABF.alloc(NH, 64), ABF.alloc(NH, 64)]; b_STb = [Buf(), Buf()]
    idb = ABF.alloc(64); b_idb = Buf()
    P.op("dve", lambda E: E.tensor_copy(out=idb[0:64, :], in_=C.ident[0:64, 0:64]), reads=[C.b_ident], writes=[b_idb])
    id64b = idb[0:64, :]
    yo = [ABF.alloc(TR), ABF.alloc(TR)]; b_yo = [Buf(), Buf()]
    P.op("dve", lambda E: E.memset(carry[:, :], 0.0), writes=[b_carry])
    P.op("dve", lambda E: E.memset(ST[0][:, :, :], 0.0), writes=[b_ST[0]])
    P.op("dve", lambda E: E.memset(STb[0][:, :, :], 0.0), writes=[b_STb[0]])
    bp = par["b"]
    toks = []
    sti = 0
    gch = 0
    for tt in range(NT_):
        c0 = tt * TR
        load_xn(P, xn, b_xn, XN, KT, c0, TR)

        def shift_evac(dst, b_dst, hidx, cidx, mu, omu, M=64):
            def ev(bi, ci, ps, psb):
                h = hidx(bi)
                cc = cidx(bi)
                P.op("dve", lambda E: E.tensor_scalar(out=dst[0:M, h, :], in0=ps, scalar1=omu[0:M, h:h + 1], scalar2=None, op0=ALU.mult), reads=[psb, bp], writes=[b_dst])
                P.op("dve", lambda E: E.scalar_tensor_tensor(out=dst[0:M, h, 1:TR], in0=ps[:, 0:TR - 1], scalar=mu[0:M, h:h + 1], in1=dst[0:M, h, 1:TR], op0=ALU.mult, op1=ALU.add), reads=[psb, bp], writes=[b_dst])
                P.op("dve", lambda E: E.scalar_tensor_tensor(out=dst[0:M, h, 0:1], in0=carry[0:M, cc:cc + 1], scalar=mu[0:M, h:h + 1], in1=dst[0:M, h, 0:1], op0=ALU.mult, op1=ALU.add), reads=[b_carry, bp], writes=[b_dst])
                P.op("dve", lambda E: E.tensor_copy(out=carry[0:M, cc:cc + 1], in_=ps[:, TR - 1:TR]), reads=[psb], writes=[b_carry])
            return ev
        for qi, (nm, dst, b_dst) in enumerate((("r", R, b_R), ("k", Kb, b_K), ("v", V, b_V))):
            for h0 in range(0, NH, 8):
                nh = min(8, NH - h0)
                proj_blocks(C, xn, b_xn, [(0, TR)], W, KT, [(cols[nm] + h0 * 64, nh * 64)], [(0, j * 64, 64) for j in range(nh)],
                            shift_evac(dst, b_dst, lambda bi, h0=h0: h0 + bi, lambda bi, h0=h0, qi=qi: qi * NH + h0 + bi, par["mu_" + nm], par["omu_" + nm]))
        lb = [(0, 0, 64), (0, 64, 64), (0, 128, 64), (0, 192, 64), (0, 256, 32)]

        def ev_l(bi, ci, ps, psb):
            M = lb[bi][2]
            shift_evac(XL, b_XL, lambda b: b, lambda b: 3 * NH + b, par["mu_l"], par["omu_l"], M=M)(bi, ci, ps, psb)
        proj_blocks(C, xn, b_xn, [(0, TR)], W, KT, [(cols["lora"], 288)], lb, ev_l)

        P.op("act", lambda E: E.activation(out=XL[0:64, 0, :], in_=XL[0:64, 0, :], func=AF.Tanh), reads=[], writes=[b_XL])
        for j in (2, 3):
            P.op("act", lambda E, j=j: E.activation(out=XL[0:64, j, :], in_=XL[0:64, j, :], func=AF.Sigmoid), reads=[], writes=[b_XL])
        P.op("act", lambda E: E.activation(out=XL[0:32, 4, :], in_=XL[0:32, 4, :], func=AF.Sigmoid), reads=[], writes=[b_XL])
        for h in range(NH):
            bk = h % 2
            P.mm(lambda E, h=h, bk=bk: E.matmul(C.ps[bk][0:64, 0:TR], lhsT=par["wup"][0:64, h * 64:(h + 1) * 64], rhs=XL[0:64, 0, :], start=True, stop=True),
                 reads=[b_XL, bp], writes=[C.psb[bk]], last=True)
            P.op("act", lambda E, h=h, bk=bk: E.activation(out=LW[0:64, h, :], in_=C.ps[bk][0:64, 0:TR], func=AF.Sigmoid, bias=par["w0"][0:64, h:h + 1]), reads=[C.psb[bk], bp], writes=[b_LW])
            bk2 = 2 + h % 2
            P.mm(lambda E, h=h, bk2=bk2: E.matmul(C.ps[bk2][0:64, 0:TR], lhsT=par["aup"][0:64, h * 64:(h + 1) * 64], rhs=XL[0:64, 1, :], start=True, stop=True),
                 reads=[b_XL, bp], writes=[C.psb[bk2]], last=True)
            P.op("act", lambda E, h=h, bk2=bk2: E.activation(out=A[0:64, h, :], in_=C.ps[bk2][0:64, 0:TR], func=AF.Sigmoid, bias=par["a0"][0:64, h:h + 1]), reads=[C.psb[bk2], bp], writes=[b_A])
            bk3 = 4 + h % 2
            for j, M in ((0, 64), (1, 64), (2, 32)):
                P.mm(lambda E, h=h, bk3=bk3, j=j, M=M: E.matmul(C.ps[bk3][0:64, 0:TR], lhsT=par["gup"][0:M, j, h * 64:(h + 1) * 64], rhs=XL[0:M, 2 + j, :], start=(j == 0), stop=(j == 2)),
                     reads=[b_XL, bp], writes=[C.psb[bk3]], last=(j == 2))
            P.op("act", lambda E, h=h, bk3=bk3: E.activation(out=G[0:64, h, :], in_=C.ps[bk3][0:64, 0:TR], func=AF.Copy), reads=[C.psb[bk3]], writes=[b_G])
        P.op("dve", lambda E: E.tensor_scalar(out=LW[0:64, :, :], in0=LW[0:64, :, :], scalar1=-math.exp(-0.5), scalar2=None, op0=ALU.mult), reads=[], writes=[b_LW])

        for h in range(NH):
            P.op("dve", lambda E, h=h: E.tensor_scalar(out=KK[0:64, h, :], in0=Kb[0:64, h, :], scalar1=par["kk"][0:64, h:h + 1], scalar2=None, op0=ALU.mult), reads=[b_K, bp], writes=[b_KK])
        for h in range(NH):
            i = h % 2
            bk = 6 + i
            P.op("act", lambda E, h=h, i=i: E.activation(out=t64[i][0:64, :], in_=KK[0:64, h, :], func=AF.Square), reads=[b_KK], writes=[b_t64[i]])
            P.mm(lambda E, i=i, bk=bk: E.matmul(C.ps[bk][0:64, 0:TR], lhsT=ones64, rhs=t64[i][0:64, :], start=True, stop=True), reads=[b_t64[i], C.b_ones], writes=[C.psb[bk]], last=True)
            P.op("act", lambda E, i=i, bk=bk: E.activation(out=t64[i][0:64, :], in_=C.ps[bk][0:64, 0:TR], func=AF.Sqrt), reads=[C.psb[bk]], writes=[b_t64[i]])
            P.op("dve", lambda E, i=i: E.tensor_scalar(out=t64[i][0:64, :], in0=t64[i][0:64, :], scalar1=1e-12, scalar2=None, op0=ALU.max), reads=[], writes=[b_t64[i]])
            P.op("dve", lambda E, i=i: E.reciprocal(out=t64[i][0:64, :], in_=t64[i][0:64, :]), reads=[], writes=[b_t64[i]])
            P.op("dve", lambda E, h=h, i=i: E.tensor_tensor(out=KK[0:64, h, :], in0=KK[0:64, h, :], in1=t64[i][0:64, :], op=ALU.mult), reads=[b_t64[i]], writes=[b_KK])
        for h in range(NH):
            P.op("dve", lambda E, h=h: E.tensor_scalar(out=PX[0:64, h, :], in0=A[0:64, h, :], scalar1=par["ka"][0:64, h:h + 1], scalar2=par["oka"][0:64, h:h + 1], op0=ALU.mult, op1=ALU.add), reads=[b_A, bp], writes=[b_PX])
        P.op("dve", lambda E: E.tensor_tensor(out=Kb[0:64, :, :], in0=Kb[0:64, :, :], in1=PX[0:64, :, :], op=ALU.mult), reads=[b_PX], writes=[b_K])
        P.op("dve", lambda E: E.tensor_tensor(out=A[0:64, :, :], in0=A[0:64, :, :], in1=KK[0:64, :, :], op=ALU.mult), reads=[b_KK], writes=[b_A])
        P.op("dve", lambda E: E.tensor_tensor(out=PX[0:64, :, :], in0=R[0:64, :, :], in1=Kb[0:64, :, :], op=ALU.mult), reads=[b_R, b_K], writes=[b_PX])
        for h in range(NH):
            i = h % 2
            bk = 6 + i
            P.op("dve", lambda E, h=h, i=i: E.tensor_scalar(out=t64[i][0:64, :], in0=PX[0:64, h, :], scalar1=par["rk"][0:64, h:h + 1], scalar2=None, op0=ALU.mult), reads=[b_PX, bp], writes=[b_t64[i]])
            P.mm(lambda E, i=i, bk=bk: E.matmul(C.ps[bk][0:64, 0:TR], lhsT=ones64, rhs=t64[i][0:64, :], start=True, stop=True), reads=[b_t64[i], C.b_ones], writes=[C.psb[bk]], last=True)
            P.op("dve", lambda E, h=h, bk=bk: E.tensor_tensor(out=BON[0:64, h, :], in0=C.ps[bk][0:64, 0:TR], in1=V[0:64, h, :], op=ALU.mult), reads=[C.psb[bk], b_V], writes=[b_BON])

        if DBG is not None and tt == 0:
            for di, (buf_, b_) in enumerate(((R, b_R), (Kb, b_K), (V, b_V), (LW, b_LW), (A, b_A), (G, b_G), (KK, b_KK), (BON, b_BON))):
                toks.append(P.dma("sp", DBG[di, :, :].rearrange("p (h t) -> p h t", h=NH), buf_[0:64, :, :], reads=[b_], writes=[Buf()]))
        def v4(ap):
            return ap[0:64, :, :].rearrange("p h (c l) -> p h c l", l=L)
        src, b_src = LW, b_LW
        pi = 0
        d = 1
        while d < L:
            dst, b_dst = CS[pi], b_CS[pi]
            for h in range(NH):
                s4 = src[0:64, h, :].rearrange("p (c l) -> p c l", l=L)
                d4 = dst[0:64, h, :].rearrange("p (c l) -> p c l", l=L)
                P.op("dve", lambda E, s4=s4, d4=d4, d=d: E.tensor_copy(out=d4[:, :, 0:d], in_=s4[:, :, 0:d]), reads=[b_src], writes=[b_dst])
                P.op("dve", lambda E, s4=s4, d4=d4, d=d: E.tensor_tensor(out=d4[:, :, d:L], in0=s4[:, :, d:L], in1=s4[:, :, 0:L - d], op=ALU.add), reads=[b_src], writes=[b_dst])
            src, b_src = dst, b_dst
            pi = 1 - pi
            d *= 2
        CUM, b_CUM = src, b_src
        OTH, b_OTH = CS[pi], b_CS[pi]
        P.op("act", lambda E: E.activation(out=PX[0:64, :, :], in_=CUM[0:64, :, :], func=AF.Exp), reads=[b_CUM], writes=[b_PX])
        P.op("dve", lambda E: E.tensor_tensor(out=R[0:64, :, :], in0=R[0:64, :, :], in1=PX[0:64, :, :], op=ALU.mult), reads=[b_PX], writes=[b_R])
        for h in range(NH):
            p4 = PX[0:64, h, :].rearrange("p (c l) -> p c l", l=L)
            P.op("dve", lambda E, h=h, p4=p4: E.tensor_copy(out=PL[0:64, h, :], in_=p4[:, :, L - 1]), reads=[b_PX], writes=[b_PL])
        P.op("act", lambda E: E.activation(out=PX[0:64, :, :], in_=CUM[0:64, :, :], func=AF.Exp, scale=-1.0), reads=[b_CUM, b_PL, b_R], writes=[b_PX])
        P.op("dve", lambda E: E.tensor_tensor(out=Kb[0:64, :, :], in0=Kb[0:64, :, :], in1=PX[0:64, :, :], op=ALU.mult), reads=[b_PX], writes=[b_K])
        P.op("dve", lambda E: E.tensor_tensor(out=A[0:64, :, :], in0=A[0:64, :, :], in1=PX[0:64, :, :], op=ALU.mult), reads=[b_PX], writes=[b_A])
        P.op("dve", lambda E: E.tensor_tensor(out=OTH[0:64, :, :], in0=CUM[0:64, :, :], in1=LW[0:64, :, :], op=ALU.subtract), reads=[b_CUM, b_LW, b_K, b_A], writes=[b_OTH])
        P.op("act", lambda E: E.activation(out=OTH[0:64, :, :], in_=OTH[0:64, :, :], func=AF.Exp), reads=[], writes=[b_OTH])
        P.op("dve", lambda E: E.tensor_tensor(out=KK[0:64, :, :], in0=KK[0:64, :, :], in1=OTH[0:64, :, :], op=ALU.mult), reads=[b_OTH], writes=[b_KK])

        fl = lambda t: t[0:64, :, :].rearrange("p h l -> p (h l)")
        P.op("act", lambda E: E.activation(out=Rb[0:64, :, :], in_=R[0:64, :, :], func=AF.Copy), reads=[b_R], writes=[b_Rb])
        P.op("pool", lambda E: E.tensor_copy(out=Kbb[0:64, :, :], in_=Kb[0:64, :, :]), reads=[b_K], writes=[b_Kbb])
        P.op("act", lambda E: E.activation(out=Ab[0:64, :, :], in_=A[0:64, :, :], func=AF.Copy), reads=[b_A], writes=[b_Ab])
        P.op("pool", lambda E: E.tensor_copy(out=KKb[0:64, :, :], in_=KK[0:64, :, :]), reads=[b_KK], writes=[b_KKb])
        P.op("pool", lambda E: E.tensor_copy(out=Vb[0:64, :, :], in_=V[0:64, :, :]), reads=[b_V], writes=[b_Vb])

        def prep_gen(c, q, cs):
            for (src_, b_s, dstl, b_dl, bank) in ((Vb, b_Vb, VT, b_VT, 0), (Kbb, b_Kbb, KtT, b_KtT, 1), (Ab, b_Ab, BtT, b_BtT, 2)):
                for h in range(NH):
                    P.mm(lambda E, src_=src_, h=h, bank=bank: E.matmul(C.ps[bank][0:64, h * 64:(h + 1) * 64], lhsT=src_[0:64, h, cs], rhs=id64b, start=True, stop=True),
                         reads=[b_s, b_idb], writes=[C.psb[bank]], last=(h == NH - 1))
                P.op("act", lambda E, dstl=dstl, bank=bank: E.activation(out=dstl[q][0:64, :, :], in_=C.ps[bank][0:64, 0:HW].rearrange("p (h l) -> p h l", l=64), func=AF.Copy),
                     reads=[C.psb[bank]], writes=[b_dl[q]])
            yield
            specs = ((Kbb, b_Kbb, KKb, b_KKb, M1, b_M1, "mS8", 3), (Ab, b_Ab, KKb, b_KKb, Aa, b_Aa, "mS8", 0), (KKb, b_KKb, Ab, b_Ab, At, b_At, "mL8", 1),
                     (Kbb, b_Kbb, Rb, b_Rb, N1, b_N1, "mI8", 2), (Ab, b_Ab, Rb, b_Rb, N2, b_N2, "mI8", 3))
            for (l_, b_l, r_, b_r, dstl, b_dl, mk, bank) in specs:
                for h in range(NH):
                    P.mm(lambda E, l_=l_, r_=r_, h=h, bank=bank: E.matmul(C.ps[bank][0:64, h * 64:(h + 1) * 64], lhsT=l_[0:64, h, cs], rhs=r_[0:64, h, cs], start=True, stop=True),
                         reads=[b_l, b_r], writes=[C.psb[bank]], last=(h == NH - 1))
                P.op("dve", lambda E, dstl=dstl, bank=bank, mk=mk: E.tensor_tensor(out=dstl[q][0:64, :, :].rearrange("p h l -> p (h l)"), in0=C.ps[bank][0:64, 0:HW], in1=cst[mk][0:64, 0:HW], op=ALU.mult),
                     reads=[C.psb[bank], cst["b"]], writes=[b_dl[q]])
            yield
            P.op("dve", lambda E: E.tensor_tensor(out=fl(Xm[q]), in0=cst["ident8"][0:64, 0:HW], in1=fl(Aa[q]), op=ALU.subtract), reads=[b_Aa[q], cst["b"]], writes=[b_X[q]])
            P.op("dve", lambda E: E.tensor_tensor(out=fl(Xt[q]), in0=cst["ident8"][0:64, 0:HW], in1=fl(At[q]), op=ALU.subtract), reads=[b_At[q], cst["b"]], writes=[b_Xt[q]])
            NIT = 5
            for it in range(NIT):
                lastit = (it == NIT - 1)
                for h in range(NH):
                    P.mm(lambda E, h=h: E.matmul(C.ps[0][0:64, h * 64:(h + 1) * 64], lhsT=At[q][0:64, h, :], rhs=Aa[q][0:64, h, :], start=True, stop=True),
                         reads=[b_At[q], b_Aa[q]], writes=[C.psb[0]], last=(h == NH - 1))
                if not lastit:
                    for h in range(NH):
                        P.mm(lambda E, h=h: E.matmul(C.ps[1][0:64, h * 64:(h + 1) * 64], lhsT=Aa[q][0:64, h, :], rhs=At[q][0:64, h, :], start=True, stop=True),
                             reads=[b_At[q], b_Aa[q]], writes=[C.psb[1]], last=(h == NH - 1))
                P.op("act", lambda E: E.activation(out=fl(Aa[q]), in_=C.ps[0][0:64, 0:HW], func=AF.Copy), reads=[C.psb[0], C.psb[1] if not lastit else C.psb[0]], writes=[b_Aa[q]])
                if not lastit:
                    P.op("act", lambda E: E.activation(out=fl(At[q]), in_=C.ps[1][0:64, 0:HW], func=AF.Copy), reads=[C.psb[1]], writes=[b_At[q]])
                yield
                for h in range(NH):
                    P.mm(lambda E, h=h: E.matmul(C.ps[2][0:64, h * 64:(h + 1) * 64], lhsT=Xt[q][0:64, h, :], rhs=Aa[q][0:64, h, :], start=True, stop=True),
                         reads=[b_Xt[q], b_Aa[q]], writes=[C.psb[2]], last=(h == NH - 1))
                if not lastit:
                    for h in range(NH):
                        P.mm(lambda E, h=h: E.matmul(C.ps[3][0:64, h * 64:(h + 1) * 64], lhsT=Aa[q][0:64, h, :], rhs=Xt[q][0:64, h, :], start=True, stop=True),
                             reads=[b_Xt[q], b_Aa[q]], writes=[C.psb[3]], last=(h == NH - 1))
                P.op("dve", lambda E: E.tensor_tensor(out=fl(Xm[q]), in0=C.ps[2][0:64, 0:HW], in1=fl(Xm[q]), op=ALU.add), reads=[C.psb[2], C.psb[3] if not lastit else C.psb[2]], writes=[b_X[q]])
                if not lastit:
                    P.op("dve", lambda E: E.tensor_tensor(out=fl(Xt[q]), in0=C.ps[3][0:64, 0:HW], in1=fl(Xt[q]), op=ALU.add), reads=[C.psb[3]], writes=[b_Xt[q]])
                yield

        def serial_gen(c, q, cs, So, b_So, Sn, b_Sn, Sob, b_Sob, Snb, b_Snb):
            for h in range(NH):
                P.mm(lambda E, h=h: E.matmul(C.ps[4][0:64, h * 64:(h + 1) * 64], lhsT=KKb[0:64, h, cs], rhs=Sob[0:64, h, :], start=True, stop=False),
                     reads=[b_KKb, b_Sob], writes=[C.psb[4]], last=False)
                P.mm(lambda E, h=h: E.matmul(C.ps[4][0:64, h * 64:(h + 1) * 64], lhsT=M1[q][0:64, h, :], rhs=VT[q][0:64, h, :], start=False, stop=True),
                     reads=[b_M1[q], b_VT[q]], writes=[C.psb[4]], last=(h == NH - 1))
            P.op("act", lambda E: E.activation(out=fl(RH[q]), in_=C.ps[4][0:64, 0:HW], func=AF.Copy), reads=[C.psb[4]], writes=[b_RH[q]])
            yield
            for h in range(NH):
                P.mm(lambda E, h=h: E.matmul(C.ps[5][0:64, h * 64:(h + 1) * 64], lhsT=Xm[q][0:64, h, :], rhs=RH[q][0:64, h, :], start=True, stop=True),
                     reads=[b_X[q], b_RH[q]], writes=[C.psb[5]], last=(h == NH - 1))
            P.op("act", lambda E: E.activation(out=fl(NU[q]), in_=C.ps[5][0:64, 0:HW], func=AF.Copy, scale=-1.0), reads=[C.psb[5]], writes=[b_NU[q]])
            yield
            for h in range(NH):
                P.mm(lambda E, h=h: E.matmul(C.ps[7][0:64, h * 64:(h + 1) * 64], lhsT=KtT[q][0:64, h, :], rhs=VT[q][0:64, h, :], start=True, stop=False),
                     reads=[b_KtT[q], b_VT[q]], writes=[C.psb[7]], last=False)
                P.mm(lambda E, h=h: E.matmul(C.ps[7][0:64, h * 64:(h + 1) * 64], lhsT=BtT[q][0:64, h, :], rhs=NU[q][0:64, h, :], start=False, stop=True),
                     reads=[b_BtT[q], b_NU[q]], writes=[C.psb[7]], last=(h == NH - 1))
            for h in range(NH):
                P.mm(lambda E, h=h: E.matmul(C.ps[6][0:64, h * 64:(h + 1) * 64], lhsT=Sob[0:64, h, :], rhs=Rb[0:64, h, cs], start=True, stop=False),
                     reads=[b_Sob, b_Rb], writes=[C.psb[6]], last=False)
                P.mm(lambda E, h=h: E.matmul(C.ps[6][0:64, h * 64:(h + 1) * 64], lhsT=VT[q][0:64, h, :], rhs=N1[q][0:64, h, :], start=False, stop=False),
                     reads=[b_VT[q], b_N1[q]], writes=[C.psb[6]], last=False)
                P.mm(lambda E, h=h: E.matmul(C.ps[6][0:64, h * 64:(h + 1) * 64], lhsT=NU[q][0:64, h, :], rhs=N2[q][0:64, h, :], start=False, stop=True),
                     reads=[b_NU[q], b_N2[q]], writes=[C.psb[6]], last=(h == NH - 1))
            P.op("dve", lambda E: E.tensor_tensor(out=fl(tmpS), in0=C.ps[7][0:64, 0:HW], in1=fl(So), op=ALU.add), reads=[C.psb[7], b_So], writes=[b_tmpS])
            for h in range(NH):
                P.op("dve", lambda E, h=h: E.tensor_scalar(out=Sn[0:64, h, :], in0=tmpS[0:64, h, :], scalar1=PL[0:64, h, c:c + 1], scalar2=None, op0=ALU.mult), reads=[b_tmpS, b_PL], writes=[b_Sn])
            P.op("act", lambda E: E.activation(out=fl(Snb), in_=fl(Sn), func=AF.Copy), reads=[b_Sn], writes=[b_Snb])
            P.op("act", lambda E: E.activation(out=YB[0:64, :, cs], in_=C.ps[6][0:64, 0:HW].rearrange("p (h l) -> p h l", l=64), func=AF.Copy), reads=[C.psb[6]], writes=[b_YB])
            yield

        def drain(g):
            for _ in g:
                pass
        qs = [(gch + c) % 2 for c in range(NCH)]
        gch += NCH
        drain(prep_gen(0, qs[0], slice(0, L)))
        for c in range(NCH):
            sg = serial_gen(c, qs[c], slice(c * L, (c + 1) * L), ST[sti], b_ST[sti], ST[1 - sti], b_ST[1 - sti], STb[sti], b_STb[sti], STb[1 - sti], b_STb[1 - sti])
            sti = 1 - sti
            pg = prep_gen(c + 1, qs[c + 1], slice((c + 1) * L, (c + 2) * L)) if c + 1 < NCH else iter(())
            alive = [sg, pg]
            while alive:
                for g in list(alive):
                    try:
                        next(g)
                    except StopIteration:
                        alive.remove(g)
        if DBG is not None and tt == 0:
            for di, (buf_, b_) in enumerate(((R, b_R), (Kb, b_K), (A, b_A), (KK, b_KK), (YB, b_YB))):
                toks.append(P.dma("sp", DBG[8 + di, :, :].rearrange("p (h t) -> p h t", h=NH), buf_[0:64, :, :], reads=[b_], writes=[Buf()]))
        for h in range(NH):
            i = h % 2
            P.mm(lambda E, h=h: E.matmul(C.ps[0][0:64, 0:TR], lhsT=ones64, rhs=YB[0:64, h, :], start=True, stop=True), reads=[b_YB, C.b_ones], writes=[C.psb[0]], last=True)
            P.op("act", lambda E, h=h, i=i: E.activation(out=t64[i][0:64, :], in_=YB[0:64, h, :], func=AF.Square), reads=[b_YB], writes=[b_t64[i]])
            P.mm(lambda E, i=i: E.matmul(C.ps[1][0:64, 0:TR], lhsT=ones64, rhs=t64[i][0:64, :], start=True, stop=True), reads=[b_t64[i], C.b_ones], writes=[C.psb[1]], last=True)
            mean = PX[0:64, 0, :]
            var = PX[0:64, 1, :]
            P.op("act", lambda E: E.activation(out=mean, in_=C.ps[0][0:64, 0:TR], func=AF.Copy, scale=1.0 / 64), reads=[C.psb[0]], writes=[b_PX])
            P.op("dve", lambda E: E.tensor_tensor(out=var, in0=mean, in1=mean, op=ALU.mult), reads=[], writes=[b_PX])
            P.op("dve", lambda E: E.scalar_tensor_tensor(out=var, in0=C.ps[1][0:64, 0:TR], scalar=1.0 / 64, in1=var, op0=ALU.mult, op1=ALU.subtract), reads=[C.psb[1]], writes=[b_PX])
            P.op("dve", lambda E: E.tensor_scalar(out=var, in0=var, scalar1=0.0, scalar2=64e-5, op0=ALU.max, op1=ALU.add), reads=[], writes=[b_PX])
            P.op("act", lambda E: E.activation(out=var, in_=var, func=AF.Sqrt), reads=[], writes=[b_PX])
            P.op("dve", lambda E: E.reciprocal(out=var, in_=var), reads=[], writes=[b_PX])
            P.op("dve", lambda E, h=h: E.tensor_tensor(out=YB[0:64, h, :], in0=YB[0:64, h, :], in1=mean, op=ALU.subtract), reads=[b_PX], writes=[b_YB])
            P.op("dve", lambda E, h=h: E.tensor_tensor(out=YB[0:64, h, :], in0=YB[0:64, h, :], in1=var, op=ALU.mult), reads=[b_PX], writes=[b_YB])
            P.op("dve", lambda E, h=h: E.tensor_scalar(out=YB[0:64, h, :], in0=YB[0:64, h, :], scalar1=par["lg"][0:64, h:h + 1], scalar2=par["lb"][0:64, h:h + 1], op0=ALU.mult, op1=ALU.add), reads=[bp], writes=[b_YB])
            P.op("dve", lambda E, h=h: E.tensor_tensor(out=YB[0:64, h, :], in0=YB[0:64, h, :], in1=BON[0:64, h, :], op=ALU.add), reads=[b_BON], writes=[b_YB])
            P.op("dve", lambda E, h=h, i=i: E.tensor_tensor(out=yo[i][0:64, :], in0=YB[0:64, h, :], in1=G[0:64, h, :], op=ALU.mult), reads=[b_G, b_YB], writes=[b_yo[i]])
            toks.append(P.dma("sp", YOUT.rows(yrow0 + h * 64, 64, c0, TR), yo[i][0:64, :], reads=[b_yo[i]], writes=[Buf()]))
    return toks


def sb_mixer(C, AF32, ABF, XN, W, cols, KT, S, NHS, cst, YOUT, yrow0, TT=512):
    P = C.P
    NB = S // 128
    scale = 128 ** -0.5
    xn = ABF.alloc(KT, TT)
    b_xn = Buf()
    QT = ABF.alloc(NHS, S); b_QT = Buf()
    KTt = ABF.alloc(NHS, S); b_KT = Buf()
    VT = ABF.alloc(NB, NHS, 128); b_VT = Buf()
    vf = [AF32.alloc(TT), AF32.alloc(TT)]; b_vf = [Buf(), Buf()]
    nvf = 0
    for tt in range(S // TT):
        c0 = tt * TT
        load_xn(P, xn, b_xn, XN, KT, c0, TT)
        for h0 in range(0, NHS, 4):
            nh = min(4, NHS - h0)
            blocks = [(0, j * 128, 128) for j in range(nh)]

            def ev_q(bi, ci, ps, psb, h0=h0, c0=c0):
                P.op("act", lambda E: E.activation(out=QT[:, h0 + bi, c0:c0 + TT], in_=ps, func=AF.Copy, scale=scale), reads=[psb], writes=[b_QT])

            def ev_k(bi, ci, ps, psb, h0=h0, c0=c0):
                P.op("act", lambda E: E.activation(out=KTt[:, h0 + bi, c0:c0 + TT], in_=ps, func=AF.Copy), reads=[psb], writes=[b_KT])
            proj_blocks(C, xn, b_xn, [(0, TT)], W, KT, [(cols["q"] + h0 * 128, nh * 128)], blocks, ev_q, bank0=0)
            proj_blocks(C, xn, b_xn, [(0, TT)], W, KT, [(cols["k"] + h0 * 128, nh * 128)], blocks, ev_k, bank0=4)
            for j in range(nh):
                h = h0 + j

                def ev_v(bi, ci, ps, psb, h=h, c0=c0):
                    nonlocal nvf
                    i = nvf % 2
                    nvf += 1
                    P.op("act", lambda E, i=i: E.activation(out=vf[i][:, :], in_=ps, func=AF.Copy), reads=[psb], writes=[b_vf[i]])
                    for bb in range(TT // 128):
                        bank = 4 + (bb % 4)
                        P.mm(lambda E, i=i, bb=bb, bank=bank: E.matmul(C.ps[bank][:, 0:128], lhsT=vf[i][:, bb * 128:(bb + 1) * 128], rhs=C.ident[:], start=True, stop=True),
                             reads=[b_vf[i], C.b_ident], writes=[C.psb[bank]], last=True)
                        P.op("dve", lambda E, bb=bb, bank=bank: E.tensor_copy(out=VT[:, c0 // 128 + bb, h, :], in_=C.ps[bank][:, 0:128]), reads=[C.psb[bank]], writes=[b_VT])
                proj_blocks(C, xn, b_xn, [(0, TT)], W, KT, [(cols["v"] + h * 128, 128)], [(0, 0, 128)], ev_v, bank0=0)
    NP = 3
    E1 = [AF32.alloc(128) for _ in range(NP)]; b_E1 = [Buf() for _ in range(NP)]
    SP = [AF32.alloc(128) for _ in range(NP)]; b_SP = [Buf() for _ in range(NP)]
    ARG = [AF32.alloc(128) for _ in range(NP)]; b_ARG = [Buf() for _ in range(NP)]
    WT = [ABF.alloc(128) for _ in range(NP)]; b_WT = [Buf() for _ in range(NP)]
    SPS = AF32.alloc(128); b_SPS = Buf()
    yo = [ABF.alloc(128), ABF.alloc(128)]; b_yo = [Buf(), Buf()]
    toks = []
    it = 0
    nq = 0
    for h in range(NHS):
        for qb in range(NB):
            bo = 6 + nq % 2
            nq += 1
            for kb in range(qb, -1, -1):
                i = it % NP
                it += 1
                bz = i
                bl = 3 + i
                diag = (kb == qb)
                P.mm(lambda E, h=h, kb=kb, qb=qb, bz=bz: E.matmul(C.ps[bz][:, 0:128], lhsT=KTt[:, h, kb * 128:(kb + 1) * 128], rhs=QT[:, h, qb * 128:(qb + 1) * 128], start=True, stop=True),
                     reads=[b_KT, b_QT], writes=[C.psb[bz]], last=True)
                P.op("act", lambda E, i=i, bz=bz: E.activation(out=E1[i][:, :], in_=C.ps[bz][:, 0:128], func=AF.Exp), reads=[C.psb[bz]], writes=[b_E1[i]])
                P.op("act", lambda E, i=i: E.activation(out=SP[i][:, :], in_=E1[i][:, :], func=AF.Ln, bias=1.0), reads=[b_E1[i]], writes=[b_SP[i]])
                P.op("dve", lambda E, i=i, bz=bz: E.tensor_tensor(out=ARG[i][:, :], in0=C.ps[bz][:, 0:128], in1=SP[i][:, :], op=ALU.subtract), reads=[C.psb[bz], b_SP[i]], writes=[b_ARG[i]])
                if diag:
                    P.op("dve", lambda E, i=i: E.tensor_tensor(out=SP[i][:, :], in0=SP[i][:, :], in1=cst["mS"][:, :], op=ALU.mult), reads=[cst["b"], b_ARG[i]], writes=[b_SP[i]])
                P.mm(lambda E, i=i, bl=bl, diag=diag: E.matmul(C.ps[bl][:, 0:128], lhsT=cst["ntri"][:, :], rhs=SP[i][:, :], start=True, stop=diag),
                     reads=[b_SP[i], cst["b"]], writes=[C.psb[bl]], last=diag)
                if not diag:
                    P.mm(lambda E, bl=bl: E.matmul(C.ps[bl][:, 0:128], lhsT=cst["nones"][:, :], rhs=SPS[:, :], start=False, stop=True),
                         reads=[b_SPS, cst["b"]], writes=[C.psb[bl]], last=True)
                P.op("dve", lambda E, i=i, bl=bl: E.tensor_tensor(out=ARG[i][:, :], in0=ARG[i][:, :], in1=C.ps[bl][:, 0:128], op=ALU.add), reads=[C.psb[bl]], writes=[b_ARG[i]])
                P.op("act", lambda E, i=i: E.activation(out=WT[i][:, :], in_=ARG[i][:, :], func=AF.Exp), reads=[b_ARG[i]], writes=[b_WT[i]])
                if diag:
                    P.op("dve", lambda E, i=i: E.tensor_tensor(out=WT[i][:, :], in0=WT[i][:, :], in1=cst["mSb"][:, :], op=ALU.mult), reads=[cst["b"]], writes=[b_WT[i]])
                if kb > 0:
                    if diag:
                        P.op("pool", lambda E, i=i: E.tensor_copy(out=SPS[:, :], in_=SP[i][:, :]), reads=[b_SP[i]], writes=[b_SPS])
                    else:
                        P.op("pool", lambda E, i=i: E.tensor_tensor(out=SPS[:, :], in0=SPS[:, :], in1=SP[i][:, :], op=ALU.add), reads=[b_SP[i]], writes=[b_SPS])
                P.mm(lambda E, i=i, h=h, kb=kb, qb=qb, bo=bo: E.matmul(C.ps[bo][:, 0:128], lhsT=VT[:, kb, h, :], rhs=WT[i][:, :], start=(kb == qb), stop=(kb == 0)),
                     reads=[b_VT, b_WT[i]], writes=[C.psb[bo]], last=True)
            j = nq % 2
            P.op("act", lambda E, j=j, bo=bo: E.activation(out=yo[j][:, :], in_=C.ps[bo][:, 0:128], func=AF.Copy), reads=[C.psb[bo]], writes=[b_yo[j]])
            toks.append(P.dma("sp", YOUT.rows(yrow0 + h * 128, 128, qb * 128, 128), yo[j][:, :], reads=[b_yo[j]], writes=[Buf()]))
    return toks

import numpy as np, ml_dtypes

LORA = 288
CONV_K = 31


class Cfg:
    def __init__(self, D=4096, DFF=11008, SEQ=2048, B=4, DG=1024):
        self.D, self.DFF, self.SEQ, self.B, self.DG = D, DFF, SEQ, B, DG
        self.KT = D // 128
        self.NT = SEQ // 2
        self.CT = DG // 128
        self.NHR = DG // 64
        self.NHS = DG // 128
        self.NHRc = self.NHR // 2
        self.NHSc = self.NHS // 2
        self.NRW = 3 * DG + LORA
        self.NIN = 2 * DG + self.NRW + 2 * DG + 3 * DG
        c = 0
        self.c_ca = c; c += DG
        self.c_cg = c; c += DG
        self.c_su = c; c += DG
        self.c_sv = c; c += DG
        self.c_r = c; c += self.NHRc * 64
        self.c_k = c; c += self.NHRc * 64
        self.c_v = c; c += self.NHRc * 64
        self.c_l = c; c += LORA
        self.c_q = c; c += self.NHSc * 128
        self.c_sk = c; c += self.NHSc * 128
        self.c_svv = c; c += self.NHSc * 128
        self.NC = c
        CT = self.CT
        o = 0
        self.o_cw = o; o += CT * CONV_K
        self.o_cb = o; o += CT
        self.o_clg = o; o += CT
        self.o_clb = o; o += CT
        self.o_slg = o; o += CT
        self.o_slb = o; o += CT
        self.NP128 = o
        NH = self.NHRc
        o = 0
        self.q_mu = o; o += 3 * NH + 5
        self.q_ka = o; o += NH
        self.q_w0 = o; o += NH
        self.q_a0 = o; o += NH
        self.q_kk = o; o += NH
        self.q_rk = o; o += NH
        self.q_lg = o; o += NH
        self.q_lb = o; o += NH
        self.q_wup = o; o += NH * 64
        self.q_aup = o; o += NH * 64
        self.q_gup = o; o += 3 * NH * 64
        self.NP64 = o
        self.NC128 = 4 * 128 + 4 * NH * 64
        self.YOWN = 2 * DG
        self.YALL = self.NHRc * 64 + self.NHSc * 128
        self.TR = 256


def consts_np(cfg):
    NH = cfg.NHRc
    c = np.zeros((128, cfg.NC128), np.float32)
    i = np.arange(128)
    c[:, 0:128] = np.eye(128)
    c[:, 128:256] = (i[:, None] < i[None, :])
    c[:, 256:384] = -(i[:, None] > i[None, :]).astype(np.float32)
    c[:, 384:512] = -1.0
    j = np.arange(64)
    e8 = np.tile(np.eye(64, dtype=np.float32), (1, NH))
    mS = np.tile((j[:, None] < j[None, :]).astype(np.float32), (1, NH))
    mI = np.tile((j[:, None] <= j[None, :]).astype(np.float32), (1, NH))
    mL = np.tile((j[:, None] > j[None, :]).astype(np.float32), (1, NH))
    o = 512
    for m in (e8, mS, mI, mL):
        c[0:64, o:o + NH * 64] = m
        o += NH * 64
    return c


def tile128(v, nt):
    return np.ascontiguousarray(np.asarray(v).reshape(nt, 128).T)


def mixer_inputs(cfg, L, role, xn_full_T, own_T):
    DG, NH, NHS, CT = cfg.DG, cfg.NHRc, cfg.NHSc, cfg.CT
    W = L["mix_w_in"]
    R0 = 2 * DG
    S0 = 2 * DG + cfg.NRW
    B0 = S0 + 2 * DG
    hr = role * NH * 64
    hs = role * NHS * 128
    colsel = np.concatenate([
        np.arange(0, 2 * DG), np.arange(S0, S0 + 2 * DG),
        R0 + hr + np.arange(NH * 64), R0 + DG + hr + np.arange(NH * 64), R0 + 2 * DG + hr + np.arange(NH * 64),
        R0 + 3 * DG + np.arange(LORA),
        B0 + hs + np.arange(NHS * 128), B0 + DG + hs + np.arange(NHS * 128), B0 + 2 * DG + hs + np.arange(NHS * 128)])
    assert len(colsel) == cfg.NC
    wmix = np.ascontiguousarray(W[:, colsel])
    p128 = np.zeros((128, cfg.NP128), np.float32)
    cw = L["conv_w"]
    p128[:, cfg.o_cw:cfg.o_cw + CT * CONV_K] = cw.T.reshape(CT, 128, CONV_K).transpose(1, 0, 2).reshape(128, CT * CONV_K)
    p128[:, cfg.o_cb:cfg.o_cb + CT] = tile128(L["conv_b"], CT)
    p128[:, cfg.o_clg:cfg.o_clg + CT] = tile128(L["conv_ln_g"], CT)
    p128[:, cfg.o_clb:cfg.o_clb + CT] = tile128(L["conv_ln_b"], CT)
    p128[:, cfg.o_slg:cfg.o_slg + CT] = tile128(L["sgu_ln_g"], CT)
    p128[:, cfg.o_slb:cfg.o_slb + CT] = tile128(L["sgu_ln_b"], CT)
    ws = L["sgu_w_s"]
    wsT = np.ascontiguousarray(ws.transpose(2, 0, 1).reshape(128, CT * 128))
    bsb = np.ascontiguousarray(np.broadcast_to(L["sgu_b_s"].reshape(1, CT * 128), (128, CT * 128)))
    p64 = np.zeros((64, cfg.NP64), np.float32)
    mu = L["rwkv_mu"]

    def h64(v, h0):
        return np.asarray(v).reshape(-1, 64)[h0:h0 + NH].T
    hh = role * NH
    o = cfg.q_mu
    p64[:, o:o + NH] = h64(mu[0:DG], hh); o += NH
    p64[:, o:o + NH] = h64(mu[DG:2 * DG], hh); o += NH
    p64[:, o:o + NH] = h64(mu[2 * DG:3 * DG], hh); o += NH
    ml = mu[3 * DG:3 * DG + LORA]
    for b, (s, n) in enumerate(((0, 64), (64, 64), (128, 64), (192, 64), (256, 32))):
        p64[0:n, o + b] = ml[s:s + n]
    p64[:, cfg.q_ka:cfg.q_ka + NH] = h64(L["rwkv_k_a"], hh)
    p64[:, cfg.q_w0:cfg.q_w0 + NH] = h64(L["rwkv_w0"], hh)
    p64[:, cfg.q_a0:cfg.q_a0 + NH] = h64(L["rwkv_a0"], hh)
    p64[:, cfg.q_kk:cfg.q_kk + NH] = h64(L["rwkv_k_k"], hh)
    p64[:, cfg.q_rk:cfg.q_rk + NH] = h64(L["rwkv_r_k"].reshape(-1), hh)
    p64[:, cfg.q_lg:cfg.q_lg + NH] = h64(L["rwkv_lnx_g"], hh)
    p64[:, cfg.q_lb:cfg.q_lb + NH] = h64(L["rwkv_lnx_b"], hh)
    p64[:, cfg.q_wup:cfg.q_wup + NH * 64] = L["rwkv_w_up"][:, hr:hr + NH * 64]
    p64[:, cfg.q_aup:cfg.q_aup + NH * 64] = L["rwkv_a_up"][:, hr:hr + NH * 64]
    gu = L["rwkv_g_up"][:, hr:hr + NH * 64]
    for b, (s, n) in enumerate(((0, 64), (64, 64), (128, 32))):
        p64[0:n, cfg.q_gup + b * NH * 64: cfg.q_gup + (b + 1) * NH * 64] = gu[s:s + n]
    return {"xno": own_T, "xn": xn_full_T, "wmix": wmix, "p128": p128, "wsT": wsT, "bsb": bsb, "p64": p64, "c128": consts_np(cfg)}


def mixer_body(P, C, cfg, T, AR, do=("conv", "sgu", "rwkv", "sb"), halo_mask=None, b_hm=None):
    CT, NH = cfg.CT, cfg.NHRc
    AF32 = ArenaView(AR, F32)
    ABF = ArenaView(AR, BF16)
    p128 = AF32.alloc(cfg.NP128); bp128 = Buf()
    P.dma("sp", p128, T["p128"], writes=[bp128])
    c128 = AF32.alloc(cfg.NC128); bc = Buf()
    P.dma("sp", c128, T["c128"], writes=[bc])
    p64 = AF32.alloc(cfg.NP64); bp64 = Buf()
    P.dma("sp", p64[0:64, :], T["p64"], writes=[bp64])
    om = AF32.alloc(4 * NH + 5)
    P.op("dve", lambda E: E.tensor_scalar(out=om[0:64, :], in0=p64[0:64, cfg.q_mu:cfg.q_mu + 4 * NH + 5], scalar1=-1.0, scalar2=1.0, op0=ALU.mult, op1=ALU.add), reads=[bp64], writes=[bp64])
    wsT = ABF.alloc(CT, 128)
    P.dma("pool", wsT, T["wsT"].rearrange("p (h i) -> p h i", i=128), writes=[bp128])
    P.op("dve", lambda E: E.memset(wsT[64:128, :, 0:64], 0.0), reads=[], writes=[bp128])
    mSb = ABF.alloc(128)
    P.op("dve", lambda E: E.tensor_copy(out=mSb, in_=c128[:, 128:256]), reads=[bc], writes=[bc])
    C.ident = c128[:, 0:128]
    C.b_ident = bc
    cst = {"b": bc, "mS": c128[:, 128:256], "ntri": c128[:, 256:384], "nones": c128[:, 384:512], "mSb": mSb,
           "ident8": c128[:, 512:512 + NH * 64], "mS8": c128[:, 512 + NH * 64:512 + 2 * NH * 64],
           "mI8": c128[:, 512 + 2 * NH * 64:512 + 3 * NH * 64], "mL8": c128[:, 512 + 3 * NH * 64:512 + 4 * NH * 64]}
    K = CONV_K
    par_c = {"b": bp128, "cw": p128[:, cfg.o_cw:cfg.o_cw + CT * K].rearrange("p (c k) -> p c k", k=K), "cb": p128[:, cfg.o_cb:cfg.o_cb + CT],
             "lg": p128[:, cfg.o_clg:cfg.o_clg + CT], "lb": p128[:, cfg.o_clb:cfg.o_clb + CT]}
    par_s = {"b": bp128, "lg": p128[:, cfg.o_slg:cfg.o_slg + CT], "lb": p128[:, cfg.o_slb:cfg.o_slb + CT], "wsT": wsT}
    q = cfg.q_mu
    par_r = {"b": bp64, "mu_r": p64[:, q:q + NH], "mu_k": p64[:, q + NH:q + 2 * NH], "mu_v": p64[:, q + 2 * NH:q + 3 * NH], "mu_l": p64[:, q + 3 * NH:q + 3 * NH + 5],
             "omu_r": om[:, 0:NH], "omu_k": om[:, NH:2 * NH], "omu_v": om[:, 2 * NH:3 * NH], "omu_l": om[:, 3 * NH:3 * NH + 5],
             "ka": p64[:, cfg.q_ka:cfg.q_ka + NH], "oka": om[:, 3 * NH + 5:4 * NH + 5],
             "w0": p64[:, cfg.q_w0:cfg.q_w0 + NH], "a0": p64[:, cfg.q_a0:cfg.q_a0 + NH], "kk": p64[:, cfg.q_kk:cfg.q_kk + NH], "rk": p64[:, cfg.q_rk:cfg.q_rk + NH],
             "lg": p64[:, cfg.q_lg:cfg.q_lg + NH], "lb": p64[:, cfg.q_lb:cfg.q_lb + NH],
             "wup": p64[:, cfg.q_wup:cfg.q_wup + NH * 64], "aup": p64[:, cfg.q_aup:cfg.q_aup + NH * 64],
             "gup": p64[:, cfg.q_gup:cfg.q_gup + 3 * NH * 64].rearrange("p (b n) -> p b n", b=3)}
    base = AR.off
    toks = []
    if "conv" in do:
        toks += conv_mixer(C, AF32, ABF, T["xno"], T["wmix"], cfg.c_ca, cfg.c_cg, cfg.KT, cfg.NT, CT, par_c, T["yown"], 0, halo_mask=halo_mask, b_hm=b_hm)
        P.barrier(); AR.reset(base)
    if "sgu" in do:
        bsb = AF32.alloc(CT, 128)
        P.dma("sp", bsb, T["bsb"].rearrange("p (h i) -> p h i", i=128), writes=[bp128])
        par_s["bsb"] = bsb
        toks += sgu_mixer(C, AF32, ABF, T["xno"], T["wmix"], cfg.c_su, cfg.c_sv, cfg.KT, cfg.NT, CT, par_s, T["yown"], cfg.DG)
        P.barrier(); AR.reset(base)
    if "rwkv" in do:
        toks += rwkv_mixer(C, AF32, ABF, T["xn"], T["wmix"], {"r": cfg.c_r, "k": cfg.c_k, "v": cfg.c_v, "lora": cfg.c_l}, cfg.KT, cfg.SEQ, NH, par_r, cst, T["yall"], 0, TR=cfg.TR, DBG=T.get("dbg"))
        P.barrier(); AR.reset(base)
    if "sb" in do:
        toks += sb_mixer(C, AF32, ABF, T["xn"], T["wmix"], {"q": cfg.c_q, "k": cfg.c_sk, "v": cfg.c_svv}, cfg.KT, cfg.SEQ, cfg.NHSc, cst, T["yall"], NH * 64)
        P.barrier(); AR.reset(base)
    return toks


def build_mixer(cfg, do=("conv", "sgu", "rwkv", "sb"), ar_bytes=190 * 1024, dbg=False):
    nc = bass.Bass("TRN2", target_bir_lowering=False)
    P = PB(nc)
    D = cfg.D
    T = {}
    T["xno"] = DV(nc.dram_tensor("xno", [D, 32 + cfg.NT], BF16, kind="ExternalInput").ap())
    T["xn"] = DV(nc.dram_tensor("xn", [D, cfg.SEQ], BF16, kind="ExternalInput").ap())
    T["wmix"] = nc.dram_tensor("wmix", [D, cfg.NC], F32, kind="ExternalInput").ap()
    T["p128"] = nc.dram_tensor("p128", [128, cfg.NP128], F32, kind="ExternalInput").ap()
    T["wsT"] = nc.dram_tensor("wsT", [128, cfg.CT * 128], F32, kind="ExternalInput").ap()
    T["bsb"] = nc.dram_tensor("bsb", [128, cfg.CT * 128], F32, kind="ExternalInput").ap()
    T["p64"] = nc.dram_tensor("p64", [64, cfg.NP64], F32, kind="ExternalInput").ap()
    T["c128"] = nc.dram_tensor("c128", [128, cfg.NC128], F32, kind="ExternalInput").ap()
    T["yown"] = nc.dram_tensor("yown", [cfg.YOWN, cfg.NT], BF16, kind="ExternalOutput").ap()
    T["yall"] = RV(nc.dram_tensor("yall", [cfg.YALL, cfg.SEQ], BF16, kind="ExternalOutput").ap())
    if dbg:
        T["dbg"] = nc.dram_tensor("dbg", [13, 64, cfg.NHRc * cfg.TR], F32, kind="ExternalOutput").ap()
    C = Ctx(P, cfg.NT)
    AR = Arena(P, "arena", ar_bytes)
    toks = mixer_body(P, C, cfg, T, AR, do)
    print("mixer arena peak KB", AR.peak * 4 / 1024)
    P.finish(toks)
    print("mixer n_inst", P.n_inst)
    return nc

import numpy as np, ml_dtypes

BF = ml_dtypes.bfloat16


def t_phase_body(P, C, cfg, AR, T, mode, xres):
    D, DFF, NT, KT = cfg.D, cfg.DFF, cfg.NT, cfg.KT
    base = AR.off
    XR = T["xres"]
    NG = {"first": 2, "mid": 3, "last": 2}[mode]
    g_all = AR.alloc((NG * KT,), F32)
    nt = norm_tmp(P, AR, KT)
    P.dma("sp", g_all, T["gn"], writes=[nt["b_g"]])
    KM = (4 * cfg.DG) // 128
    xn = AR.alloc((max(KT, KM), NT), BF16)
    b_xn = Buf()
    FT = DFF // 128
    chmax = max(4, -(-FT // 4))
    gT = AR.alloc((chmax, NT), BF16)
    b_gT = Buf()
    rt = rmw_tmp(P, AR)
    stmp = [AR.alloc((512,), F32) for _ in range(2)]
    b_stmp = [Buf(), Buf()]
    toks = []
    gi = 0
    if mode in ("mid", "last"):
        if "yT" in T:
            for k in range(KM):
                P.dma("sp", xn[:, k, :], T["yT"][k * 128:(k + 1) * 128, :], writes=[b_xn])
        else:
            load_y_fused(P, cfg, gT, T, xn, b_xn)
        proj_rmw(C, xn, b_xn, 0, KM, T["wmo"], 0, D, XR, xres, 1.0, rt)
        rmsnorm(C, XR, g_all[:, gi * KT:(gi + 1) * KT], D, xn_sb=xn, b_xn=b_xn, xres=xres, tmp=nt)
        gi += 1
        ffn(C, xn, b_xn, T["win2"], T["wout2"], D, DFF, XR, xres, rt, gT, b_gT, stmp, b_stmp)
    if mode in ("first", "mid"):
        rmsnorm(C, XR, g_all[:, gi * KT:(gi + 1) * KT], D, xn_sb=xn, b_xn=b_xn, xres=xres, tmp=nt)
        gi += 1
        ffn(C, xn, b_xn, T["win1"], T["wout1"], D, DFF, XR, xres, rt, gT, b_gT, stmp, b_stmp)
        rmsnorm(C, XR, g_all[:, gi * KT:(gi + 1) * KT], D, xn_sb=xn, b_xn=b_xn, xres=xres, tmp=nt)
        for k in range(KT):
            toks.append(P.dma("sp", T["xn2"].tile(k, 0, NT), xn[:, k, :], reads=[b_xn], writes=[T.get("b_xn2", Buf())]))
    else:
        toks += rmsnorm(C, XR, g_all[:, gi * KT:(gi + 1) * KT], D, out_f32=T["out"], xres=xres, tmp=nt)
    P.barrier()
    AR.reset(base)
    return toks


def load_y_fused(P, cfg, gT, T, xn, b_xn):
    NT, DG = cfg.NT, cfg.DG
    CT = cfg.CT
    nr, ns = cfg.NHRc * 64, cfg.NHSc * 128
    YA = cfg.YALL
    tA = [gT[:, 0, :], gT[:, 1, :]]
    tB = [gT[:, 2, :], gT[:, 3, :]]
    bA = [Buf(), Buf()]
    bB = [Buf(), Buf()]
    rm = T["rolemask"]
    k = 0
    it = 0
    def own(row0, ntile):
        nonlocal k
        for j in range(ntile):
            P.dma("sp", xn[:, k, :], T["yown"][row0 + j * 128:row0 + (j + 1) * 128, :], reads=[T["b_yown"]], writes=[b_xn])
            k += 1
    def gathered(row0, ntile):
        nonlocal k, it
        for r in range(2):
            for j in range(ntile):
                i = it % 2
                it += 1
                rr = r * YA + row0 + j * 128
                P.dma("sp", tA[i], T["yallg"](r, row0 + j * 128, 0, NT), reads=[T["b_yallg"]], writes=[bA[i]])
                P.dma("sp", tB[i], T["yallg"](r, row0 + j * 128, NT, NT), reads=[T["b_yallg"]], writes=[bB[i]])
                P.op("dve", lambda E, i=i: E.tensor_scalar(out=tA[i], in0=tA[i], scalar1=rm[:, 0:1], scalar2=None, op0=ALU.mult), reads=[T["b_rm"]], writes=[bA[i]])
                P.op("dve", lambda E, i=i, kk=k: E.scalar_tensor_tensor(out=xn[:, kk, :], in0=tB[i], scalar=rm[:, 1:2], in1=tA[i], op0=ALU.mult, op1=ALU.add), reads=[bA[i], bB[i], T["b_rm"]], writes=[b_xn])
                k += 1
    own(0, CT)
    gathered(0, nr // 128)
    own(DG, CT)
    gathered(nr, ns // 128)


def build_T(cfg, mode):
    nc = bass.Bass("TRN2", target_bir_lowering=False)
    P = PB(nc)
    D, DFF, NT, KT = cfg.D, cfg.DFF, cfg.NT, cfg.KT
    T = {}
    NG = {"first": 2, "mid": 3, "last": 2}[mode]
    T["xin"] = nc.dram_tensor("xin", [D, NT], F32, kind="ExternalInput").ap()
    T["gn"] = nc.dram_tensor("gn", [128, NG * KT], F32, kind="ExternalInput").ap()
    if mode in ("mid", "last"):
        T["yT"] = nc.dram_tensor("yT", [4 * cfg.DG, NT], BF16, kind="ExternalInput").ap()
        T["wmo"] = nc.dram_tensor("wmo", [4 * cfg.DG, D], F32, kind="ExternalInput").ap()
        T["win2"] = nc.dram_tensor("win2", [D, 2 * DFF], F32, kind="ExternalInput").ap()
        T["wout2"] = nc.dram_tensor("wout2", [DFF, D], F32, kind="ExternalInput").ap()
    if mode in ("first", "mid"):
        T["win1"] = nc.dram_tensor("win1", [D, 2 * DFF], F32, kind="ExternalInput").ap()
        T["wout1"] = nc.dram_tensor("wout1", [DFF, D], F32, kind="ExternalInput").ap()
        T["xn2"] = DV(nc.dram_tensor("xn2", [D, NT], BF16, kind="ExternalOutput").ap())
        T["xres"] = nc.dram_tensor("xres", [D, NT], F32, kind="ExternalOutput").ap()
    else:
        T["xres"] = nc.dram_tensor("xres", [D, NT], F32, kind="Internal").ap()
        T["out"] = nc.dram_tensor("out", [D, NT], F32, kind="ExternalOutput").ap()
    C = Ctx(P, NT)
    AR = Arena(P, "arena", 190 * 1024)
    H = NT // 512
    xres = {(f, h): Buf() for f in range(KT) for h in range(H)}
    for f in range(KT):
        for h in range(H):
            P.dma("sp", T["xres"][f * 128:(f + 1) * 128, h * 512:(h + 1) * 512], T["xin"][f * 128:(f + 1) * 128, h * 512:(h + 1) * 512], writes=[xres[(f, h)]])
    toks = t_phase_body(P, C, cfg, AR, T, mode, xres)
    toks += [b.w for b in xres.values()]
    P.finish(toks)
    return nc


def gn_pack(cfg, vecs):
    return np.ascontiguousarray(np.concatenate([tile128(v, cfg.KT) for v in vecs], axis=1))


_PROGS = {}


def _prog(cfg, key, builder):
    k = (cfg.D, cfg.DFF, cfg.SEQ, cfg.DG, key)
    if k not in _PROGS:
        _PROGS[k] = builder()
    return _PROGS[k]


def run_module(cfg, inp):
    B, SEQ, D, NT = cfg.B, cfg.SEQ, cfg.D, cfg.NT
    NCORE = 2 * B
    depth = inp["ffn1_w_in"].shape[0]
    cores = list(range(NCORE))
    x = inp["x"]
    xres = [np.ascontiguousarray(x[c // 2, (c % 2) * NT:(c % 2 + 1) * NT, :].T) for c in cores]
    yT = None
    out = None
    for l in range(depth + 1):
        mode = "first" if l == 0 else ("last" if l == depth else "mid")
        nc = _prog(cfg, "T" + mode, lambda: build_T(cfg, mode))
        common = {}
        if mode == "first":
            common["gn"] = gn_pack(cfg, [inp["ffn1_norm"][0], inp["mix_norm"][0]])
        elif mode == "mid":
            common["gn"] = gn_pack(cfg, [inp["ffn2_norm"][l - 1], inp["ffn1_norm"][l], inp["mix_norm"][l]])
        else:
            common["gn"] = gn_pack(cfg, [inp["ffn2_norm"][l - 1], inp["final_norm"]])
        if mode in ("mid", "last"):
            common["wmo"] = inp["mix_w_out"][l - 1]
            common["win2"] = inp["ffn2_w_in"][l - 1]
            common["wout2"] = inp["ffn2_w_out"][l - 1]
        if mode in ("first", "mid"):
            common["win1"] = inp["ffn1_w_in"][l]
            common["wout1"] = inp["ffn1_w_out"][l]
        maps = []
        for c in cores:
            m = dict(common)
            m["xin"] = xres[c]
            if mode in ("mid", "last"):
                m["yT"] = yT[c]
            maps.append(m)
        res = run_bass_kernel_spmd(nc, maps, core_ids=cores)
        if mode == "last":
            out = np.empty((B, SEQ, D), np.float32)
            for c in cores:
                out[c // 2, (c % 2) * NT:(c % 2 + 1) * NT, :] = res.results[c]["out"].T
            return out
        xres = [res.results[c]["xres"] for c in cores]
        xn2 = [res.results[c]["xn2"] for c in cores]
        L = {k: inp[k][l] for k in ("mix_w_in", "conv_w", "conv_b", "conv_ln_g", "conv_ln_b", "rwkv_mu", "rwkv_w0", "rwkv_w_up", "rwkv_a0",
                                    "rwkv_a_up", "rwkv_g_up", "rwkv_k_k", "rwkv_k_a", "rwkv_r_k", "rwkv_lnx_g", "rwkv_lnx_b",
                                    "sgu_ln_g", "sgu_ln_b", "sgu_w_s", "sgu_b_s")}
        ncm = _prog(cfg, "M", lambda: build_mixer(cfg))
        maps = []
        shared = {}
        for c in cores:
            b, role = c // 2, c % 2
            full = np.concatenate([xn2[2 * b], xn2[2 * b + 1]], axis=1)
            own = np.zeros((D, 32 + NT), BF)
            own[:, 32:] = xn2[c]
            if role == 1:
                own[:, 0:32] = xn2[2 * b][:, NT - 32:NT]
            if role not in shared:
                shared[role] = mixer_inputs(cfg, L, role, None, None)
            m = dict(shared[role])
            m["xn"] = full
            m["xno"] = own
            maps.append(m)
        res = run_bass_kernel_spmd(ncm, maps, core_ids=cores)
        DG = cfg.DG
        nr, ns = cfg.NHRc * 64, cfg.NHSc * 128
        yT = []
        for c in cores:
            b, role = c // 2, c % 2
            ts = slice(role * NT, (role + 1) * NT)
            yo = res.results[c]["yown"]
            ya = [res.results[2 * b]["yall"], res.results[2 * b + 1]["yall"]]
            yT.append(np.ascontiguousarray(np.concatenate([
                yo[0:DG], ya[0][0:nr, ts], ya[1][0:nr, ts], yo[DG:2 * DG], ya[0][nr:nr + ns, ts], ya[1][nr:nr + ns, ts]], axis=0)))
    return out


CC_BYTES = 2 * 2 ** 20
MIX_KEYS = ("mix_w_in", "conv_w", "conv_b", "conv_ln_g", "conv_ln_b", "rwkv_mu", "rwkv_w0", "rwkv_w_up", "rwkv_a0",
            "rwkv_a_up", "rwkv_g_up", "rwkv_k_k", "rwkv_k_a", "rwkv_r_k", "rwkv_lnx_g", "rwkv_lnx_b",
            "sgu_ln_g", "sgu_ln_b", "sgu_w_s", "sgu_b_s")


def build_fused(cfg, depth):
    nc = bass.Bass("TRN2", target_bir_lowering=False)
    P = PB(nc)
    D, DFF, NT, KT, SEQ = cfg.D, cfg.DFF, cfg.NT, cfg.KT, cfg.SEQ
    ext = lambda n, sh, dt: nc.dram_tensor(n, list(sh), dt, kind="ExternalInput").ap()
    X = {}
    X["xin"] = ext("xin", [D, NT], F32)
    NGT = 3 * depth + 1
    X["gn"] = ext("gn", [128, NGT * KT], F32)
    X["rolemask"] = ext("rolemask", [128, 2], F32)
    X["c128"] = ext("c128", [128, cfg.NC128], F32)
    for l in range(depth):
        X[f"win1_{l}"] = ext(f"win1_{l}", [D, 2 * DFF], F32)
        X[f"wout1_{l}"] = ext(f"wout1_{l}", [DFF, D], F32)
        X[f"win2_{l}"] = ext(f"win2_{l}", [D, 2 * DFF], F32)
        X[f"wout2_{l}"] = ext(f"wout2_{l}", [DFF, D], F32)
        X[f"wmo_{l}"] = ext(f"wmo_{l}", [4 * cfg.DG, D], F32)
        X[f"wmix_{l}"] = ext(f"wmix_{l}", [D, cfg.NC], F32)
        X[f"p128_{l}"] = ext(f"p128_{l}", [128, cfg.NP128], F32)
        X[f"wsT_{l}"] = ext(f"wsT_{l}", [128, cfg.CT * 128], F32)
        X[f"bsb_{l}"] = ext(f"bsb_{l}", [128, cfg.CT * 128], F32)
        X[f"p64_{l}"] = ext(f"p64_{l}", [64, cfg.NP64], F32)
    OUT = nc.dram_tensor("out", [D, NT], F32, kind="ExternalOutput").ap()
    XRES = nc.dram_tensor("xres", [D, NT], F32, kind="Internal").ap()
    groups = [[2 * b, 2 * b + 1] for b in range(cfg.B)]
    C = Ctx(P, NT)
    AR = Arena(P, "arena", 190 * 1024)
    rm = AR.alloc((2,), F32)
    b_rm = Buf()
    P.dma("sp", rm, X["rolemask"], writes=[b_rm])
    base0 = AR.off
    H = NT // 512
    xres = {(f, h): Buf() for f in range(KT) for h in range(H)}
    for f in range(KT):
        for h in range(H):
            P.dma("sp", XRES[f * 128:(f + 1) * 128, h * 512:(h + 1) * 512], X["xin"][f * 128:(f + 1) * 128, h * 512:(h + 1) * 512], writes=[xres[(f, h)]])
    toks = []
    gi = 0
    prev = None
    for l in range(depth + 1):
        mode = "first" if l == 0 else ("last" if l == depth else "mid")
        NG = {"first": 2, "mid": 3, "last": 2}[mode]
        T = {"xres": XRES, "gn": X["gn"][:, gi * KT:(gi + NG) * KT], "rolemask": rm, "b_rm": b_rm}
        gi += NG
        if mode in ("mid", "last"):
            T["yown"], T["b_yown"], T["yallg"], T["b_yallg"] = prev
            T["wmo"] = X[f"wmo_{l - 1}"]
            T["win2"] = X[f"win2_{l - 1}"]
            T["wout2"] = X[f"wout2_{l - 1}"]
        if mode in ("first", "mid"):
            T["win1"] = X[f"win1_{l}"]
            T["wout1"] = X[f"wout1_{l}"]
            CRX = min(D, CC_BYTES // (NT * 2))
            xn2_ts = [nc.dram_tensor(f"xn2_{l}_{j}", [CRX, NT], BF16) for j in range(D // CRX)]
            xng_ts = [nc.dram_tensor(f"xng_{l}_{j}", [2 * CRX, NT], BF16) for j in range(D // CRX)]

            def xn2_w(k, c0, n, xn2_ts=xn2_ts, CRX=CRX):
                d0 = k * 128
                return xn2_ts[d0 // CRX].ap()[d0 % CRX:d0 % CRX + 128, c0:c0 + n]
            T["xn2"] = DV(fn=xn2_w)
            T["b_xn2"] = Buf()
        else:
            T["out"] = OUT
        toks += t_phase_body(P, C, cfg, AR, T, mode, xres)
        if mode == "last":
            break
        b_xng = Buf()
        for j in range(len(xn2_ts)):
            P.collective("AllGather", [xn2_ts[j].ap().opt()], [xng_ts[j].ap().opt()], groups, reads=[T["b_xn2"]], writes=[b_xng])
        P.barrier()

        def xn_full(k, c0, n, xng_ts=xng_ts, CRX=CRX):
            r = c0 // NT
            assert (c0 + n - 1) // NT == r
            d0 = k * 128
            i = d0 % CRX
            return xng_ts[d0 // CRX].ap()[r * CRX + i:r * CRX + i + 128, c0 - r * NT:c0 - r * NT + n]

        def xn_full_g(k0, nk, c0, n, xng_ts=xng_ts, CRX=CRX):
            r = c0 // NT
            assert (c0 + n - 1) // NT == r
            d0 = k0 * 128
            i = d0 % CRX
            assert i + nk * 128 <= CRX
            return xng_ts[d0 // CRX].ap()[r * CRX + i:r * CRX + i + nk * 128, c0 - r * NT:c0 - r * NT + n].rearrange("(k p) n -> p k n", p=128)

        def xn_own_g(k0, nk, c0, n, xng_ts=xng_ts, xn2_ts=xn2_ts, CRX=CRX):
            d0 = k0 * 128
            i = d0 % CRX
            assert i + nk * 128 <= CRX
            if c0 == 0:
                assert n == 32
                return xng_ts[d0 // CRX].ap()[i:i + nk * 128, NT - 32:NT].rearrange("(k p) n -> p k n", p=128)
            return xn2_ts[d0 // CRX].ap()[i:i + nk * 128, c0 - 32:c0 - 32 + n].rearrange("(k p) n -> p k n", p=128)

        def xn_own(k, c0, n, xng_ts=xng_ts, xn2_ts=xn2_ts, CRX=CRX):
            d0 = k * 128
            i = d0 % CRX
            if c0 == 0:
                assert n == 32
                return xng_ts[d0 // CRX].ap()[i:i + 128, NT - 32:NT]
            return xn2_ts[d0 // CRX].ap()[i:i + 128, c0 - 32:c0 - 32 + n]
        yown_t = nc.dram_tensor(f"yown_{l}", [cfg.YOWN, NT], BF16, kind="Internal")
        CRY = min(cfg.YALL, CC_BYTES // (SEQ * 2))
        yall_ts = [nc.dram_tensor(f"yall_{l}_{j}", [CRY, SEQ], BF16) for j in range(cfg.YALL // CRY)]
        yallg_ts = [nc.dram_tensor(f"yallg_{l}_{j}", [2 * CRY, SEQ], BF16) for j in range(cfg.YALL // CRY)]

        def yall_w(r0, nr_, c0, n, yall_ts=yall_ts, CRY=CRY):
            assert r0 // CRY == (r0 + nr_ - 1) // CRY
            return yall_ts[r0 // CRY].ap()[r0 % CRY:r0 % CRY + nr_, c0:c0 + n]

        def yallg_r(r, row, c0, n, yallg_ts=yallg_ts, CRY=CRY):
            i = row % CRY
            return yallg_ts[row // CRY].ap()[r * CRY + i:r * CRY + i + 128, c0:c0 + n]
        TM = {"xno": DV(fn=xn_own, fn_group=xn_own_g, kg=CRX // 128), "xn": DV(fn=xn_full, fn_group=xn_full_g, kg=CRX // 128), "wmix": X[f"wmix_{l}"], "p128": X[f"p128_{l}"], "wsT": X[f"wsT_{l}"],
              "bsb": X[f"bsb_{l}"], "p64": X[f"p64_{l}"], "c128": X["c128"], "yown": yown_t.ap(), "yall": RV(fn=yall_w)}
        mtoks = mixer_body(P, C, cfg, TM, AR, halo_mask=rm[:, 1:2], b_hm=b_rm)
        AR.reset(base0)
        b_yown = Buf()
        b_yall = Buf()
        P.wait_all("pool", mtoks)
        P.wait_all("sp", mtoks)
        b_yallg = Buf()
        for j in range(len(yall_ts)):
            P.collective("AllGather", [yall_ts[j].ap().opt()], [yallg_ts[j].ap().opt()], groups, reads=[], writes=[b_yallg])
        P.barrier()
        prev = (yown_t.ap(), b_yown, yallg_r, b_yallg)
    toks += [b.w for b in xres.values() if b.w is not None]
    P.finish(toks)
    print("fused n_inst", P.n_inst, "arena peak KB", AR.peak * 4 / 1024)
    return nc


def run_fused(cfg, inp):
    B, SEQ, D, NT = cfg.B, cfg.SEQ, cfg.D, cfg.NT
    NCORE = 2 * B
    depth = inp["ffn1_w_in"].shape[0]
    cores = list(range(NCORE))
    nc = _prog(cfg, ("F", depth), lambda: build_fused(cfg, depth))
    x = inp["x"]
    gvecs = []
    for l in range(depth + 1):
        if l > 0:
            gvecs.append(inp["ffn2_norm"][l - 1])
        if l < depth:
            gvecs += [inp["ffn1_norm"][l], inp["mix_norm"][l]]
        else:
            gvecs.append(inp["final_norm"])
    common = {"gn": gn_pack(cfg, gvecs), "c128": consts_np(cfg)}
    for l in range(depth):
        common[f"win1_{l}"] = inp["ffn1_w_in"][l]
        common[f"wout1_{l}"] = inp["ffn1_w_out"][l]
        common[f"win2_{l}"] = inp["ffn2_w_in"][l]
        common[f"wout2_{l}"] = inp["ffn2_w_out"][l]
        common[f"wmo_{l}"] = inp["mix_w_out"][l]
    per_role = {0: {}, 1: {}}
    for l in range(depth):
        L = {k: inp[k][l] for k in MIX_KEYS}
        for role in (0, 1):
            mi = mixer_inputs(cfg, L, role, None, None)
            for k in ("wmix", "p128", "wsT", "bsb", "p64"):
                per_role[role][f"{k}_{l}"] = mi[k]
    maps = []
    for c in cores:
        role = c % 2
        m = dict(common)
        m.update(per_role[role])
        m["xin"] = np.ascontiguousarray(x[c // 2, role * NT:(role + 1) * NT, :].T)
        rmk = np.zeros((128, 2), np.float32)
        rmk[:, 0] = 1 - role
        rmk[:, 1] = role
        m["rolemask"] = rmk
        maps.append(m)
    res = run_bass_kernel_spmd(nc, maps, core_ids=cores)
    out = np.empty((B, SEQ, D), np.float32)
    for c in cores:
        out[c // 2, (c % 2) * NT:(c % 2 + 1) * NT, :] = res.results[c]["out"].T
    return out


def kernel(**inputs):
    inp = {k: np.asarray(v) for k, v in inputs.items()}
    cfg = Cfg()
    return run_fused(cfg, inp)
```
